# Optimizing a Trainium2 kernel written in Bass

```python
import jax, jax.numpy as jnp
from jax import lax
import numpy as np

D_MODEL = 1024
BATCH = 2
SEQ = 8192
DEPTH = 1
DEC_BATCH = 128
DEC_SEQ = 4
PAST_LEN = 8192
PAGE_SIZE = 128

N_META = 16
D_MLSTM = D_MODEL // 2
D_SWA = D_MODEL - D_MLSTM
M_HEADS = 4
M_DH = D_MLSTM // M_HEADS
A_HEADS = 8
A_KV_HEADS = 2
A_GROUP = A_HEADS // A_KV_HEADS
A_DH = D_SWA // A_HEADS
KV_W = A_KV_HEADS * A_DH
WINDOW = 128
CHUNK = 128
LN_EPS = 1e-5
DN_ALPHA = (2.0 * DEPTH) ** 0.25
DN_BETA = (8.0 * DEPTH) ** -0.25
PROJ_SIZES = (D_MLSTM, D_MLSTM, D_MLSTM, D_MLSTM, D_MLSTM, M_HEADS, M_HEADS, D_SWA, KV_W, KV_W, D_SWA)
N_IN = sum(PROJ_SIZES)
F_GATE_OFF = 5 * D_MLSTM + M_HEADS

kernel_name = "hymba_mlstm_swa_sink_decode_step"


def layer_norm(x, g, b):
    xf = x.astype(jnp.float32)
    mu = jnp.mean(xf, -1, keepdims=True)
    var = jnp.mean(jnp.square(xf - mu), -1, keepdims=True)
    y = (xf - mu) * lax.rsqrt(var + LN_EPS) * g.astype(jnp.float32) + b.astype(jnp.float32)
    return y.astype(x.dtype)


def split_proj(u):
    parts, off = [], 0
    for n in PROJ_SIZES:
        parts.append(u[..., off:off + n])
        off += n
    return parts


def mlstm_inputs(mq, mk, mv, mi, mf):
    f32 = jnp.float32
    shp = mq.shape[:2] + (M_HEADS, M_DH)
    q = mq.astype(f32).reshape(shp)
    k = mk.astype(f32).reshape(shp) * (M_DH ** -0.5)
    v = mv.astype(f32).reshape(shp)
    log_i = mi.astype(f32)
    log_f = jax.nn.log_sigmoid(mf.astype(f32))
    return q, k, v, log_i, log_f


def mlstm_chunk(state, q, k, v, log_i, log_f):
    c, n, m = state
    L = q.shape[1]
    b = jnp.cumsum(log_f, axis=1)
    dmat = b[:, :, None, :] - b[:, None, :, :] + log_i[:, None, :, :]
    causal = jnp.tril(jnp.ones((L, L), bool))[:, :, None]
    dmat = jnp.where(causal, dmat, -jnp.inf)
    inter = b + m[:, None, :]
    m_t = jnp.maximum(inter, jnp.max(dmat, axis=2))
    w = jnp.exp(dmat - m_t[:, :, None, :])
    s_inter = jnp.exp(inter - m_t)
    qk = jnp.einsum('bthd,bshd->btsh', q, k) * w
    num = jnp.einsum('btsh,bshd->bthd', qk, v) + jnp.einsum('bthk,bhkv->bthv', q, c) * s_inter[..., None]
    den = jnp.sum(qk, axis=2) + jnp.einsum('bthk,bhk->bth', q, n) * s_inter
    h = num / jnp.maximum(jnp.abs(den), jnp.exp(-m_t))[..., None]
    b_last = b[:, -1]
    m_new = m_t[:, -1]
    w_state = jnp.exp(b_last[:, None, :] - b + log_i - m_new[:, None, :])
    decay = jnp.exp(b_last + m - m_new)
    c_new = decay[..., None, None] * c + jnp.einsum('bsh,bshk,bshv->bhkv', w_state, k, v)
    n_new = decay[..., None] * n + jnp.einsum('bsh,bshk->bhk', w_state, k)
    return (c_new, n_new, m_new), h


def mlstm_prompt(q, k, v, log_i, log_f):
    f32 = jnp.float32
    B, T = q.shape[:2]
    n_chunks = (T - N_META) // CHUNK
    st = (jnp.zeros((B, M_HEADS, M_DH, M_DH), f32), jnp.zeros((B, M_HEADS, M_DH), f32),
          jnp.zeros((B, M_HEADS), f32))
    st, h_meta = mlstm_chunk(st, q[:, :N_META], k[:, :N_META], v[:, :N_META],
                             log_i[:, :N_META], log_f[:, :N_META])

    def to_chunks(a):
        a = a[:, N_META:]
        return jnp.swapaxes(a.reshape((B, n_chunks, CHUNK) + a.shape[2:]), 0, 1)

    def body(carry, xs):
        return mlstm_chunk(carry, *xs)

    st, h_real = lax.scan(body, st, tuple(to_chunks(a) for a in (q, k, v, log_i, log_f)))
    h_real = jnp.swapaxes(h_real, 0, 1).reshape(B, T - N_META, M_HEADS, M_DH)
    return jnp.concatenate([h_meta, h_real], axis=1), st


def swa_inputs(aq, ak, av):
    f32 = jnp.float32
    B, T = aq.shape[:2]
    q = aq.astype(f32).reshape(B, T, A_KV_HEADS, A_GROUP, A_DH)
    k = ak.astype(f32).reshape(B, T, A_KV_HEADS, A_DH)
    v = av.astype(f32).reshape(B, T, A_KV_HEADS, A_DH)
    return q, k, v


def sink_softmax(s, sink):
    sk = sink.astype(jnp.float32)[:, :, None, None]
    mx = jnp.maximum(jnp.max(s, axis=-1, keepdims=True), sk)
    p = jnp.exp(s - mx)
    return p / (jnp.sum(p, axis=-1, keepdims=True) + jnp.exp(sk - mx))


def swa_prompt(q, k, v, sink):
    B, T = q.shape[:2]
    n_real = T - N_META
    nb = n_real // WINDOW
    q = q * (A_DH ** -0.5)
    qm, km, vm = q[:, :N_META], k[:, :N_META], v[:, :N_META]
    s = jnp.einsum('bqkgd,bskd->bkgqs', qm, km)
    s = jnp.where(jnp.tril(jnp.ones((N_META, N_META), bool)), s, -jnp.inf)
    o_meta = jnp.einsum('bkgqs,bskd->bqkgd', sink_softmax(s, sink), vm)
    qb = q[:, N_META:].reshape(B, nb, WINDOW, A_KV_HEADS, A_GROUP, A_DH)
    kb = k[:, N_META:].reshape(B, nb, WINDOW, A_KV_HEADS, A_DH)
    vb = v[:, N_META:].reshape(B, nb, WINDOW, A_KV_HEADS, A_DH)
    pad = ((0, 0), (1, 0), (0, 0), (0, 0), (0, 0))
    kprev = jnp.pad(kb, pad)[:, :-1]
    vprev = jnp.pad(vb, pad)[:, :-1]
    kmeta = jnp.broadcast_to(km[:, None], (B, nb, N_META, A_KV_HEADS, A_DH))
    vmeta = jnp.broadcast_to(vm[:, None], (B, nb, N_META, A_KV_HEADS, A_DH))
    kcat = jnp.concatenate([kmeta, kprev, kb], axis=2)
    vcat = jnp.concatenate([vmeta, vprev, vb], axis=2)
    diff = jnp.arange(WINDOW)[:, None] - (jnp.arange(2 * WINDOW)[None, :] - WINDOW)
    band = (diff >= 0) & (diff < WINDOW)
    has_prev = (jnp.arange(nb) > 0)[:, None, None]
    prev_col = (jnp.arange(2 * WINDOW) < WINDOW)[None, None, :]
    band_b = band[None] & (has_prev | ~prev_col)
    mask = jnp.concatenate([jnp.ones((nb, WINDOW, N_META), bool), band_b], axis=-1)
    s = jnp.einsum('bnqkgd,bnskd->bnkgqs', qb, kcat)
    s = jnp.where(mask[None, :, None, None], s, -jnp.inf)
    o_real = jnp.einsum('bnkgqs,bnskd->bnqkgd', sink_softmax(s, sink), vcat)
    o_real = o_real.reshape(B, n_real, A_KV_HEADS, A_GROUP, A_DH)
    return jnp.concatenate([o_meta, o_real], axis=1)


def swa_sample(q, k, v, ck, cv, sink):
    f32 = jnp.float32
    L = q.shape[1]
    kk = jnp.concatenate([ck.astype(f32), k], axis=1)
    vv = jnp.concatenate([cv.astype(f32), v], axis=1)
    qpos = PAST_LEN + jnp.arange(L)
    buf_pos = PAST_LEN - WINDOW + jnp.arange(WINDOW)
    meta_ok = jnp.ones((L, N_META), bool)
    win_ok = ((qpos[:, None] - buf_pos[None, :]) < WINDOW) & (buf_pos[None, :] >= N_META)
    dnew = jnp.arange(L)[:, None] - jnp.arange(L)[None, :]
    new_ok = (dnew >= 0) & (dnew < WINDOW)
    mask = jnp.concatenate([meta_ok, win_ok, new_ok], axis=1)
    s = jnp.einsum('bqkgd,bskd->bkgqs', q * (A_DH ** -0.5), kk)
    s = jnp.where(mask, s, -jnp.inf)
    o = jnp.einsum('bkgqs,bskd->bqkgd', sink_softmax(s, sink), vv)
    new_k = jnp.concatenate([ck[:, :N_META],
                             jnp.concatenate([ck[:, N_META:], k.astype(ck.dtype)], axis=1)[:, -WINDOW:]], axis=1)
    new_v = jnp.concatenate([cv[:, :N_META],
                             jnp.concatenate([cv[:, N_META:], v.astype(cv.dtype)], axis=1)[:, -WINDOW:]], axis=1)
    return o, new_k, new_v


def mix_and_residual(x, h_m, o_pre, z_m, h_a, z_a, norm_g, w_o, g, b):
    f32 = jnp.float32
    B, T = x.shape[:2]
    h_m = h_m * jax.nn.sigmoid(o_pre.astype(f32)).reshape(h_m.shape)
    mu = jnp.mean(h_m, -1, keepdims=True)
    var = jnp.mean(jnp.square(h_m - mu), -1, keepdims=True)
    h_m = ((h_m - mu) * lax.rsqrt(var + LN_EPS)).reshape(B, T, D_MLSTM) * norm_g.astype(f32)
    y_m = h_m * jax.nn.silu(z_m.astype(f32))
    y_a = h_a.reshape(B, T, D_SWA) * jax.nn.silu(z_a.astype(f32))
    mix = jnp.concatenate([y_m, y_a], axis=-1).astype(w_o.dtype) @ w_o
    return layer_norm(DN_ALPHA * x + mix.astype(x.dtype), g, b)


def setup_inputs(seed: int = 0) -> dict:
    key = jax.random.key(seed)
    ks = jax.random.split(key, 18)
    f32 = jnp.float32

    def nrm(k, shape, scale=1.0):
        return scale * jax.random.normal(k, shape, f32)

    buf = (DEPTH, DEC_BATCH, N_META + WINDOW, A_KV_HEADS, A_DH)
    b_in = nrm(ks[10], (DEPTH, N_IN), 0.02)
    b_in = b_in.at[:, F_GATE_OFF:F_GATE_OFF + M_HEADS].add(jnp.linspace(3.0, 6.0, M_HEADS, dtype=f32))
    return {
        "x_prompt": nrm(ks[0], (BATCH, SEQ, D_MODEL)),
        "x_sample": nrm(ks[1], (DEC_BATCH, DEC_SEQ, D_MODEL)),
        "cache_swa_k": nrm(ks[2], buf),
        "cache_swa_v": nrm(ks[3], buf),
        "state_mlstm_c": nrm(ks[4], (DEPTH, DEC_BATCH, M_HEADS, M_DH, M_DH), 0.3),
        "state_mlstm_n": nrm(ks[5], (DEPTH, DEC_BATCH, M_HEADS, M_DH), 0.3),
        "state_mlstm_m": nrm(ks[6], (DEPTH, DEC_BATCH, M_HEADS), 0.5),
        "meta_tokens": nrm(ks[7], (N_META, D_MODEL)),
        "ln0_g": 1.0 + nrm(ks[8], (D_MODEL,), 0.02),
        "ln0_b": nrm(ks[9], (D_MODEL,), 0.02),
        "w_in": nrm(ks[11], (DEPTH, D_MODEL, N_IN), D_MODEL ** -0.5),
        "b_in": b_in,
        "a_sinks": nrm(ks[12], (DEPTH, A_HEADS), 0.5),
        "m_norm_g": 1.0 + nrm(ks[13], (DEPTH, D_MLSTM), 0.02),
        "w_out": nrm(ks[14], (DEPTH, D_MODEL, D_MODEL), DN_BETA * D_MODEL ** -0.5),
        "ln_g": 1.0 + nrm(ks[15], (DEPTH, D_MODEL), 0.02),
        "ln_b": nrm(ks[16], (DEPTH, D_MODEL), 0.02),
    }


def reference(x_prompt, x_sample, cache_swa_k, cache_swa_v, state_mlstm_c, state_mlstm_n, state_mlstm_m,
              meta_tokens, ln0_g, ln0_b, w_in, b_in, a_sinks, m_norm_g, w_out, ln_g, ln_b):
    f32 = jnp.float32
    B = x_prompt.shape[0]
    meta = jnp.broadcast_to(meta_tokens.astype(x_prompt.dtype)[None], (B, N_META, D_MODEL))
    hp = layer_norm(jnp.concatenate([meta, x_prompt], axis=1), ln0_g, ln0_b)
    hs = layer_norm(x_sample, ln0_g, ln0_b)
    pk, pv, pc, pn, pm = [], [], [], [], []
    sk, sv, sc, sn, sm = [], [], [], [], []
    for l in range(DEPTH):
        sink = a_sinks[l].reshape(A_KV_HEADS, A_GROUP)
        mq, mk, mv, mo, mz, mi, mf, aq, ak, av, az = split_proj(hp @ w_in[l] + b_in[l])
        h_m, (c1, n1, m1) = mlstm_prompt(*mlstm_inputs(mq, mk, mv, mi, mf))
        q_a, k_a, v_a = swa_inputs(aq, ak, av)
        h_a = swa_prompt(q_a, k_a, v_a, sink)
        pk.append(jnp.concatenate([k_a[:, :N_META], k_a[:, -WINDOW:]], axis=1).astype(cache_swa_k.dtype))
        pv.append(jnp.concatenate([v_a[:, :N_META], v_a[:, -WINDOW:]], axis=1).astype(cache_swa_v.dtype))
        pc.append(c1.astype(state_mlstm_c.dtype))
        pn.append(n1.astype(state_mlstm_n.dtype))
        pm.append(m1.astype(state_mlstm_m.dtype))
        hp = mix_and_residual(hp, h_m, mo, mz, h_a, az, m_norm_g[l], w_out[l], ln_g[l], ln_b[l])
        mq, mk, mv, mo, mz, mi, mf, aq, ak, av, az = split_proj(hs @ w_in[l] + b_in[l])
        st0 = (state_mlstm_c[l].astype(f32), state_mlstm_n[l].astype(f32), state_mlstm_m[l].astype(f32))
        (c2, n2, m2), h_m = mlstm_chunk(st0, *mlstm_inputs(mq, mk, mv, mi, mf))
        q_a, k_a, v_a = swa_inputs(aq, ak, av)
        h_a, nk, nv = swa_sample(q_a, k_a, v_a, cache_swa_k[l], cache_swa_v[l], sink)
        sk.append(nk)
        sv.append(nv)
        sc.append(c2.astype(state_mlstm_c.dtype))
        sn.append(n2.astype(state_mlstm_n.dtype))
        sm.append(m2.astype(state_mlstm_m.dtype))
        hs = mix_and_residual(hs, h_m, mo, mz, h_a, az, m_norm_g[l], w_out[l], ln_g[l], ln_b[l])
    y_prompt = hp[:, N_META:]
    y_sample = hs
    return (y_prompt, y_sample,
            jnp.stack(pk), jnp.stack(pv), jnp.stack(pc), jnp.stack(pn), jnp.stack(pm),
            jnp.stack(sk), jnp.stack(sv), jnp.stack(sc), jnp.stack(sn), jnp.stack(sm))
```

```python
import contextlib
import numpy as np
import concourse.bass as bass
import concourse.mybir as mybir
from concourse.bass_utils import run_bass_kernel_spmd

F32 = mybir.dt.float32
BF16 = mybir.dt.bfloat16
ALU = mybir.AluOpType
AF = mybir.ActivationFunctionType

COMPUTE = ("pe", "act", "dve", "pool")
STRICT_SAME_ENGINE = False

D = 1024
NIN = 3848
NMETA = 16
O_MK, O_MV, O_AK, O_AV, O_MI, O_MF, O_MQ, O_MO, O_MZ, O_AQ, O_AZ = 0, 512, 1024, 1152, 1280, 1284, 1288, 1800, 2312, 2824, 3336
N1 = 1288
N2 = NIN - N1
COL_PERM = [(512, 1024), (1024, 1536), (3080, 3208), (3208, 3336), (2560, 2568), (0, 512), (1536, 2048), (2048, 2560), (2568, 3080), (3336, 3848)]
LN_EPS = 1e-5
DN_ALPHA = 2.0 ** 0.25
KSCALE = 128.0 ** -0.5
ASCALE = 64.0 ** -0.5
NEG = -1.0e30


class Op:
    __slots__ = ("eng", "fn", "reads", "writes", "dma", "key", "idx", "deps", "marked", "kcount", "alld", "fin", "lat")

    def __init__(self, eng, fn, reads, writes, dma, key):
        self.eng, self.fn, self.reads, self.writes, self.dma, self.key = eng, fn, reads, writes, dma, key
        self.deps = []
        self.marked = False
        self.kcount = 0


class Prog:
    def __init__(self, nc):
        self.nc = nc
        self.ops = []
        self.nkeys = {}
        self.filler = None
        self.fill_frac = 0.7
        self.fill_on = lambda o: True
        self.excl = set()

    def op(self, eng, fn, reads=(), writes=()):
        writes = tuple(writes) + tuple(r for r in reads if id(r) in self.excl and not any(r is w for w in writes))
        o = Op(eng, fn, tuple(reads), tuple(writes), False, None)
        self.ops.append(o)
        return o

    def dma(self, eng, fn, reads=(), writes=(), key=None, lat=3.0):
        o = Op(eng, fn, tuple(reads), tuple(writes), True, key)
        o.lat = lat
        self.nkeys[key] = self.nkeys.get(key, 0) + 1
        o.kcount = self.nkeys[key]
        self.ops.append(o)
        return o

    COST = {"pe": 0.25, "act": 0.45, "dve": 0.5, "pool": 0.9, "sp": 0.05}

    def _schedule(self, window=96):
        self._analyze(mark=False)
        per = {}
        for o in self.ops:
            per.setdefault(o.eng, []).append(o)
            o.fin = None
        free = {e: 0.0 for e in per}
        order = []
        nleft = len(self.ops)
        while nleft:
            best = None
            for e, lst in per.items():
                cnt = 0
                for o in lst:
                    if o.fin is not None:
                        continue
                    cnt += 1
                    if cnt > window:
                        break
                    rdy = 0.0
                    ok = True
                    for p in o.alld:
                        if p.fin is None:
                            ok = False
                            break
                        f = p.fin + (0.0 if (p.eng == e and not p.dma) else 0.25)
                        if f > rdy:
                            rdy = f
                    if not ok:
                        continue
                    st = max(free[e], rdy)
                    if best is None or (st, o.idx) < (best[0], best[1].idx):
                        best = (st, o)
                    if st <= free[e]:
                        break
            st, o = best
            if self.filler is not None and any(t is self.filler[1] for t in o.reads + o.writes):
                self.filler = None
            if self.filler is not None and o.eng == "pe" and self.fill_on(o):
                gap = st - free["pe"]
                if gap > 0.6:
                    nf = min(int(gap * self.fill_frac / 0.25), 24)
                    for _ in range(nf):
                        f = Op("pe", self.filler[0], (), (self.filler[1],), False, None)
                        f.idx = -1
                        f.alld = []
                        f.fin = free["pe"] + 0.25
                        free["pe"] = f.fin
                        order.append(f)
                    st = max(st, free["pe"])
            c = self.COST[o.eng]
            free[o.eng] = st + c
            o.fin = st + c + (o.lat if o.dma else 0.0)
            order.append(o)
            nleft -= 1
            lst = per[o.eng]
            while lst and lst[0].fin is not None:
                lst.pop(0)
        self.ops = order
        self.nkeys = {}
        for o in self.ops:
            o.marked = False
            if o.dma:
                self.nkeys[o.key] = self.nkeys.get(o.key, 0) + 1
                o.kcount = self.nkeys[o.key]
        self.sim_time = max(free.values())

    def _analyze(self, mark=True):
        wr, rd = {}, {}
        SERIAL = False

        def add(lst, o):
            if not o.dma:
                for i, x in enumerate(lst):
                    if (not x.dma) and x.eng == o.eng:
                        lst[i] = o
                        return
            lst.append(o)

        for idx, o in enumerate(self.ops):
            o.idx = idx
            deps = {}
            for r in o.reads:
                for p in wr.get(id(r), ()):
                    deps[p.idx] = (p, "raw")
            for w in o.writes:
                for p in rd.get(id(w), ()):
                    deps.setdefault(p.idx, (p, "war"))
                for p in wr.get(id(w), ()):
                    deps.setdefault(p.idx, (p, "waw"))
            wids = set(id(w) for w in o.writes)
            for w in o.writes:
                if rd.get(id(w)):
                    wr[id(w)] = [o]
                    rd[id(w)] = []
                else:
                    add(wr.setdefault(id(w), []), o)
            for r in o.reads:
                if id(r) not in wids:
                    add(rd.setdefault(id(r), []), o)
            if SERIAL and idx > 0:
                pp = self.ops[idx - 1]
                deps.setdefault(pp.idx, (pp, "raw"))
            o.deps = []
            o.alld = []
            for p, kind in deps.values():
                if p is o:
                    continue
                o.alld.append(p)
                if (not p.dma) and (not o.dma) and p.eng == o.eng:
                    if p.eng == "pe" or (kind != "raw" and not STRICT_SAME_ENGINE):
                        continue
                o.deps.append(p)
                if mark:
                    p.marked = True

    def emit(self, final_wait_eng="sp", schedule=True):
        nc = self.nc
        if schedule:
            self._schedule()
        self._analyze()
        with contextlib.ExitStack() as es:
            esem = {e: es.enter_context(nc.semaphore("s_" + e)) for e in COMPUTE}
            ksem = {k: es.enter_context(nc.semaphore("k_%s" % (str(k),))) for k in self.nkeys}
            cnt = {e: 0 for e in COMPUTE}
            val = {}
            for o in self.ops:
                if o.dma:
                    val[o.idx] = (ksem[o.key], 16 * o.kcount)
                elif o.marked:
                    cnt[o.eng] += 1
                    val[o.idx] = (esem[o.eng], cnt[o.eng])
            block = es.enter_context(nc.Block())

            def run_engine(ename):
                def body(eng):
                    waited = {}
                    for o in self.ops:
                        if o.eng != ename:
                            continue
                        for p in o.deps:
                            sem, v = val[p.idx]
                            if waited.get(id(sem), 0) >= v:
                                continue
                            waited[id(sem)] = v
                            eng.wait_ge(sem, v)
                        ins = o.fn()
                        if o.dma:
                            ins.then_inc(ksem[o.key], 16)
                        elif o.marked:
                            ins.then_inc(esem[o.eng], 1)
                    if ename == final_wait_eng:
                        for k, n in self.nkeys.items():
                            eng.wait_ge(ksem[k], 16 * n)
                return body

            block.sync(run_engine("sp"))
            block.tensor(run_engine("pe"))
            block.scalar(run_engine("act"))
            block.vector(run_engine("dve"))
            block.gpsimd(run_engine("pool"))


class Rot:
    def __init__(self, tiles):
        self.tiles = tiles
        self.i = -1

    def next(self):
        self.i = (self.i + 1) % len(self.tiles)
        return self.tiles[self.i]

    def cur(self):
        return self.tiles[self.i]

    def prev(self):
        return self.tiles[(self.i - 1) % len(self.tiles)]


def build(NOWN, DO_SAMPLE=True):
    NPRE = 3 * NOWN
    NSLOT = 1 + NPRE
    nc = bass.Bass("TRN2", target_bir_lowering=False)

    def din(name, shape):
        return nc.dram_tensor(name, list(shape), F32, kind="ExternalInput").ap()

    def dout(name, shape):
        return nc.dram_tensor(name, list(shape), F32, kind="ExternalOutput").ap()

    xown = din("xown", [NOWN * 128, D])
    xpre = din("xpre", [NSLOT * 128, D])
    rvalid = din("rvalid", [4, NSLOT * 128])
    rneg = din("rneg", [4, NSLOT * 128])
    pm1 = din("pm1", [128, 512])
    cid = din("cid", [128, 128])
    ctri = din("ctri", [128, 512])
    cprev = din("cprev", [128, 512])
    w_in = din("w_in", [D, NIN])
    b_in = din("b_in", [1, NIN])
    w_out = din("w_out", [D, D])
    vecs = din("vecs", [4, D])
    mng = din("mng", [1, 512])
    sinks = din("sinks", [1, 8])

    y_o = dout("y", [NOWN * 128, D])
    pk_o = dout("pk", [144, 128])
    pv_o = dout("pv", [144, 128])
    pc_o = dout("pc", [4, 128, 128])
    pn_o = dout("pn", [4, 128])
    pm_o = dout("pm", [4, 1])

    if DO_SAMPLE:
        xs = din("xs", [64, D])
        ck_i = din("ck", [16, 144, 128])
        cv_i = din("cv", [16, 144, 128])
        sc_i = din("sc", [16, 4, 128, 128])
        sn_i = din("sn", [64, 128])
        sm_i = din("sm", [4, 16])
        cseqcol = din("cseqcol", [128, 16])
        cblk4 = din("cblk4", [128, 512])
        cwin = din("cwin", [128, 256])
        cnew = din("cnew", [128, 256])
        sinks16 = din("sinks16", [16, 2])
        ys_o = dout("ys", [64, D])
        sk_o = dout("sk", [16, 144, 128])
        sv_o = dout("sv", [16, 144, 128])
        sc_o = dout("sco", [16, 4, 128, 128])
        sn_o = dout("sno", [64, 128])
        sm_o = dout("smo", [4, 16])

    P = Prog(nc)
    es = contextlib.ExitStack()
    KDBG = False
    dbg_out = {}

    def DBGDUMP(name, t, ap, shape):
        if not KDBG:
            return
        d = nc.dram_tensor("dbg_" + name, list(shape), t.dtype if hasattr(t, "dtype") else F32, kind="ExternalOutput").ap()
        dbg_out[name] = d
        P.dma("sp", lambda: nc.sync.dma_start(out=d, in_=ap), reads=[t], key="D_" + name)

    def sb(name, shape, dt=F32):
        return es.enter_context(nc.sbuf_tensor(name, list(shape), dt))

    def psum(name, shape, dt=F32):
        t = es.enter_context(nc.psum_tensor(name, list(shape), dt))
        P.excl.add(id(t))
        return t

    def rot(name, shape, dt=F32, n=2):
        return Rot([sb("%s%d" % (name, i), shape, dt) for i in range(n)])

    V = lambda fn, r=(), w=(): P.op("dve", fn, r, w)
    A = lambda fn, r=(), w=(): P.op("act", fn, r, w)
    G = lambda fn, r=(), w=(): P.op("pool", fn, r, w)
    T = lambda fn, r=(), w=(): P.op("pe", fn, r, w)
    kctr = [0]

    def LD(out_t, out_ap, in_ap, lat=3.0, **kw):
        P.dma("sp", lambda: nc.sync.dma_start(out=out_ap, in_=in_ap, **kw), writes=[out_t], key="L_" + out_t.name, lat=lat)

    def ST(out_ap, in_t, in_ap, **kw):
        P.dma("sp", lambda: nc.sync.dma_start(out=out_ap, in_=in_ap, **kw), reads=[in_t], key="S_" + in_t.name)

    with es:
        GB = Rot([psum("g%d" % i, [128, 512]) for i in range(6)])
        GB6 = GB
        pDum = GB.tiles[5]
        WARM_PRE, WARM_OWN = 18, 0


        def warm(n):
            for _ in range(n):
                T(lambda: nc.tensor.matmul(pDum[:], lhsT=idb[:], rhs=tri4[:], start=True, stop=True), [], [pDum])
        pT = psum("pT", [128, 1024], BF16)
        pS = psum("pS", [128, 512])

        wst = rot("wst", [128, D], F32, 2)
        xR = rot("xt", [128, D], F32, 2)
        hpR = rot("hp", [128, D], F32, 2)
        cstage = wst.tiles[0]
        idf = sb("idf", [128, 128]); idb = sb("idb", [128, 128], BF16)
        ones4 = sb("ones4", [4, 128]); ones1b = sb("ones1b", [128, 128], BF16)
        onescol = sb("onescol", [128, 1], BF16); negid4 = sb("negid4", [4, 4]); zer4 = sb("zer4", [4, 128])
        tri4 = sb("tri4", [128, 512], BF16); prev4 = sb("prev4", [128, 512], BF16); pm1b = sb("pm1b", [128, 512], BF16)
        srow = sb("srow", [1, 8])
        binb1 = sb("binb1", [128, N1], BF16); binb2 = sb("binb2", [128, N2], BF16)
        gcol = sb("gcol", [128, 8]); b0col = sb("b0col", [128, 8]); b0g = sb("b0g", [128, 8], BF16); growf = sb("growf", [1, 8])
        ln0g = sb("ln0g", [128, D]); lng = sb("lng", [128, D]); lnb = sb("lnb", [128, D]); nmrR = rot("nmr", [128, 1], F32, 2); nmr = nmrR.next()
        esink = sb("esink", [128, 8]); biasg = sb("biasg", [128, 8]); mngcol = sb("mngcol", [128, 4])
        bob = sb("bob", [128, D], BF16)
        w1 = sb("w1", [128, 8, N1], BF16); w2 = sb("w2", [128, 8, N2], BF16)

        def wsl(k, c0, n):
            if c0 + n <= N1:
                return w1, w1[:, k, c0:c0 + n]
            assert c0 >= N1
            return w2, w2[:, k, c0 - N1:c0 - N1 + n]

        def bsl(c0, n):
            if c0 + n <= N1:
                return binb1, binb1[:, c0:c0 + n]
            assert c0 >= N1
            return binb2, binb2[:, c0 - N1:c0 - N1 + n]
        wo_bf = sb("wo_bf", [128, 8, D], BF16)
        NWQ = 4
        WH = NIN // NWQ
        rvR = rot("rvt", [4, 128], F32, 2); rnR = rot("rnt", [4, 128], F32, 2)

        LD(idf, idf[:], cid)
        A(lambda: nc.scalar.copy(out=idb[:], in_=idf[:]), [idf], [idb])
        G(lambda: nc.gpsimd.memset(ones4[:], 1.0), [], [ones4])
        G(lambda: nc.gpsimd.memset(ones1b[:], 0.0), [], [ones1b])
        G(lambda: nc.gpsimd.memset(ones1b[0:1, :], 1.0), [], [ones1b])
        G(lambda: nc.gpsimd.memset(binb1[:], 0.0), [], [binb1])
        G(lambda: nc.gpsimd.memset(binb2[:], 0.0), [], [binb2])
        LD(gcol, gcol[:], vecs[0:1, :].rearrange("o (k p) -> p (o k)", p=128), allow_slow_non_contiguous=True)
        LD(b0col, b0col[:], vecs[1:2, :].rearrange("o (k p) -> p (o k)", p=128), allow_slow_non_contiguous=True)
        rgc = sb("rgc", [128, 8])
        V(lambda: nc.vector.reciprocal(out=rgc[:], in_=gcol[:]), [gcol], [rgc])
        V(lambda: nc.vector.tensor_tensor(out=b0g[:], in0=b0col[:], in1=rgc[:], op=ALU.mult), [b0col, rgc], [b0g])
        G(lambda: nc.gpsimd.memset(onescol[:], 1.0), [], [onescol])
        G(lambda: nc.gpsimd.memset(zer4[:], 0.0), [], [zer4])
        V(lambda: nc.vector.tensor_scalar(out=negid4[:], in0=idf[0:4, 0:4], scalar1=-1.0, scalar2=None, op0=ALU.mult), [idf], [negid4])
        cast_eng = [("pool", lambda o, i: nc.gpsimd.tensor_copy(out=o, in_=i)),
                    ("dve", lambda o, i: nc.vector.tensor_copy(out=o, in_=i)),
                    ("act", lambda o, i: nc.scalar.copy(out=o, in_=i))]
        scl_eng = [("pool", lambda o, i, sc: nc.gpsimd.tensor_scalar(out=o, in0=i, scalar1=sc, scalar2=None, op0=ALU.mult)),
                   ("dve", lambda o, i, sc: nc.vector.tensor_scalar(out=o, in0=i, scalar1=sc, scalar2=None, op0=ALU.mult)),
                   ("act", lambda o, i, sc: nc.scalar.mul(out=o, in_=i, mul=sc))]
        ci = 0

        def WLD(st, out_ap, in_ap, dq="sp"):
            if dq == "sp":
                P.dma("sp", lambda: nc.sync.dma_start(out=out_ap, in_=in_ap), writes=[st], key="L_" + st.name, lat=14.0)
            else:
                P.dma("pool", lambda: nc.gpsimd.dma_start(out=out_ap, in_=in_ap), writes=[st], key="L_" + st.name, lat=20.0)

        def load_w(wt, c_lo, c_hi, piece, engs, stg=None, dq="sp"):
            nonlocal ci
            for k in range(8):
                c = c_lo
                while c < c_hi:
                    n_ = min(piece, c_hi - c)
                    st = (stg or wst).next()
                    WLD(st, st[:, 0:n_], w_in[k * 128:(k + 1) * 128, c:c + n_], dq)
                    en, f = engs[ci % len(engs)]; ci += 1
                    tl, ap = wsl(k, c, n_)
                    P.op(en, lambda f=f, st=st, ap=ap, n_=n_, k=k: f(ap, st[:, 0:n_], gcol[:, k:k + 1]), [st, gcol], [tl])
                    c += n_

        def bias_rows(c_lo, c_hi):
            c = c_lo
            while c < c_hi:
                n_ = min(512, c_hi - c)
                st = wst.next()
                WLD(st, st[0:1, 0:n_], b_in[:, c:c + n_])
                pb = GB.next()
                for k in range(8):
                    tl, ap = wsl(k, c, n_)
                    T(lambda k=k, pb=pb, ap=ap, n_=n_: nc.tensor.matmul(pb[0:1, 0:n_], lhsT=b0g[:, k:k + 1], rhs=ap, start=(k == 0), stop=(k == 7)),
                      [b0g, tl], [pb])
                bt, bap = bsl(c, n_)
                V(lambda pb=pb, st=st, bap=bap, n_=n_: nc.vector.tensor_tensor(out=bap[0:1, :], in0=pb[0:1, 0:n_], in1=st[0:1, 0:n_], op=ALU.add),
                  [pb, st], [bt])
                if c <= O_MI and O_MI + 8 <= c + n_:
                    o_ = O_MI - c
                    V(lambda pb=pb, st=st, o_=o_: nc.vector.tensor_tensor(out=growf[:], in0=pb[0:1, o_:o_ + 8], in1=st[0:1, o_:o_ + 8], op=ALU.add),
                      [pb, st], [growf])
                c += n_

        load_w(w1, 0, N1, 644, scl_eng[1:3], stg=Rot(wst.tiles + xR.tiles + hpR.tiles))
        for (src, dst) in ((ctri, tri4), (cprev, prev4), (pm1, pm1b)):
            LD(cstage, cstage[:, 0:512], src)
            V(lambda dst=dst: nc.vector.tensor_copy(out=dst[:], in_=cstage[:, 0:512]), [cstage], [dst])
        LD(srow, srow[:], sinks)
        ones1f = sb("ones1f", [1, 128])
        G(lambda: nc.gpsimd.memset(ones1f[:], 1.0), [], [ones1f])

        def bcast_row(dst, dst_ap, row_t, row_ap, n, func=None):
            pb = GB.next()
            T(lambda pb=pb: nc.tensor.matmul(pb[:, 0:n], lhsT=ones1f[:], rhs=row_ap, start=True, stop=True), [ones1f, row_t], [pb])
            if func is None:
                V(lambda pb=pb: nc.vector.tensor_copy(out=dst_ap, in_=pb[:, 0:n]), [pb], [dst])
            else:
                A(lambda pb=pb: nc.scalar.activation(out=dst_ap, in_=pb[:, 0:n], func=func), [pb], [dst])

        G(lambda: nc.gpsimd.memset(bob[:], 0.0), [], [bob])
        for i, dst in enumerate((ln0g, None, lng, lnb)):
            for hf in range(2):
                rs_ = cstage
                LD(rs_, rs_[0:1, 0:512], vecs[i:i + 1, hf * 512:(hf + 1) * 512])
                if dst is None:
                    A(lambda hf=hf: nc.scalar.mul(out=bob[0:1, hf * 512:(hf + 1) * 512], in_=cstage[0:1, 0:512], mul=DN_ALPHA), [cstage], [bob])
                else:
                    bcast_row(dst, dst[:, hf * 512:(hf + 1) * 512], rs_, rs_[0:1, 0:512], 512)
        A(lambda: nc.scalar.mul(out=ln0g[:], in_=ln0g[:], mul=DN_ALPHA), [ln0g], [ln0g])
        rs_ = cstage
        LD(mngcol, mngcol[:], mng.rearrange("o (k p) -> p (o k)", p=128), allow_slow_non_contiguous=True)
        bcast_row(esink, esink[:], srow, srow[:], 8, func=AF.Exp)

        bias_rows(0, N1)
        load_w(w2, N1, NIN, 640, scl_eng[1:3], dq="pool")
        for k in range(8):
            st = wst.next()
            WLD(st, st[:, 0:D], w_out[k * 128:(k + 1) * 128, :], "pool")
            if k < 4:
                en, f = scl_eng[1 + ci % 2]; ci += 1
                P.op(en, lambda f=f, st=st, k=k: f(wo_bf[:, k, :], st[:, 0:D], mngcol[:, k:k + 1]), [st, mngcol], [wo_bf])
            else:
                en, f = cast_eng[1 + ci % 2]; ci += 1
                P.op(en, lambda f=f, st=st, k=k: f(wo_bf[:, k, :], st[:, 0:D]), [st], [wo_bf])
        bcast_row(biasg, biasg[:], growf, growf[:], 8)

        Cst = sb("Cst", [128, 4, 128]); nst = sb("nst", [128, 4])
        Cbf = sb("Cbf", [128, 4, 128], BF16); nbf = sb("nbf", [128, 4], BF16)
        V(lambda: nc.vector.memset(Cst[:], 0.0), [], [Cst])
        V(lambda: nc.vector.memset(nst[:], 0.0), [], [nst])
        V(lambda: nc.vector.memset(Cbf[:], 0.0), [], [Cbf])
        V(lambda: nc.vector.memset(nbf[:], 0.0), [], [nbf])
        bnegR = rot("bneg", [4, 128], F32, 2)
        UR = rot("Urow", [4, 128], F32, 2)
        UendR = rot("Uend", [128, 4], F32, 2)
        for t in bnegR.tiles + UR.tiles + UendR.tiles:
            V(lambda t=t: nc.vector.memset(t[:], 0.0), [], [t])
        bnegR.next(); UR.next(); UendR.next()

        st6R = rot("st6", [128, 2, 6], F32, 2); st6 = st6R.next(); mvR = rot("mv", [128, 2], F32, 2); mv = mvR.next(); rstdR = rot("rstd", [128, 1], F32, 2); rstd = rstdR.next()
        hbR = rot("hb", [128, D], BF16, 2); hb = hbR.next()
        xTR = rot("xT", [128, 8, 128], BF16, 2)
        ktmR = rot("ktm", [128, 4, 128], BF16, 2); ktm = ktmR.next(); vbfR = rot("vbf", [128, 4, 128], BF16, 2); vbf = vbfR.next()
        kwR = rot("kw", [128, 4, 128], BF16, 2); kw = kwR.next()
        gtR = rot("gt", [128, 8], F32, 2); gt = gtR.next(); speR = rot("spe", [4, 128], F32, 2); spe = speR.next(); sprR = rot("spr", [4, 128], F32, 2); spr = sprR.next(); spmR = rot("spm", [4, 128], F32, 2); spm = spmR.next()
        limR = rot("lim", [4, 128], F32, 2); lim = limR.next(); u_rR = rot("u_r", [4, 128], F32, 2); u_r = u_rR.next(); m_rR = rot("m_r", [4, 128], F32, 2); m_r = m_rR.next()
        colsR = rot("cols", [128, 12], F32, 2); cols = colsR.next(); diagUR = rot("diagU", [4, 4], F32, 2); diagU = diagUR.next(); exinR = rot("exin", [128, 16], F32, 2); exin = exinR.next(); exR = rot("ex", [128, 16], F32, 2); ex = exR.next()
        akTR = rot("akT", [128, 128], BF16, 3); vaR = rot("va", [128, 2, 65], BF16, 3)
        akTm = sb("akTm", [128, 16], BF16); vam = sb("vam", [16, 2, 65], BF16)
        kavf = sb("kavf", [128, 256])
        for t in vaR.tiles + [vam]:
            V(lambda t=t: nc.vector.memset(t[:], 1.0), [], [t])
        eo = sb("eo", [128, 512]); ez = sb("ez", [128, 512]); eaz = sb("eaz", [128, 512])
        qT = sb("qT", [128, 4, 128], BF16); kT = sb("kT", [128, 4, 128], BF16); aqT = sb("aqT", [128, 4, 128], BF16)
        bd = sb("bd", [4, 4, 128]); Wt = sb("Wt", [128, 4, 128], BF16); Wm = sb("Wm", [128, 4, 128], BF16)
        PTt = sb("PTt", [128, 4, 128], BF16); hi = sb("hi", [128, 4, 128]); hn = hi
        d1 = sb("d1", [128, 4]); d2 = sb("d2", [128, 4]); rden = sb("rden", [128, 4])
        so = eo; hg = hi; st4 = sb("st4", [128, 4, 6]); mv4 = sb("mv4", [128, 4, 2])
        rs4 = sb("rs4", [128, 4]); sz = ez; ybf = sb("ybf", [128, D], BF16)
        Eown = sb("Eown", [128, 512], BF16); Eprev = sb("Eprev", [128, 512], BF16); Emeta = sb("Emeta", [16, 512], BF16)
        Eown2 = Eown; Eprev2 = Eprev
        dsum = sb("dsum", [128, 4]); rsa = sb("rsa", [128, 4]); ya = sb("ya", [128, 4, 64]); saz = eaz
        yT = sb("yT", [128, 8, 128], BF16)

        def layer_norm(src, dst_f, gbc, bbc, dst_b=None):
            for hf in range(2):
                V(lambda hf=hf, st6=st6: nc.vector.bn_stats(out=st6[:, hf, :], in_=src[:, hf * 512:(hf + 1) * 512]), [src], [st6])
            V(lambda st6=st6, mv=mv: nc.vector.bn_aggr(out=mv[:], in_=st6[:]), [st6], [mv])
            A(lambda mv=mv, rstd=rstd: nc.scalar.activation(out=rstd[:], in_=mv[:, 1:2], func=AF.Ln, bias=LN_EPS), [mv], [rstd])
            A(lambda rstd=rstd: nc.scalar.activation(out=rstd[:], in_=rstd[:], func=AF.Exp, scale=-0.5), [rstd], [rstd])
            V(lambda mv=mv, rstd=rstd, nmr=nmr: nc.vector.tensor_scalar(out=nmr[:], in0=mv[:, 0:1], scalar1=rstd[:, 0:1], scalar2=-1.0, op0=ALU.mult, op1=ALU.mult),
              [mv, rstd], [nmr])
            A(lambda rstd=rstd, nmr=nmr: nc.scalar.activation(out=dst_f[:], in_=src[:], func=AF.Identity, scale=rstd[:, 0:1], bias=nmr[:, 0:1]),
              [src, rstd, nmr], [dst_f])
            V(lambda: nc.vector.tensor_tensor(out=dst_f[:], in0=dst_f[:], in1=gbc[:], op=ALU.mult), [dst_f, gbc], [dst_f])
            V(lambda: nc.vector.tensor_tensor(out=dst_f[:], in0=dst_f[:], in1=bbc[:], op=ALU.add), [dst_f, bbc], [dst_f])
            if dst_b is not None:
                A(lambda: nc.scalar.copy(out=dst_b[:], in_=dst_f[:]), [dst_f], [dst_b])

        def ln0(src, dst_b, hp_f=None):
            for hf in range(2):
                V(lambda hf=hf, st6=st6: nc.vector.bn_stats(out=st6[:, hf, :], in_=src[:, hf * 512:(hf + 1) * 512]), [src], [st6])
            V(lambda st6=st6, mv=mv: nc.vector.bn_aggr(out=mv[:], in_=st6[:]), [st6], [mv])
            A(lambda mv=mv, rstd=rstd: nc.scalar.activation(out=rstd[:], in_=mv[:, 1:2], func=AF.Ln, bias=LN_EPS), [mv], [rstd])
            A(lambda rstd=rstd: nc.scalar.activation(out=rstd[:], in_=rstd[:], func=AF.Exp, scale=-0.5), [rstd], [rstd])
            V(lambda mv=mv, rstd=rstd: nc.vector.tensor_scalar(out=dst_b[:], in0=src[:], scalar1=mv[:, 0:1], scalar2=rstd[:, 0:1],
                                              op0=ALU.subtract, op1=ALU.mult), [src, mv, rstd], [dst_b])
            if hp_f is not None:
                V(lambda mv=mv, rstd=rstd, nmr=nmr: nc.vector.tensor_scalar(out=nmr[:], in0=mv[:, 0:1], scalar1=rstd[:, 0:1], scalar2=-1.0, op0=ALU.mult, op1=ALU.mult),
                  [mv, rstd], [nmr])
                A(lambda rstd=rstd, nmr=nmr: nc.scalar.activation(out=hp_f[:], in_=src[:], func=AF.Identity, scale=rstd[:, 0:1], bias=nmr[:, 0:1]),
                  [src, rstd, nmr], [hp_f])

        def transpose8(src_b, dstT, np_=128):
            for k in range(8):
                T(lambda k=k: nc.tensor.transpose(out=pT[:, k * 128:k * 128 + np_], in_=src_b[0:np_, k * 128:(k + 1) * 128],
                                                  identity=idb[0:np_, 0:np_]), [src_b, idb], [pT])
            A(lambda: nc.scalar.copy(out=dstT[:, :, 0:np_], in_=pT[:].rearrange("p (k t) -> p k t", k=8)[:, :, 0:np_]), [pT], [dstT])

        NOBIAS = False

        def proj_tm(xT, c0, n, pb, nt=128, with_bias=True):
            if NOBIAS:
                with_bias = False
            if with_bias:
                bt, bap = bsl(c0, n)
                T(lambda pb=pb, bap=bap: nc.tensor.matmul(pb[0:nt, 0:n], lhsT=ones1b[:, 0:nt], rhs=bap, start=True, stop=False),
                  [ones1b, bt], [pb])
            for k in range(8):
                tl, wap = wsl(k, c0, n)
                T(lambda k=k, pb=pb, xT=xT, wap=wap: nc.tensor.matmul(pb[0:nt, 0:n], lhsT=xT[:, k, 0:nt], rhs=wap,
                                               start=(k == 0 and not with_bias), stop=(k == 7)), [xT, tl], [pb])

        def proj_fm(xT, c0, m, out_ap, pb, nt=128):
            bt, bap = bsl(c0, m)
            T(lambda pb=pb, bap=bap: nc.tensor.matmul(out_ap, lhsT=bap, rhs=ones1b[:, 0:nt], start=True, stop=False), [ones1b, bt], [pb])
            for k in range(8):
                tl, wap = wsl(k, c0, m)
                T(lambda k=k, pb=pb, xT=xT, wap=wap: nc.tensor.matmul(out_ap, lhsT=wap, rhs=xT[:, k, 0:nt], start=False, stop=(k == 7)),
                  [xT, tl], [pb])

        def gate_rows(xT, slot, own):
            for k in range(8):
                T(lambda k=k, xT=xT: nc.tensor.matmul(pS[:, 280:288], lhsT=xT[:, k, :], rhs=w1[:, k, O_MI:O_MI + 8],
                                               start=(k == 0), stop=(k == 7)), [xT, w1], [pS])
            V(lambda gt=gt: nc.vector.tensor_tensor(out=gt[:], in0=pS[:, 280:288], in1=biasg[:], op=ALU.add), [pS, biasg], [gt])
            T(lambda gt=gt: nc.tensor.transpose(out=pS[0:4, 0:128], in_=gt[:, 0:4], identity=idf[:]), [gt, idf], [pS])
            T(lambda gt=gt: nc.tensor.transpose(out=pS[0:4, 128:256], in_=gt[:, 4:8], identity=idf[:]), [gt, idf], [pS])
            A(lambda spe=spe: nc.scalar.activation(out=spe[:], in_=pS[0:4, 128:256], func=AF.Exp, scale=-1.0), [pS], [spe])
            A(lambda spe=spe, spr=spr: nc.scalar.activation(out=spr[:], in_=spe[:], func=AF.Ln, bias=1.0), [spe], [spr])
            if own:
                V(lambda lim=lim: nc.vector.tensor_copy(out=lim[:], in_=pS[0:4, 0:128]), [pS], [lim])
                spsrc = spr
            else:
                sl = slice(slot * 128, (slot + 1) * 128)
                rvt = rvR.next(); rnt = rnR.next()
                LD(rvt, rvt[:], rvalid[:, sl])
                LD(rnt, rnt[:], rneg[:, sl])
                V(lambda rvt=rvt, spr=spr, spm=spm: nc.vector.tensor_tensor(out=spm[:], in0=spr[:], in1=rvt[:], op=ALU.mult), [spr, rvt], [spm])
                V(lambda rnt=rnt, lim=lim: nc.vector.tensor_tensor(out=lim[:], in0=pS[0:4, 0:128], in1=rnt[:], op=ALU.add), [pS, rnt], [lim])
                spsrc = spm
            bp = bnegR.cur(); bc = bnegR.next()
            V(lambda: nc.vector.tensor_tensor_scan(out=bc[:], data0=spsrc[:], data1=zer4[:], initial=bp[:, 127:128],
                                                   op0=ALU.add, op1=ALU.add), [spsrc, zer4, bp], [bc])
            V(lambda lim=lim, u_r=u_r: nc.vector.tensor_tensor(out=u_r[:], in0=lim[:], in1=bc[:], op=ALU.add), [lim, bc], [u_r])
            Up = UR.cur(); Uc = UR.next()
            V(lambda Uc=Uc, u_r=u_r: nc.vector.tensor_tensor_scan(out=Uc[:], data0=u_r[:], data1=u_r[:], initial=Up[:, 127:128],
                                                   op0=ALU.max, op1=ALU.max), [u_r, Up], [Uc])
            T(lambda u_r=u_r: nc.tensor.transpose(out=pS[:, 256:260], in_=u_r[:], identity=idf[0:4, 0:4]), [u_r, idf], [pS])
            ncol = 4
            if own:
                V(lambda Uc=Uc, m_r=m_r: nc.vector.tensor_tensor(out=m_r[:], in0=Uc[:], in1=bc[:], op=ALU.subtract), [Uc, bc], [m_r])
                T(lambda Uc=Uc: nc.tensor.transpose(out=pS[:, 260:264], in_=Uc[:], identity=idf[0:4, 0:4]), [Uc, idf], [pS])
                T(lambda m_r=m_r: nc.tensor.transpose(out=pS[:, 264:268], in_=m_r[:], identity=idf[0:4, 0:4]), [m_r, idf], [pS])
                ncol = 12
            V(lambda cols=cols: nc.vector.tensor_copy(out=cols[:, 0:ncol], in_=pS[:, 256:256 + ncol]), [pS], [cols])
            V(lambda Uc=Uc, diagU=diagU: nc.vector.tensor_scalar(out=diagU[:], in0=idf[0:4, 0:4], scalar1=Uc[:, 127:128], scalar2=None, op0=ALU.mult),
              [idf, Uc], [diagU])
            T(lambda diagU=diagU: nc.tensor.matmul(pS[:, 272:276], lhsT=ones4[:], rhs=diagU[:], start=True, stop=True), [ones4, diagU], [pS])
            Uprev = UendR.cur(); Uend = UendR.next()
            V(lambda: nc.vector.tensor_copy(out=Uend[:], in_=pS[:, 272:276]), [pS], [Uend])
            V(lambda cols=cols, exin=exin: nc.vector.tensor_tensor(out=exin[:, 0:4], in0=cols[:, 0:4], in1=Uend[:], op=ALU.subtract), [cols, Uend], [exin])
            V(lambda exin=exin: nc.vector.tensor_tensor(out=exin[:, 4:8], in0=Uprev[:], in1=Uend[:], op=ALU.subtract), [Uprev, Uend], [exin])
            ne = 8
            if own:
                V(lambda cols=cols, exin=exin: nc.vector.tensor_tensor(out=exin[:, 8:12], in0=Uprev[:], in1=cols[:, 4:8], op=ALU.subtract), [Uprev, cols], [exin])
                V(lambda cols=cols, exin=exin: nc.vector.tensor_scalar(out=exin[:, 12:16], in0=cols[:, 8:12], scalar1=-1.0, scalar2=None, op0=ALU.mult), [cols], [exin])
                ne = 16
            A(lambda exin=exin, ex=ex: nc.scalar.activation(out=ex[:, 0:ne], in_=exin[:, 0:ne], func=AF.Exp), [exin], [ex])
            return Uc

        def state_update():
            V(lambda ktm=ktm, kw=kw, ex=ex: nc.vector.tensor_tensor(out=kw[:], in0=ktm[:], in1=ex[:, 0:4].unsqueeze(2).to_broadcast([128, 4, 128]),
                                                                    op=ALU.mult), [ktm, ex], [kw])
            pb = GB.next()
            for h in range(4):
                T(lambda h=h, pb=pb, vbf=vbf, kw=kw: nc.tensor.matmul(pb[:, h * 128:(h + 1) * 128], lhsT=kw[:, h, :], rhs=vbf[:, h, :], start=True, stop=True),
                  [kw, vbf], [pb])
            for h in range(4):
                T(lambda h=h, kw=kw: nc.tensor.matmul(pS[:, 296 + h:297 + h], lhsT=kw[:, h, :], rhs=onescol[:], start=True, stop=True),
                  [kw, onescol], [pS])
            V(lambda ex=ex: nc.vector.tensor_tensor(out=Cst[:], in0=Cst[:], in1=ex[:, 4:8].unsqueeze(2).to_broadcast([128, 4, 128]), op=ALU.mult),
              [Cst, ex], [Cst])
            V(lambda pb=pb: nc.vector.tensor_tensor(out=Cst[:].rearrange("p h d -> p (h d)"), in0=Cst[:].rearrange("p h d -> p (h d)"), in1=pb[:], op=ALU.add),
              [Cst, pb], [Cst])
            V(lambda ex=ex: nc.vector.tensor_tensor(out=nst[:], in0=nst[:], in1=ex[:, 4:8], op=ALU.mult), [nst, ex], [nst])
            V(lambda: nc.vector.tensor_tensor(out=nst[:], in0=nst[:], in1=pS[:, 296:300], op=ALU.add), [nst, pS], [nst])

        def swa_kv(xT, want_f32):
            akT = akTR.next(); va = vaR.next()
            pb = GB.next()
            proj_fm(xT, O_AK, 128, pb[:, 0:128], pb)
            A(lambda pb=pb, akT=akT: nc.scalar.copy(out=akT[:], in_=pb[:, 0:128]), [pb], [akT])
            pb2 = GB.next()
            proj_tm(xT, O_AK, 256, pb2)
            V(lambda pb2=pb2, va=va: nc.vector.tensor_copy(out=va[:, :, 0:64], in_=pb2[:, 128:256].rearrange("p (k d) -> p k d", k=2)), [pb2], [va])
            if want_f32:
                A(lambda pb2=pb2: nc.scalar.copy(out=kavf[:], in_=pb2[:, 0:256]), [pb2], [kavf])
            return akT, va

        def chunk(kind, src_ap, slot=None, c=None):
            own = kind == "own"
            nonlocal hb, ktm, vbf, kw, gt, spe, spr, spm, lim, u_r, m_r, cols, diagU, exin, ex, st6, mv, rstd, nmr
            hb = hbR.next()
            ktm = ktmR.next()
            vbf = vbfR.next()
            kw = kwR.next()
            gt = gtR.next()
            spe = speR.next()
            spr = sprR.next()
            spm = spmR.next()
            lim = limR.next()
            u_r = u_rR.next()
            m_r = m_rR.next()
            cols = colsR.next()
            diagU = diagUR.next()
            exin = exinR.next()
            ex = exR.next()
            st6 = st6R.next()
            mv = mvR.next()
            rstd = rstdR.next()
            nmr = nmrR.next()
            xt = xR.next()
            LD(xt, xt[:], src_ap, lat=6.0)
            hp = hpR.next() if own else None
            ln0(xt, hb, hp)
            xT = xTR.next()
            transpose8(hb, xT)
            warm(WARM_PRE if not own else WARM_OWN)
            pb = GB.next(); proj_tm(xT, O_MK, 512, pb)
            A(lambda pb=pb, ktm=ktm: nc.scalar.mul(out=ktm[:].rearrange("p h d -> p (h d)"), in_=pb[:], mul=KSCALE), [pb], [ktm])
            pb = GB.next(); proj_tm(xT, O_MV, 512, pb)
            A(lambda pb=pb, vbf=vbf: nc.scalar.copy(out=vbf[:].rearrange("p h d -> p (h d)"), in_=pb[:]), [pb], [vbf])
            Uc = gate_rows(xT, slot, own)
            if kind == "meta":
                akT, va = swa_kv(xT, True)
                V(lambda akT=akT: nc.vector.tensor_copy(out=akTm[:], in_=akT[:, 0:16]), [akT], [akTm])
                V(lambda va=va: nc.vector.tensor_copy(out=vam[:, :, 0:64], in_=va[0:16, :, 0:64]), [va], [vam])
                ST(pk_o[0:16, :], kavf, kavf[0:16, 0:128])
                ST(pv_o[0:16, :], kavf, kavf[0:16, 128:256])
            elif kind == "prelast":
                swa_kv(xT, False)
            if own and c == NOWN - 1:
                DBGDUMP("cols", cols, cols[:], [128, 12])
                DBGDUMP("ex", ex, ex[:], [128, 16])
                DBGDUMP("gt", gt, gt[:], [128, 8])
                DBGDUMP("ktm", ktm, ktm[:].rearrange("p h d -> p (h d)"), [128, 512])
                DBGDUMP("vbf", vbf, vbf[:].rearrange("p h d -> p (h d)"), [128, 512])
                DBGDUMP("hp", hp, hp[:], [128, 1024])
                DBGDUMP("Cpre", Cst, Cst[:].rearrange("p h d -> p (h d)"), [128, 512])
            if own:
                own_chunk(xT, hp, c, Uc)
            state_update()
            if own:
                A(lambda: nc.scalar.copy(out=Cbf[:], in_=Cst[:]), [Cst], [Cbf])
                A(lambda: nc.scalar.copy(out=nbf[:], in_=nst[:]), [nst], [nbf])
                if c == NOWN - 1:
                    ST(pc_o.rearrange("h k v -> k h v"), Cst, Cst[:])
                    ST(pn_o.rearrange("h k -> k h"), nst, nst[:], allow_slow_non_contiguous=True)
                    ST(pm_o, m_r, m_r[:, 127:128])
            elif kind == "prelast" or (kind == "meta" and NPRE == 0):
                A(lambda: nc.scalar.copy(out=Cbf[:], in_=Cst[:]), [Cst], [Cbf])
                A(lambda: nc.scalar.copy(out=nbf[:], in_=nst[:]), [nst], [nbf])

        def own_chunk(xT, hp, c, Uc):
            akTp, vap = akTR.cur(), vaR.cur()
            last = (c == NOWN - 1)
            pb = GB.next()
            for h in range(4):
                proj_fm(xT, O_MQ + h * 128, 128, pb[:, h * 128:(h + 1) * 128], pb)
            A(lambda pb=pb, h=h: nc.scalar.copy(out=qT[:].rearrange("p h t -> p (h t)"), in_=pb[:]), [pb], [qT])
            for h in range(4):
                T(lambda h=h, ktm=ktm: nc.tensor.transpose(out=pT[:, h * 128:(h + 1) * 128], in_=ktm[:, h, :], identity=idb[:]), [ktm, idb], [pT])
            A(lambda: nc.scalar.copy(out=kT[:].rearrange("p h t -> p (h t)"), in_=pT[:, 0:512]), [pT], [kT])
            warm(WARM_OWN)
            for h in range(4):
                V(lambda h=h, Uc=Uc: nc.vector.tensor_scalar(out=bd[:, h, :], in0=Uc[:], scalar1=negid4[:, h:h + 1], scalar2=None, op0=ALU.mult),
                  [Uc, negid4], [bd])
            pU = GB.next()
            T(lambda pU=pU, h=h: nc.tensor.matmul(pU[:], lhsT=ones4[:], rhs=bd[:].rearrange("p h t -> p (h t)"), start=True, stop=True), [ones4, bd], [pU])
            for h in range(4):
                A(lambda h=h, pU=pU, cols=cols: nc.scalar.activation(out=Wt[:, h, :], in_=pU[:, h * 128:(h + 1) * 128], func=AF.Exp, bias=cols[:, h:h + 1]),
                  [pU, cols], [Wt])
            V(lambda h=h: nc.vector.tensor_tensor(out=Wm[:].rearrange("p h t -> p (h t)"), in0=Wt[:].rearrange("p h t -> p (h t)"),
                                              in1=tri4[:], op=ALU.mult), [Wt, tri4], [Wm])
            pSc = GB.next()
            for h in range(4):
                T(lambda h=h, pSc=pSc: nc.tensor.matmul(pSc[:, h * 128:(h + 1) * 128], lhsT=kT[:, h, :], rhs=qT[:, h, :], start=True, stop=True),
                  [kT, qT], [pSc])
            V(lambda pSc=pSc, h=h: nc.vector.tensor_tensor(out=PTt[:].rearrange("p h t -> p (h t)"), in0=pSc[:], in1=Wm[:].rearrange("p h t -> p (h t)"),
                                              op=ALU.mult), [pSc, Wm], [PTt])
            pN = GB.next(); pI = GB.next()
            for h in range(4):
                T(lambda h=h, pN=pN, vbf=vbf: nc.tensor.matmul(pN[:, h * 128:(h + 1) * 128], lhsT=PTt[:, h, :], rhs=vbf[:, h, :], start=True, stop=True),
                  [PTt, vbf], [pN])
            for h in range(4):
                T(lambda h=h, pI=pI: nc.tensor.matmul(pI[:, h * 128:(h + 1) * 128], lhsT=qT[:, h, :], rhs=Cbf[:, h, :], start=True, stop=True),
                  [qT, Cbf], [pI])
            for h in range(4):
                T(lambda h=h: nc.tensor.matmul(pS[:, 288 + h:289 + h], lhsT=PTt[:, h, :], rhs=onescol[:], start=True, stop=True),
                  [PTt, onescol], [pS])
            for h in range(4):
                T(lambda h=h: nc.tensor.matmul(pS[:, 292 + h:293 + h], lhsT=qT[:, h, :], rhs=nbf[:, h:h + 1], start=True, stop=True),
                  [qT, nbf], [pS])
            for h in range(4):
                A(lambda h=h, pI=pI, ex=ex: nc.scalar.mul(out=hi[:, h, :], in_=pI[:, h * 128:(h + 1) * 128], mul=ex[:, 8 + h:9 + h]),
                  [pI, ex], [hi])
            V(lambda pN=pN, h=h: nc.vector.tensor_tensor(out=hn[:].rearrange("p h t -> p (h t)"), in0=hi[:].rearrange("p h t -> p (h t)"), in1=pN[:],
                                              op=ALU.add), [hi, pN], [hn])
            V(lambda ex=ex: nc.vector.tensor_tensor(out=d1[:], in0=pS[:, 292:296], in1=ex[:, 8:12], op=ALU.mult), [pS, ex], [d1])
            V(lambda: nc.vector.tensor_tensor(out=d2[:], in0=d1[:], in1=pS[:, 288:292], op=ALU.add), [d1, pS], [d2])
            V(lambda: nc.vector.scalar_tensor_tensor(out=d1[:], in0=d2[:], scalar=-1.0, in1=d2[:], op0=ALU.mult, op1=ALU.max), [d2], [d1])
            V(lambda ex=ex: nc.vector.tensor_tensor(out=d2[:], in0=d1[:], in1=ex[:, 12:16], op=ALU.max), [d1, ex], [d2])
            V(lambda: nc.vector.reciprocal(out=rden[:], in_=d2[:]), [d2], [rden])
            pb = GB.next(); proj_tm(xT, O_MO, 512, pb)
            A(lambda pb=pb: nc.scalar.activation(out=eo[:], in_=pb[:], func=AF.Exp, scale=-1.0), [pb], [eo])
            A(lambda: nc.scalar.activation(out=so[:], in_=eo[:], func=AF.Ln, bias=1.0), [eo], [so])
            A(lambda: nc.scalar.activation(out=so[:], in_=so[:], func=AF.Exp, scale=-1.0), [so], [so])
            V(lambda: nc.vector.tensor_tensor(out=hg[:], in0=hn[:], in1=rden[:].unsqueeze(2).to_broadcast([128, 4, 128]), op=ALU.mult),
              [hn, rden], [hg])
            V(lambda h=h: nc.vector.tensor_tensor(out=hg[:].rearrange("p h t -> p (h t)"), in0=hg[:].rearrange("p h t -> p (h t)"), in1=so[:],
                                              op=ALU.mult), [hg, so], [hg])
            for h in range(4):
                V(lambda h=h: nc.vector.bn_stats(out=st4[:, h, :], in_=hg[:, h, :]), [hg], [st4])
            for h in range(4):
                V(lambda h=h: nc.vector.bn_aggr(out=mv4[:, h, :], in_=st4[:, h, :]), [st4], [mv4])
            A(lambda: nc.scalar.activation(out=rs4[:].unsqueeze(2), in_=mv4[:, :, 1:2], func=AF.Ln, bias=LN_EPS), [mv4], [rs4])
            A(lambda: nc.scalar.activation(out=rs4[:], in_=rs4[:], func=AF.Exp, scale=-0.5), [rs4], [rs4])
            for h in range(4):
                V(lambda h=h: nc.vector.tensor_scalar(out=hg[:, h, :], in0=hg[:, h, :], scalar1=mv4[:, h, 0:1], scalar2=rs4[:, h:h + 1],
                                                      op0=ALU.subtract, op1=ALU.mult), [hg, mv4, rs4], [hg])
            pb = GB.next(); proj_tm(xT, O_MZ, 512, pb)
            A(lambda pb=pb: nc.scalar.activation(out=ez[:], in_=pb[:], func=AF.Exp, scale=-1.0), [pb], [ez])
            A(lambda: nc.scalar.activation(out=ez[:], in_=ez[:], func=AF.Ln, bias=1.0), [ez], [ez])
            A(lambda: nc.scalar.activation(out=ez[:], in_=ez[:], func=AF.Exp, scale=-1.0), [ez], [ez])
            V(lambda pb=pb: nc.vector.tensor_tensor(out=ez[:], in0=ez[:], in1=pb[:], op=ALU.mult), [ez, pb], [ez])
            V(lambda h=h: nc.vector.tensor_tensor(out=ybf[:, 0:512], in0=hg[:].rearrange("p h t -> p (h t)"), in1=sz[:], op=ALU.mult),
              [hg, sz], [ybf])

            akT, va = swa_kv(xT, last)
            if last:
                ST(pk_o[16:144, :], kavf, kavf[:, 0:128])
                ST(pv_o[16:144, :], kavf, kavf[:, 128:256])
            pb = GB.next(); proj_tm(xT, O_AQ, 512, pb)
            A(lambda pb=pb: nc.scalar.copy(out=Wt[:].rearrange("p g (k d) -> p g k d", k=2),
                                           in_=pb[:].rearrange("p (k g d) -> p g k d", k=2, g=4)), [pb], [Wt])
            for g in range(4):
                T(lambda g=g: nc.tensor.transpose(out=pT[:, g * 128:(g + 1) * 128], in_=Wt[:, g, :], identity=idb[:]), [Wt, idb], [pT])
            V(lambda: nc.vector.tensor_copy(out=aqT[:].rearrange("p g t -> p (g t)"), in_=pT[:, 0:512]), [pT], [aqT])

            pb = GB.next(); proj_tm(xT, O_AZ, 512, pb)
            A(lambda pb=pb: nc.scalar.activation(out=eaz[:], in_=pb[:], func=AF.Exp, scale=-1.0), [pb], [eaz])
            A(lambda: nc.scalar.activation(out=eaz[:], in_=eaz[:], func=AF.Ln, bias=1.0), [eaz], [eaz])
            A(lambda: nc.scalar.activation(out=eaz[:], in_=eaz[:], func=AF.Exp, scale=-1.0), [eaz], [eaz])
            V(lambda pb=pb: nc.vector.tensor_tensor(out=eaz[:], in0=eaz[:], in1=pb[:], op=ALU.mult), [eaz, pb], [eaz])
            warm(WARM_OWN)
            pmask = pm1b if c == 0 else prev4
            for kap in range(2):
                ks = slice(64 * kap, 64 * kap + 64)
                pb = GB.next()
                for g in range(4):
                    T(lambda g=g, pb=pb, akT=akT, ks=ks: nc.tensor.matmul(pb[:, g * 128:(g + 1) * 128], lhsT=akT[ks, :], rhs=aqT[ks, g, :], start=True, stop=True),
                      [akT, aqT], [pb])
                A(lambda pb=pb: nc.scalar.activation(out=Eown[:], in_=pb[:], func=AF.Exp, scale=ASCALE), [pb], [Eown])
                V(lambda: nc.vector.tensor_tensor(out=Eown2[:], in0=Eown[:], in1=tri4[:], op=ALU.mult), [Eown, tri4], [Eown2])
                pb = GB.next()
                for g in range(4):
                    T(lambda g=g, pb=pb, akTp=akTp, ks=ks: nc.tensor.matmul(pb[:, g * 128:(g + 1) * 128], lhsT=akTp[ks, :], rhs=aqT[ks, g, :], start=True, stop=True),
                      [akTp, aqT], [pb])
                A(lambda pb=pb: nc.scalar.activation(out=Eprev[:], in_=pb[:], func=AF.Exp, scale=ASCALE), [pb], [Eprev])
                V(lambda pmask=pmask: nc.vector.tensor_tensor(out=Eprev2[:], in0=Eprev[:], in1=pmask[:], op=ALU.mult), [Eprev, pmask], [Eprev2])
                pb = GB.next()
                for g in range(4):
                    T(lambda g=g, pb=pb, ks=ks: nc.tensor.matmul(pb[0:16, g * 128:(g + 1) * 128], lhsT=akTm[ks, :], rhs=aqT[ks, g, :], start=True, stop=True),
                      [akTm, aqT], [pb])
                A(lambda pb=pb: nc.scalar.activation(out=Emeta[:], in_=pb[0:16, :], func=AF.Exp, scale=ASCALE), [pb], [Emeta])
                po = GB.next()
                for g in range(4):
                    oap = po[:, g * 65:(g + 1) * 65]
                    T(lambda g=g, oap=oap, po=po, kap=kap: nc.tensor.matmul(oap, lhsT=Emeta[:, g * 128:(g + 1) * 128], rhs=vam[:, kap, :], start=True, stop=False),
                      [Emeta, vam], [po])
                    T(lambda g=g, oap=oap, po=po, vap=vap, kap=kap: nc.tensor.matmul(oap, lhsT=Eprev2[:, g * 128:(g + 1) * 128], rhs=vap[:, kap, :], start=False, stop=False),
                      [Eprev2, vap], [po])
                    T(lambda g=g, oap=oap, po=po, va=va, kap=kap: nc.tensor.matmul(oap, lhsT=Eown2[:, g * 128:(g + 1) * 128], rhs=va[:, kap, :], start=False, stop=True),
                      [Eown2, va], [po])
                po3 = po[:, 0:260].rearrange("p (g e) -> p g e", g=4)
                V(lambda po3=po3, po=po, kap=kap: nc.vector.tensor_tensor(out=dsum[:].unsqueeze(2), in0=po3[:, :, 64:65],
                                                          in1=esink[:, 4 * kap:4 * kap + 4].unsqueeze(2), op=ALU.add),
                  [po, esink], [dsum])
                V(lambda: nc.vector.reciprocal(out=rsa[:], in_=dsum[:]), [dsum], [rsa])
                V(lambda po3=po3, po=po: nc.vector.tensor_tensor(out=ya[:], in0=po3[:, :, 0:64], in1=rsa[:].unsqueeze(2).to_broadcast([128, 4, 64]),
                                                          op=ALU.mult), [po, rsa], [ya])
                V(lambda kap=kap, g=g: nc.vector.tensor_tensor(out=ybf[:, 512 + 256 * kap:768 + 256 * kap], in0=ya[:].rearrange("p g d -> p (g d)"),
                                                  in1=saz[:, 256 * kap:256 * kap + 256], op=ALU.mult), [ya, saz], [ybf])

            if c == NOWN - 1:
                DBGDUMP("ybf", ybf, ybf[:], [128, 1024])
                DBGDUMP("hn", hi, hi[:].rearrange("p h d -> p (h d)"), [128, 512])
            transpose8(ybf, yT)
            warm(WARM_OWN)
            pm_ = [GB.next(), GB.next()]
            for hf in range(2):
                T(lambda hf=hf: nc.tensor.matmul(pm_[hf][:], lhsT=ones1b[:, 0:128], rhs=bob[:, hf * 512:(hf + 1) * 512], start=True, stop=False),
                  [ones1b, bob], [pm_[hf]])
                for k in range(8):
                    T(lambda k=k, hf=hf: nc.tensor.matmul(pm_[hf][:], lhsT=yT[:, k, :], rhs=wo_bf[:, k, hf * 512:(hf + 1) * 512],
                                                          start=False, stop=(k == 7)), [yT, wo_bf], [pm_[hf]])
            V(lambda hp=hp: nc.vector.tensor_tensor(out=hp[:], in0=hp[:], in1=ln0g[:], op=ALU.mult), [hp, ln0g], [hp])
            for hf in range(2):
                V(lambda hf=hf, hp=hp: nc.vector.tensor_tensor(out=hp[:, hf * 512:(hf + 1) * 512], in0=hp[:, hf * 512:(hf + 1) * 512],
                                                               in1=pm_[hf][:], op=ALU.add),
                  [hp, pm_[hf]], [hp])
            layer_norm(hp, hp, lng, lnb)
            ST(y_o[c * 128:(c + 1) * 128, :], hp, hp[:])


        def sample_program():
            nonlocal hb, ktm, vbf, kw, gt, spe, spr, spm, lim, u_r, m_r, cols, diagU, exin, ex, st6, mv, rstd, nmr
            hb = hbR.next()
            ktm = ktmR.next()
            vbf = vbfR.next()
            kw = kwR.next()
            gt = gtR.next()
            spe = speR.next()
            spr = sprR.next()
            spm = spmR.next()
            lim = limR.next()
            u_r = u_rR.next()
            m_r = m_rR.next()
            cols = colsR.next()
            diagU = diagUR.next()
            exin = exinR.next()
            ex = exR.next()
            st6 = st6R.next()
            mv = mvR.next()
            rstd = rstdR.next()
            nmr = nmrR.next()
            seqcol = sb("seqcol", [128, 16])
            blk4 = sb("blk4", [128, 512], BF16); winm = sb("winm", [128, 256], BF16); newm = sb("newm", [128, 256], BF16)
            esk16 = sb("esk16", [16, 2])
            LD(seqcol, seqcol[:], cseqcol)
            LD(cstage, cstage[:, 0:512], cblk4)
            V(lambda: nc.vector.tensor_copy(out=blk4[:], in_=cstage[:, 0:512]), [cstage], [blk4])
            LD(cstage, cstage[:, 0:256], cwin)
            V(lambda: nc.vector.tensor_copy(out=winm[:], in_=cstage[:, 0:256]), [cstage], [winm])
            LD(cstage, cstage[:, 0:256], cnew)
            V(lambda: nc.vector.tensor_copy(out=newm[:], in_=cstage[:, 0:256]), [cstage], [newm])
            LD(esk16, esk16[:], sinks16)
            A(lambda: nc.scalar.activation(out=esk16[:], in_=esk16[:], func=AF.Exp), [esk16], [esk16])

            xt = xR.next()
            V(lambda xt=xt: nc.vector.memset(xt[:], 0.0), [], [xt])
            LD(xt, xt[0:64, :], xs)
            hp = hpR.next()
            ln0(xt, hb, hp)
            xT = xTR.next()
            transpose8(hb, xT)
            vaug = sb("vaug", [128, 4, 129], BF16)
            V(lambda: nc.vector.memset(vaug[:], 1.0), [], [vaug])
            pb = GB.next(); proj_tm(xT, O_MK, 512, pb)
            A(lambda pb=pb, ktm=ktm: nc.scalar.mul(out=ktm[:].rearrange("p h d -> p (h d)"), in_=pb[:], mul=KSCALE), [pb], [ktm])
            pb = GB.next(); proj_tm(xT, O_MV, 512, pb)
            V(lambda pb=pb: nc.vector.tensor_copy(out=vaug[:, :, 0:128], in_=pb[:].rearrange("p (h d) -> p h d", h=4)), [pb], [vaug])

            for k in range(8):
                T(lambda k=k, xT=xT: nc.tensor.matmul(pS[:, 280:288], lhsT=xT[:, k, :], rhs=w1[:, k, O_MI:O_MI + 8],
                                                      start=(k == 0), stop=(k == 7)), [xT, w1], [pS])
            V(lambda gt=gt: nc.vector.tensor_tensor(out=gt[:], in0=pS[:, 280:288], in1=biasg[:], op=ALU.add), [pS, biasg], [gt])
            T(lambda gt=gt: nc.tensor.transpose(out=pS[0:4, 0:128], in_=gt[:, 0:4], identity=idf[:]), [gt, idf], [pS])
            T(lambda gt=gt: nc.tensor.transpose(out=pS[0:4, 128:256], in_=gt[:, 4:8], identity=idf[:]), [gt, idf], [pS])
            A(lambda spe=spe: nc.scalar.activation(out=spe[:], in_=pS[0:4, 128:256], func=AF.Exp, scale=-1.0), [pS], [spe])
            A(lambda spe=spe, spr=spr: nc.scalar.activation(out=spr[:], in_=spe[:], func=AF.Ln, bias=1.0), [spe], [spr])
            V(lambda lim=lim: nc.vector.tensor_copy(out=lim[:], in_=pS[0:4, 0:128]), [pS], [lim])
            m0 = sb("m0", [4, 16]); bs = sb("bs", [4, 128]); Us = sb("Us", [4, 128]); ueb = sb("ueb", [4, 128]); m0b = sb("m0b", [4, 128])
            decr = sb("decr", [4, 16]); bdd = sb("bdd", [4, 16, 4]); decbc = sb("decbc", [128, 64])
            LD(m0, m0[:], sm_i)
            for t_ in (bs, Us, ueb, m0b):
                V(lambda t_=t_: nc.vector.memset(t_[:], 0.0), [], [t_])
            v3 = lambda t_: t_[:, 0:64].rearrange("p (j l) -> p j l", l=4)
            V(lambda: nc.vector.tensor_copy(out=v3(bs)[:, :, 0:1], in_=v3(spr)[:, :, 0:1]), [spr], [bs])
            for l in range(1, 4):
                V(lambda l=l: nc.vector.tensor_tensor(out=v3(bs)[:, :, l:l + 1], in0=v3(bs)[:, :, l - 1:l], in1=v3(spr)[:, :, l:l + 1], op=ALU.add),
                  [bs, spr], [bs])
            V(lambda lim=lim, u_r=u_r: nc.vector.tensor_tensor(out=u_r[:], in0=lim[:], in1=bs[:], op=ALU.add), [lim, bs], [u_r])
            V(lambda: nc.vector.tensor_tensor(out=v3(Us)[:, :, 0:1], in0=v3(u_r)[:, :, 0:1], in1=m0[:].unsqueeze(2), op=ALU.max), [u_r, m0], [Us])
            for l in range(1, 4):
                V(lambda l=l: nc.vector.tensor_tensor(out=v3(Us)[:, :, l:l + 1], in0=v3(Us)[:, :, l - 1:l], in1=v3(u_r)[:, :, l:l + 1], op=ALU.max),
                  [Us, u_r], [Us])
            V(lambda m_r=m_r: nc.vector.tensor_tensor(out=m_r[:], in0=Us[:], in1=bs[:], op=ALU.subtract), [Us, bs], [m_r])
            V(lambda: nc.vector.tensor_copy(out=v3(ueb), in_=v3(Us)[:, :, 3:4].to_broadcast([4, 16, 4])), [Us], [ueb])
            V(lambda: nc.vector.tensor_copy(out=v3(m0b), in_=m0[:].unsqueeze(2).to_broadcast([4, 16, 4])), [m0], [m0b])
            mnew = sb("mnew", [4, 16])
            V(lambda: nc.vector.tensor_copy(out=mnew[:].unsqueeze(2), in_=v3(m_r)[:, :, 3:4]), [m_r], [mnew])
            ST(sm_o, mnew, mnew[:])
            cols5 = sb("cols5", [128, 20])
            for i_, rt in enumerate((u_r, Us, m_r, ueb, m0b)):
                T(lambda i_=i_, rt=rt: nc.tensor.transpose(out=pS[:, 256 + 4 * i_:260 + 4 * i_], in_=rt[:], identity=idf[0:4, 0:4]), [rt, idf], [pS])
            V(lambda: nc.vector.tensor_copy(out=cols5[:], in_=pS[:, 256:276]), [pS], [cols5])
            V(lambda cols=cols: nc.vector.tensor_copy(out=cols[:, 0:4], in_=cols5[:, 0:4]), [cols5], [cols])
            V(lambda exin=exin: nc.vector.tensor_tensor(out=exin[:, 0:4], in0=cols5[:, 0:4], in1=cols5[:, 12:16], op=ALU.subtract), [cols5], [exin])
            V(lambda exin=exin: nc.vector.tensor_tensor(out=exin[:, 8:12], in0=cols5[:, 16:20], in1=cols5[:, 4:8], op=ALU.subtract), [cols5], [exin])
            V(lambda exin=exin: nc.vector.tensor_scalar(out=exin[:, 12:16], in0=cols5[:, 8:12], scalar1=-1.0, scalar2=None, op0=ALU.mult), [cols5], [exin])
            V(lambda exin=exin: nc.vector.memset(exin[:, 4:8], 0.0), [], [exin])
            A(lambda exin=exin, ex=ex: nc.scalar.activation(out=ex[:, 0:16], in_=exin[:, 0:16], func=AF.Exp), [exin], [ex])
            V(lambda: nc.vector.tensor_tensor(out=decr[:].unsqueeze(2), in0=m0[:].unsqueeze(2), in1=v3(Us)[:, :, 3:4], op=ALU.subtract), [m0, Us], [decr])
            A(lambda: nc.scalar.activation(out=decr[:], in_=decr[:], func=AF.Exp), [decr], [decr])
            for h in range(4):
                V(lambda h=h: nc.vector.tensor_scalar(out=bdd[:, :, h:h + 1], in0=decr[:].unsqueeze(2), scalar1=idf[0:4, h:h + 1], scalar2=None, op0=ALU.mult),
                  [decr, idf], [bdd])
            T(lambda: nc.tensor.matmul(pS[:, 300:364], lhsT=ones4[:], rhs=bdd[:].rearrange("p j h -> p (j h)"), start=True, stop=True),
              [ones4, bdd], [pS])
            V(lambda: nc.vector.tensor_copy(out=decbc[:], in_=pS[:, 300:364]), [pS], [decbc])

            akT, va = swa_kv(xT, True)
            pb = GB.next(); proj_tm(xT, O_MO, 512, pb)
            A(lambda pb=pb: nc.scalar.activation(out=eo[:], in_=pb[:], func=AF.Exp, scale=-1.0), [pb], [eo])
            pb = GB.next(); proj_tm(xT, O_MZ, 512, pb)
            A(lambda pb=pb: nc.scalar.activation(out=ez[:], in_=pb[:], func=AF.Exp, scale=-1.0), [pb], [ez])
            A(lambda: nc.scalar.activation(out=ez[:], in_=ez[:], func=AF.Ln, bias=1.0), [ez], [ez])
            A(lambda: nc.scalar.activation(out=ez[:], in_=ez[:], func=AF.Exp, scale=-1.0), [ez], [ez])
            V(lambda pb=pb: nc.vector.tensor_tensor(out=ez[:], in0=ez[:], in1=pb[:], op=ALU.mult), [ez, pb], [ez])
            pb = GB.next(); proj_tm(xT, O_AZ, 512, pb)
            A(lambda pb=pb: nc.scalar.activation(out=eaz[:], in_=pb[:], func=AF.Exp, scale=-1.0), [pb], [eaz])
            A(lambda: nc.scalar.activation(out=eaz[:], in_=eaz[:], func=AF.Ln, bias=1.0), [eaz], [eaz])
            A(lambda: nc.scalar.activation(out=eaz[:], in_=eaz[:], func=AF.Exp, scale=-1.0), [eaz], [eaz])
            V(lambda pb=pb: nc.vector.tensor_tensor(out=eaz[:], in0=eaz[:], in1=pb[:], op=ALU.mult), [eaz, pb], [eaz])
            pb = GB.next()
            for h in range(4):
                proj_fm(xT, O_MQ + h * 128, 128, pb[:, h * 128:(h + 1) * 128], pb)
            A(lambda pb=pb: nc.scalar.copy(out=qT[:].rearrange("p h t -> p (h t)"), in_=pb[:]), [pb], [qT])
            for h in range(4):
                T(lambda h=h, ktm=ktm: nc.tensor.transpose(out=pT[:, h * 128:(h + 1) * 128], in_=ktm[:, h, :], identity=idb[:]), [ktm, idb], [pT])
            A(lambda: nc.scalar.copy(out=kT[:].rearrange("p h t -> p (h t)"), in_=pT[:, 0:512]), [pT], [kT])
            pb = GB.next(); proj_tm(xT, O_AQ, 512, pb)
            A(lambda pb=pb: nc.scalar.copy(out=Wt[:].rearrange("p g (k d) -> p g k d", k=2),
                                           in_=pb[:].rearrange("p (k g d) -> p g k d", k=2, g=4)), [pb], [Wt])
            for g in range(4):
                T(lambda g=g: nc.tensor.transpose(out=pT[:, g * 128:(g + 1) * 128], in_=Wt[:, g, :], identity=idb[:]), [Wt, idb], [pT])
            V(lambda: nc.vector.tensor_copy(out=aqT[:].rearrange("p g t -> p (g t)"), in_=pT[:, 0:512]), [pT], [aqT])

            for h in range(4):
                V(lambda h=h: nc.vector.tensor_scalar(out=bd[:, h, :], in0=Us[:], scalar1=negid4[:, h:h + 1], scalar2=None, op0=ALU.mult),
                  [Us, negid4], [bd])
            pU = GB6.tiles[4]
            T(lambda: nc.tensor.matmul(pU[:], lhsT=ones4[:], rhs=bd[:].rearrange("p h t -> p (h t)"), start=True, stop=True), [ones4, bd], [pU])
            for h in range(4):
                A(lambda h=h, cols=cols: nc.scalar.activation(out=Wt[:, h, :], in_=pU[:, h * 128:(h + 1) * 128], func=AF.Exp, bias=cols[:, h:h + 1]),
                  [pU, cols], [Wt])
            V(lambda: nc.vector.tensor_tensor(out=Wm[:].rearrange("p h t -> p (h t)"), in0=Wt[:].rearrange("p h t -> p (h t)"),
                                              in1=blk4[:], op=ALU.mult), [Wt, blk4], [Wm])
            pSc = GB6.tiles[5]
            for h in range(4):
                T(lambda h=h: nc.tensor.matmul(pSc[:, h * 128:(h + 1) * 128], lhsT=kT[:, h, :], rhs=qT[:, h, :], start=True, stop=True),
                  [kT, qT], [pSc])
            V(lambda: nc.vector.tensor_tensor(out=PTt[:].rearrange("p h t -> p (h t)"), in0=pSc[:], in1=Wm[:].rearrange("p h t -> p (h t)"),
                                              op=ALU.mult), [pSc, Wm], [PTt])
            pN = GB6.tiles[4]
            for h in range(4):
                T(lambda h=h: nc.tensor.matmul(pN[:, h * 128:(h + 1) * 128], lhsT=PTt[:, h, :], rhs=vaug[:, h, 0:128], start=True, stop=True),
                  [PTt, vaug], [pN])
            for h in range(4):
                T(lambda h=h: nc.tensor.matmul(pS[:, 288 + h:289 + h], lhsT=PTt[:, h, :], rhs=onescol[:], start=True, stop=True),
                  [PTt, onescol], [pS])
            xsp = xR.next()
            hnum_ap = xsp[:, 0:512]
            V(lambda: nc.vector.tensor_copy(out=hnum_ap, in_=pN[:]), [pN], [xsp])
            V(lambda: nc.vector.tensor_copy(out=d2[:], in_=pS[:, 288:292]), [pS], [d2])

            snl = sb("snl", [64, 128]); nT = sb("nT", [128, 64]); nTn = sb("nTn", [128, 64]); snout = snl
            LD(snl, snl[:], sn_i)
            T(lambda: nc.tensor.transpose(out=pS[:, 364:428], in_=snl[:], identity=idf[0:64, 0:64]), [snl, idf], [pS])
            V(lambda: nc.vector.tensor_copy(out=nT[:], in_=pS[:, 364:428]), [pS], [nT])
            CbR = rot("Cb", [128, 4, 129], BF16, 2)
            qmR = rot("qm", [128, 4, 64], BF16, 2); kwmR = rot("kwm", [64, 4, 128], BF16, 2)
            for t_ in qmR.tiles:
                V(lambda t_=t_: nc.vector.memset(t_[:], 0.0), [], [t_])
            for h in range(4):
                V(lambda h=h, ktm=ktm, kw=kw, ex=ex: nc.vector.tensor_scalar(out=kw[:, h, :], in0=ktm[:, h, :], scalar1=ex[:, h:h + 1], scalar2=None, op0=ALU.mult),
                  [ktm, ex], [kw])
            pIs = GB6.tiles[0:4]
            pUp = [GB6.tiles[4], GB6.tiles[5]]
            clR = Rot(wst.tiles + [t_ for t_ in hpR.tiles if t_ is not hp])
            for j in range(16):
                Clt = clR.next(); Cb = CbR.next(); qm = qmR.next(); kwm = kwmR.next()
                Cl = Clt[:, 0:516].rearrange("p (h e) -> p h e", e=129)
                LD(Clt, Cl[:, :, 0:128], sc_i[j].rearrange("h k v -> k h v"))
                G(lambda Cl=Cl, j=j: nc.gpsimd.tensor_copy(out=Cl[:, :, 128:129], in_=nT[:, 4 * j:4 * j + 4].unsqueeze(2)), [nT], [Clt])
                A(lambda Cl=Cl, Cb=Cb: nc.scalar.copy(out=Cb[:], in_=Cl), [Clt], [Cb])
                if j >= 2:
                    G(lambda qm=qm, j=j: nc.gpsimd.memset(qm[:, :, 4 * (j - 2):4 * (j - 2) + 4], 0.0), [], [qm])
                G(lambda qm=qm, j=j: nc.gpsimd.tensor_copy(out=qm[:, :, 4 * j:4 * j + 4], in_=qT[:, :, 4 * j:4 * j + 4]), [qT], [qm])
                for h in range(4):
                    T(lambda h=h, qm=qm, Cb=Cb, j=j: nc.tensor.matmul(pIs[h][0:64, 0:129], lhsT=qm[:, h, :], rhs=Cb[:, h, :],
                                                                      start=(j == 0), stop=(j == 15)), [qm, Cb], [pIs[h]])
                V(lambda kwm=kwm, j=j, kw=kw: nc.vector.tensor_scalar(out=kwm[:].rearrange("p h d -> p (h d)"), in0=kw[0:64].rearrange("p h d -> p (h d)"),
                                                              scalar1=seqcol[0:64, j:j + 1], scalar2=None, op0=ALU.mult), [kw, seqcol], [kwm])
                for h in range(4):
                    pu = pUp[h // 2]
                    T(lambda h=h, pu=pu, kwm=kwm: nc.tensor.matmul(pu[:, (h % 2) * 129:(h % 2) * 129 + 129], lhsT=kwm[:, h, :], rhs=vaug[0:64, h, :],
                                                                   start=True, stop=True), [kwm, vaug], [pu])
                for h in range(4):
                    pu = pUp[h // 2]
                    V(lambda h=h, pu=pu, Cl=Cl, j=j: nc.vector.scalar_tensor_tensor(
                        out=Cl[:, h, :], in0=Cl[:, h, :], scalar=decbc[:, 4 * j + h:4 * j + h + 1],
                        in1=pu[:, (h % 2) * 129:(h % 2) * 129 + 129], op0=ALU.mult, op1=ALU.add), [Clt, decbc, pu], [Clt])
                G(lambda Cl=Cl, j=j: nc.gpsimd.tensor_copy(out=nTn[:, 4 * j:4 * j + 4].unsqueeze(2), in_=Cl[:, :, 128:129]), [Clt], [nTn])
                ST(sc_o[j].rearrange("h k v -> k h v"), Clt, Cl[:, :, 0:128])
            T(lambda: nc.tensor.transpose(out=pS[0:64, 0:128], in_=nTn[:], identity=idf[:]), [nTn, idf], [pS])
            V(lambda: nc.vector.tensor_copy(out=snout[:], in_=pS[0:64, 0:128]), [pS], [snout])
            ST(sn_o, snout, snout[:])

            for h in range(4):
                A(lambda h=h, ex=ex: nc.scalar.mul(out=hi[0:64, h, :], in_=pIs[h][0:64, 0:128], mul=ex[0:64, 8 + h:9 + h]), [pIs[h], ex], [hi])
                V(lambda h=h, ex=ex: nc.vector.tensor_tensor(out=d1[0:64, h:h + 1], in0=pIs[h][0:64, 128:129], in1=ex[0:64, 8 + h:9 + h], op=ALU.mult),
                  [pIs[h], ex], [d1])
            V(lambda: nc.vector.tensor_tensor(out=hi[0:64].rearrange("p h t -> p (h t)"), in0=hi[0:64].rearrange("p h t -> p (h t)"),
                                              in1=xsp[0:64, 0:512], op=ALU.add), [hi, xsp], [hi])
            V(lambda: nc.vector.tensor_tensor(out=d2[0:64], in0=d1[0:64], in1=d2[0:64], op=ALU.add), [d1, d2], [d2])
            V(lambda: nc.vector.scalar_tensor_tensor(out=d1[0:64], in0=d2[0:64], scalar=-1.0, in1=d2[0:64], op0=ALU.mult, op1=ALU.max), [d2], [d1])
            V(lambda ex=ex: nc.vector.tensor_tensor(out=d2[0:64], in0=d1[0:64], in1=ex[0:64, 12:16], op=ALU.max), [d1, ex], [d2])
            V(lambda: nc.vector.reciprocal(out=rden[0:64], in_=d2[0:64]), [d2], [rden])
            A(lambda: nc.scalar.activation(out=eo[:], in_=eo[:], func=AF.Ln, bias=1.0), [eo], [eo])
            A(lambda: nc.scalar.activation(out=eo[:], in_=eo[:], func=AF.Exp, scale=-1.0), [eo], [eo])
            V(lambda: nc.vector.tensor_tensor(out=hi[0:64], in0=hi[0:64], in1=rden[0:64].unsqueeze(2).to_broadcast([64, 4, 128]), op=ALU.mult),
              [hi, rden], [hi])
            V(lambda: nc.vector.tensor_tensor(out=hi[0:64].rearrange("p h t -> p (h t)"), in0=hi[0:64].rearrange("p h t -> p (h t)"),
                                              in1=eo[0:64], op=ALU.mult), [hi, eo], [hi])
            for h in range(4):
                V(lambda h=h: nc.vector.bn_stats(out=st4[0:64, h, :], in_=hi[0:64, h, :]), [hi], [st4])
            for h in range(4):
                V(lambda h=h: nc.vector.bn_aggr(out=mv4[0:64, h, :], in_=st4[0:64, h, :]), [st4], [mv4])
            A(lambda: nc.scalar.activation(out=rs4[0:64].unsqueeze(2), in_=mv4[0:64, :, 1:2], func=AF.Ln, bias=LN_EPS), [mv4], [rs4])
            A(lambda: nc.scalar.activation(out=rs4[0:64], in_=rs4[0:64], func=AF.Exp, scale=-0.5), [rs4], [rs4])
            for h in range(4):
                V(lambda h=h: nc.vector.tensor_scalar(out=hi[0:64, h, :], in0=hi[0:64, h, :], scalar1=mv4[0:64, h, 0:1], scalar2=rs4[0:64, h:h + 1],
                                                      op0=ALU.subtract, op1=ALU.mult), [hi, mv4, rs4], [hi])
            V(lambda: nc.vector.memset(ybf[:], 0.0), [], [ybf])
            V(lambda: nc.vector.tensor_tensor(out=ybf[0:64, 0:512], in0=hi[0:64].rearrange("p h t -> p (h t)"), in1=ez[0:64], op=ALU.mult),
              [hi, ez], [ybf])

            kst = rot("kst", [128, 128], F32, 2); kmst = rot("kmst", [16, 128], F32, 2)
            vst = rot("vst", [128, 128], F32, 2); vmst = rot("vmst", [16, 128], F32, 2)
            KwT = rot("KwT", [128, 128], BF16, 2); KmT = rot("KmT", [128, 16], BF16, 2)
            VwR = rot("Vw", [128, 2, 65], BF16, 2); VmR = rot("Vm", [16, 2, 65], BF16, 2)
            for t_ in VwR.tiles + VmR.tiles:
                V(lambda t_=t_: nc.vector.memset(t_[:], 1.0), [], [t_])
            Sw = [GB6.tiles[0], GB6.tiles[1]]; Sm = [GB6.tiles[2], GB6.tiles[3]]; Sn = [GB6.tiles[4], GB6.tiles[5]]
            for j in range(16):
                ks_ = kst.next(); km_ = kmst.next(); kw_ = KwT.next(); kmT_ = KmT.next()
                LD(ks_, ks_[:], ck_i[j, 16:144, :]); LD(km_, km_[:], ck_i[j, 0:16, :])
                T(lambda ks_=ks_: nc.tensor.transpose(out=pS[:, 0:128], in_=ks_[:], identity=idf[:]), [ks_, idf], [pS])
                T(lambda km_=km_: nc.tensor.transpose(out=pS[:, 128:144], in_=km_[:], identity=idf[0:16, 0:16]), [km_, idf], [pS])
                A(lambda kw_=kw_: nc.scalar.copy(out=kw_[:], in_=pS[:, 0:128]), [pS], [kw_])
                A(lambda kmT_=kmT_: nc.scalar.copy(out=kmT_[:], in_=pS[:, 128:144]), [pS], [kmT_])
                for kap in range(2):
                    ksl = slice(64 * kap, 64 * kap + 64)
                    qv = aqT[ksl, :, 4 * j:4 * j + 4]
                    T(lambda kap=kap, ksl=ksl, qv=qv, kw_=kw_, j=j: nc.tensor.matmul(Sw[kap][:, 16 * j:16 * j + 16], lhsT=kw_[ksl, :], rhs=qv,
                                                                                  start=True, stop=True), [kw_, aqT], [Sw[kap]])
                    T(lambda kap=kap, ksl=ksl, qv=qv, kmT_=kmT_, j=j: nc.tensor.matmul(Sm[kap][0:16, 16 * j:16 * j + 16], lhsT=kmT_[ksl, :], rhs=qv,
                                                                                    start=True, stop=True), [kmT_, aqT], [Sm[kap]])
                    T(lambda kap=kap, ksl=ksl, qv=qv, akT=akT, j=j: nc.tensor.matmul(Sn[kap][0:64, 16 * j:16 * j + 16], lhsT=akT[ksl, 0:64], rhs=qv,
                                                                                  start=True, stop=True), [akT, aqT], [Sn[kap]])
            Ew = sb("Ew", [128, 2, 256], BF16); Em = sb("Em", [16, 2, 256], BF16); En = sb("En", [64, 2, 256], BF16)
            for kap in range(2):
                A(lambda kap=kap: nc.scalar.activation(out=Ew[:, kap, :], in_=Sw[kap][:, 0:256], func=AF.Exp, scale=ASCALE), [Sw[kap]], [Ew])
                A(lambda kap=kap: nc.scalar.activation(out=Em[:, kap, :], in_=Sm[kap][0:16, 0:256], func=AF.Exp, scale=ASCALE), [Sm[kap]], [Em])
                A(lambda kap=kap: nc.scalar.activation(out=En[:, kap, :], in_=Sn[kap][0:64, 0:256], func=AF.Exp, scale=ASCALE), [Sn[kap]], [En])
                V(lambda kap=kap: nc.vector.tensor_tensor(out=Ew[:, kap, :], in0=Ew[:, kap, :], in1=winm[:], op=ALU.mult), [Ew, winm], [Ew])
                V(lambda kap=kap: nc.vector.tensor_tensor(out=En[:, kap, :], in0=En[:, kap, :], in1=newm[0:64, :], op=ALU.mult), [En, newm], [En])
            osb = sb("osb", [16, 16, 64]); rso = sb("rso", [16, 2, 16])
            yas_t = xsp
            yas = xsp[0:64, 512:1024].rearrange("p (k g d) -> p k g d", k=2, g=4)
            scr_t = nc.dram_tensor("scr", [16, 4, 2, 4, 64], F32)
            scr = scr_t.ap()
            for j in range(16):
                vs_ = vst.next(); vm_ = vmst.next(); Vw = VwR.next(); Vm = VmR.next()
                LD(vs_, vs_[:], cv_i[j, 16:144, :]); LD(vm_, vm_[:], cv_i[j, 0:16, :])
                G(lambda vs_=vs_, Vw=Vw: nc.gpsimd.tensor_copy(out=Vw[:, :, 0:64], in_=vs_[:].rearrange("p (k d) -> p k d", k=2)), [vs_], [Vw])
                G(lambda vm_=vm_, Vm=Vm: nc.gpsimd.tensor_copy(out=Vm[:, :, 0:64], in_=vm_[:].rearrange("p (k d) -> p k d", k=2)), [vm_], [Vm])
                for kap in range(2):
                    po = GB6.tiles[3 * kap + j // 7]
                    oap = po[0:16, (j % 7) * 65:(j % 7) * 65 + 65]
                    cs = slice(16 * j, 16 * j + 16)
                    T(lambda kap=kap, j=j, oap=oap, cs=cs, Vm=Vm: nc.tensor.matmul(oap, lhsT=Em[:, kap, cs], rhs=Vm[:, kap, :], start=True, stop=False),
                      [Em, Vm], [po])
                    T(lambda kap=kap, j=j, oap=oap, cs=cs, Vw=Vw: nc.tensor.matmul(oap, lhsT=Ew[:, kap, cs], rhs=Vw[:, kap, :], start=False, stop=False),
                      [Ew, Vw], [po])
                    T(lambda kap=kap, j=j, oap=oap, cs=cs, va=va: nc.tensor.matmul(oap, lhsT=En[:, kap, cs], rhs=va[0:64, kap, :], start=False, stop=True),
                      [En, va], [po])
            for kap in range(2):
                for b3 in range(3):
                    nj = 7 if b3 < 2 else 2
                    po = GB6.tiles[3 * kap + b3]
                    pv_ = po[0:16, 0:65 * nj].rearrange("p (j e) -> p j e", e=65)
                    V(lambda kap=kap, b3=b3, nj=nj, pv_=pv_: nc.vector.tensor_scalar(
                        out=rso[:, kap, 7 * b3:7 * b3 + nj].unsqueeze(2), in0=pv_[:, :, 64:65], scalar1=esk16[:, kap:kap + 1],
                        scalar2=None, op0=ALU.add), [po, esk16], [rso])
                    V(lambda kap=kap, b3=b3, nj=nj: nc.vector.reciprocal(out=rso[:, kap, 7 * b3:7 * b3 + nj], in_=rso[:, kap, 7 * b3:7 * b3 + nj]),
                      [rso], [rso])
                    V(lambda kap=kap, b3=b3, nj=nj, pv_=pv_: nc.vector.tensor_tensor(
                        out=osb[:, 7 * b3:7 * b3 + nj, :], in0=pv_[:, :, 0:64],
                        in1=rso[:, kap, 7 * b3:7 * b3 + nj].unsqueeze(2).to_broadcast([16, nj, 64]), op=ALU.mult), [po, rso], [osb])
                for g in range(4):
                    P.dma("sp", lambda kap=kap, g=g: nc.sync.dma_start(out=scr[:, :, kap, g, :].rearrange("j l d -> l j d"),
                                                                       in_=osb[4 * g:4 * g + 4, :, :]),
                          reads=[osb], writes=[scr_t], key="S_scr")
                P.dma("sp", lambda kap=kap: nc.sync.dma_start(out=yas[:, kap, :, :],
                                                              in_=scr[:, :, kap, :, :].rearrange("j l g d -> (j l) g d")),
                      reads=[scr_t], writes=[yas_t], key="L_yas")
            V(lambda: nc.vector.tensor_tensor(out=ybf[0:64, 512:1024], in0=xsp[0:64, 512:1024], in1=eaz[0:64], op=ALU.mult),
              [xsp, eaz], [ybf])

            transpose8(ybf, yT)
            pm_ = [GB.next(), GB.next()]
            for hf in range(2):
                T(lambda hf=hf: nc.tensor.matmul(pm_[hf][:], lhsT=ones1b[:, 0:128], rhs=bob[:, hf * 512:(hf + 1) * 512], start=True, stop=False),
                  [ones1b, bob], [pm_[hf]])
                for k in range(8):
                    T(lambda k=k, hf=hf: nc.tensor.matmul(pm_[hf][:], lhsT=yT[:, k, :], rhs=wo_bf[:, k, hf * 512:(hf + 1) * 512],
                                                          start=False, stop=(k == 7)), [yT, wo_bf], [pm_[hf]])
            V(lambda hp=hp: nc.vector.tensor_tensor(out=hp[:], in0=hp[:], in1=ln0g[:], op=ALU.mult), [hp, ln0g], [hp])
            for hf in range(2):
                V(lambda hf=hf, hp=hp: nc.vector.tensor_tensor(out=hp[:, hf * 512:(hf + 1) * 512], in0=hp[:, hf * 512:(hf + 1) * 512],
                                                               in1=pm_[hf][:], op=ALU.add),
                  [hp, pm_[hf]], [hp])
            layer_norm(hp, hp, lng, lnb)
            ST(ys_o, hp, hp[0:64, :])
            for (ci_, co_, off) in ((ck_i, sk_o, 0), (cv_i, sv_o, 128)):
                P.dma("sp", lambda ci_=ci_, co_=co_: nc.sync.dma_start(out=co_[:, 0:16, :], in_=ci_[:, 0:16, :]), key="X_c0")
                P.dma("sp", lambda ci_=ci_, co_=co_: nc.sync.dma_start(out=co_[:, 16:140, :], in_=ci_[:, 20:144, :]), key="X_c1")
                for j in range(16):
                    ST(co_[j, 140:144, :], kavf, kavf[4 * j:4 * j + 4, off:off + 128])

        GB = Rot(GB6.tiles[0:5])
        for s in range(NSLOT):
            kind = "meta" if s == 0 else ("prelast" if s == NSLOT - 1 else "pre")
            chunk(kind, xpre[s * 128:(s + 1) * 128, :], slot=s)
        bias_rows(N1, NIN)
        GB = Rot(GB6.tiles)
        for c in range(NOWN):
            chunk("own", xown[c * 128:(c + 1) * 128, :], c=c)
        if DO_SAMPLE:
            sample_program()

        P.emit()
    return nc


def _consts():
    s = np.arange(128)[:, None]
    t = np.arange(128)[None, :]
    tri = (s <= t).astype(np.float32)
    prv = (s > t).astype(np.float32)
    return np.eye(128, dtype=np.float32), np.tile(tri, (1, 4)), np.tile(prv, (1, 4))


def prep_prompt(inputs, NOWN):
    NPRE = 3 * NOWN
    NSLOT = 1 + NPRE
    xp = np.asarray(inputs["x_prompt"], np.float32)
    meta = np.asarray(inputs["meta_tokens"], np.float32)
    cid, ctri, cprev = _consts()
    vecs = np.stack([np.asarray(inputs[k], np.float32).reshape(-1) for k in ("ln0_g", "ln0_b")] +
                    [np.asarray(inputs[k], np.float32).reshape(-1) for k in ("ln_g", "ln_b")])
    common = dict(cid=cid, ctri=ctri, cprev=cprev,
                  w_in=np.ascontiguousarray(np.concatenate([np.asarray(inputs["w_in"], np.float32)[0][:, a:b] for a, b in COL_PERM], 1)),
                  b_in=np.ascontiguousarray(np.concatenate([np.asarray(inputs["b_in"], np.float32).reshape(NIN)[a:b] for a, b in COL_PERM]).reshape(1, NIN)),
                  w_out=np.ascontiguousarray(np.asarray(inputs["w_out"], np.float32)[0]),
                  vecs=vecs, mng=np.asarray(inputs["m_norm_g"], np.float32).reshape(1, 512),
                  sinks=np.asarray(inputs["a_sinks"], np.float32).reshape(1, 8))
    maps = []
    for core in range(8):
        b, r = core // 4, core % 4
        xown = np.ascontiguousarray(xp[b, r * NOWN * 128:(r + 1) * NOWN * 128])
        xpre = np.zeros((NSLOT * 128, D), np.float32)
        valid = np.zeros((NSLOT * 128,), np.float32)
        xpre[0:NMETA] = meta
        valid[0:NMETA] = 1.0
        npre = r * NOWN
        if npre:
            xpre[(NSLOT - npre) * 128:] = xp[b, 0:npre * 128]
            valid[(NSLOT - npre) * 128:] = 1.0
        rvalid = np.tile(valid[None, :], (4, 1))
        rneg = np.where(rvalid > 0, 0.0, NEG).astype(np.float32)
        pm1 = cprev if r > 0 else np.zeros_like(cprev)
        m = dict(common)
        m.update(xown=xown, xpre=xpre, rvalid=rvalid, rneg=rneg, pm1=pm1)
        maps.append(m)
    return maps


def prep_sample(inputs, maps):
    xs = np.asarray(inputs["x_sample"], np.float32)
    ck = np.asarray(inputs["cache_swa_k"], np.float32)[0].reshape(128, 144, 128)
    cv = np.asarray(inputs["cache_swa_v"], np.float32)[0].reshape(128, 144, 128)
    sc = np.asarray(inputs["state_mlstm_c"], np.float32)[0]
    sn = np.asarray(inputs["state_mlstm_n"], np.float32)[0]
    sm = np.asarray(inputs["state_mlstm_m"], np.float32)[0]
    sinks = np.asarray(inputs["a_sinks"], np.float32).reshape(8)
    s_ = np.arange(128)
    j_ = np.arange(16)
    cseqcol = ((s_[:, None] // 4 == j_[None, :]) & (s_[:, None] < 64)).astype(np.float32)
    t_ = np.arange(64)
    cseqbc = np.broadcast_to((t_[None, :] // 4 == j_[:, None]).astype(np.float32).reshape(1, 1024), (128, 1024)).copy()
    t128 = np.arange(128)
    blk = ((s_[:, None] // 4 == t128[None, :] // 4) & (s_[:, None] <= t128[None, :]) & (s_[:, None] < 64) & (t128[None, :] < 64))
    cblk4 = np.tile(blk.astype(np.float32), (1, 4))
    l_ = np.arange(4)
    win = (s_[:, None] > l_[None, :]).astype(np.float32)
    cwin = np.broadcast_to(win[:, None, None, :], (128, 16, 4, 4)).reshape(128, 256).copy()
    new = ((s_[:, None, None] // 4 == j_[None, :, None]) & (s_[:, None, None] % 4 <= l_[None, None, :]) & (s_[:, None, None] < 64))
    cnew = np.broadcast_to(new[:, :, None, :], (128, 16, 4, 4)).astype(np.float32).reshape(128, 256).copy()
    sinks16 = np.zeros((16, 2), np.float32)
    for kap in range(2):
        for g in range(4):
            sinks16[4 * g:4 * g + 4, kap] = sinks[4 * kap + g]
    for core in range(8):
        sl = slice(16 * core, 16 * core + 16)
        maps[core].update(
            xs=np.ascontiguousarray(xs[sl].reshape(64, D)), ck=np.ascontiguousarray(ck[sl]), cv=np.ascontiguousarray(cv[sl]),
            sc=np.ascontiguousarray(sc[sl]), sn=np.ascontiguousarray(sn[sl].reshape(64, 128)),
            sm=np.ascontiguousarray(sm[sl].T), cseqcol=cseqcol, cblk4=cblk4, cwin=cwin, cnew=cnew, sinks16=sinks16)
    return maps


_NC_CACHE = {}


def kernel(**inputs):
    NOWN = np.asarray(inputs["x_prompt"]).shape[1] // 512
    B = 2
    maps = prep_sample(inputs, prep_prompt(inputs, NOWN))
    nc = build(NOWN, DO_SAMPLE=True)
    res = run_bass_kernel_spmd(nc, maps, core_ids=list(range(8)))
    R = res.results
    S = NOWN * 512
    y = np.zeros((B, S, D), np.float32)
    for core in range(8):
        b, r = core // 4, core % 4
        y[b, r * NOWN * 128:(r + 1) * NOWN * 128] = R[core]["y"]
    last = [3, 7]
    pk = np.stack([R[c]["pk"].reshape(144, 2, 64) for c in last])[None]
    pv = np.stack([R[c]["pv"].reshape(144, 2, 64) for c in last])[None]
    pc = np.stack([R[c]["pc"] for c in last])[None]
    pn = np.stack([R[c]["pn"] for c in last])[None]
    pm = np.stack([R[c]["pm"].reshape(4) for c in last])[None]
    ys = np.concatenate([R[c]["ys"].reshape(16, 4, D) for c in range(8)], 0)
    sk = np.concatenate([R[c]["sk"].reshape(16, 144, 2, 64) for c in range(8)], 0)[None]
    sv = np.concatenate([R[c]["sv"].reshape(16, 144, 2, 64) for c in range(8)], 0)[None]
    sco = np.concatenate([R[c]["sco"] for c in range(8)], 0)[None]
    sno = np.concatenate([R[c]["sno"].reshape(16, 4, 128) for c in range(8)], 0)[None]
    smo = np.concatenate([R[c]["smo"].T for c in range(8)], 0)[None]
    f = lambda a: np.ascontiguousarray(a, dtype=np.float32)
    return tuple(f(a) for a in (y, ys, pk, pv, pc, pn, pm, sk, sv, sco, sno, smo))
```

```python
import contextlib
import numpy as np
import concourse.bass as bass
import concourse.mybir as mybir
from concourse.bass_utils import run_bass_kernel_spmd

F32 = mybir.dt.float32
BF16 = mybir.dt.bfloat16
ALU = mybir.AluOpType
AF = mybir.ActivationFunctionType

COMPUTE = ("pe", "act", "dve", "pool")
STRICT_SAME_ENGINE = False

D = 1024
NIN = 3848
NMETA = 16
O_MK, O_MV, O_AK, O_AV, O_MI, O_MF, O_MQ, O_MO, O_MZ, O_AQ, O_AZ = 0, 512, 1024, 1152, 1280, 1284, 1288, 1800, 2312, 2824, 3336
N1 = 1288
N2 = NIN - N1
COL_PERM = [(512, 1024), (1024, 1536), (3080, 3208), (3208, 3336), (2560, 2568), (0, 512), (1536, 2048), (2048, 2560), (2568, 3080), (3336, 3848)]
LN_EPS = 1e-5
DN_ALPHA = 2.0 ** 0.25
KSCALE = 128.0 ** -0.5
ASCALE = 64.0 ** -0.5
NEG = -1.0e30


class Op:
    __slots__ = ("eng", "fn", "reads", "writes", "dma", "key", "idx", "deps", "marked", "kcount", "alld", "fin", "lat")

    def __init__(self, eng, fn, reads, writes, dma, key):
        self.eng, self.fn, self.reads, self.writes, self.dma, self.key = eng, fn, reads, writes, dma, key
        self.deps = []
        self.marked = False
        self.kcount = 0


class Prog:
    def __init__(self, nc):
        self.nc = nc
        self.ops = []
        self.nkeys = {}
        self.filler = None
        self.fill_frac = 0.7
        self.fill_on = lambda o: True
        self.excl = set()

    def op(self, eng, fn, reads=(), writes=()):
        writes = tuple(writes) + tuple(r for r in reads if id(r) in self.excl and not any(r is w for w in writes))
        o = Op(eng, fn, tuple(reads), tuple(writes), False, None)
        self.ops.append(o)
        return o

    def dma(self, eng, fn, reads=(), writes=(), key=None, lat=3.0):
        o = Op(eng, fn, tuple(reads), tuple(writes), True, key)
        o.lat = lat
        self.nkeys[key] = self.nkeys.get(key, 0) + 1
        o.kcount = self.nkeys[key]
        self.ops.append(o)
        return o

    COST = {"pe": 0.25, "act": 0.45, "dve": 0.5, "pool": 0.9, "sp": 0.05}

    def _schedule(self, window=96):
        self._analyze(mark=False)
        per = {}
        for o in self.ops:
            per.setdefault(o.eng, []).append(o)
            o.fin = None
        free = {e: 0.0 for e in per}
        order = []
        nleft = len(self.ops)
        while nleft:
            best = None
            for e, lst in per.items():
                cnt = 0
                for o in lst:
                    if o.fin is not None:
                        continue
                    cnt += 1
                    if cnt > window:
                        break
                    rdy = 0.0
                    ok = True
                    for p in o.alld:
                        if p.fin is None:
                            ok = False
                            break
                        f = p.fin + (0.0 if (p.eng == e and not p.dma) else 0.25)
                        if f > rdy:
                            rdy = f
                    if not ok:
                        continue
                    st = max(free[e], rdy)
                    if best is None or (st, o.idx) < (best[0], best[1].idx):
                        best = (st, o)
                    if st <= free[e]:
                        break
            st, o = best
            if self.filler is not None and any(t is self.filler[1] for t in o.reads + o.writes):
                self.filler = None
            if self.filler is not None and o.eng == "pe" and self.fill_on(o):
                gap = st - free["pe"]
                if gap > 0.6:
                    nf = min(int(gap * self.fill_frac / 0.25), 24)
                    for _ in range(nf):
                        f = Op("pe", self.filler[0], (), (self.filler[1],), False, None)
                        f.idx = -1
                        f.alld = []
                        f.fin = free["pe"] + 0.25
                        free["pe"] = f.fin
                        order.append(f)
                    st = max(st, free["pe"])
            c = self.COST[o.eng]
            free[o.eng] = st + c
            o.fin = st + c + (o.lat if o.dma else 0.0)
            order.append(o)
            nleft -= 1
            lst = per[o.eng]
            while lst and lst[0].fin is not None:
                lst.pop(0)
        self.ops = order
        self.nkeys = {}
        for o in self.ops:
            o.marked = False
            if o.dma:
                self.nkeys[o.key] = self.nkeys.get(o.key, 0) + 1
                o.kcount = self.nkeys[o.key]
        self.sim_time = max(free.values())

    def _analyze(self, mark=True):
        wr, rd = {}, {}
        SERIAL = False

        def add(lst, o):
            if not o.dma:
                for i, x in enumerate(lst):
                    if (not x.dma) and x.eng == o.eng:
                        lst[i] = o
                        return
            lst.append(o)

        for idx, o in enumerate(self.ops):
            o.idx = idx
            deps = {}
            for r in o.reads:
                for p in wr.get(id(r), ()):
                    deps[p.idx] = (p, "raw")
            for w in o.writes:
                for p in rd.get(id(w), ()):
                    deps.setdefault(p.idx, (p, "war"))
                for p in wr.get(id(w), ()):
                    deps.setdefault(p.idx, (p, "waw"))
            wids = set(id(w) for w in o.writes)
            for w in o.writes:
                if rd.get(id(w)):
                    wr[id(w)] = [o]
                    rd[id(w)] = []
                else:
                    add(wr.setdefault(id(w), []), o)
            for r in o.reads:
                if id(r) not in wids:
                    add(rd.setdefault(id(r), []), o)
            if SERIAL and idx > 0:
                pp = self.ops[idx - 1]
                deps.setdefault(pp.idx, (pp, "raw"))
            o.deps = []
            o.alld = []
            for p, kind in deps.values():
                if p is o:
                    continue
                o.alld.append(p)
                if (not p.dma) and (not o.dma) and p.eng == o.eng:
                    if p.eng == "pe" or (kind != "raw" and not STRICT_SAME_ENGINE):
                        continue
                o.deps.append(p)
                if mark:
                    p.marked = True

    def emit(self, final_wait_eng="sp", schedule=True):
        nc = self.nc
        if schedule:
            self._schedule()
        self._analyze()
        with contextlib.ExitStack() as es:
            esem = {e: es.enter_context(nc.semaphore("s_" + e)) for e in COMPUTE}
            ksem = {k: es.enter_context(nc.semaphore("k_%s" % (str(k),))) for k in self.nkeys}
            cnt = {e: 0 for e in COMPUTE}
            val = {}
            for o in self.ops:
                if o.dma:
                    val[o.idx] = (ksem[o.key], 16 * o.kcount)
                elif o.marked:
                    cnt[o.eng] += 1
                    val[o.idx] = (esem[o.eng], cnt[o.eng])
            block = es.enter_context(nc.Block())

            def run_engine(ename):
                def body(eng):
                    waited = {}
                    for o in self.ops:
                        if o.eng != ename:
                            continue
                        for p in o.deps:
                            sem, v = val[p.idx]
                            if waited.get(id(sem), 0) >= v:
                                continue
                            waited[id(sem)] = v
                            eng.wait_ge(sem, v)
                        ins = o.fn()
                        if o.dma:
                            ins.then_inc(ksem[o.key], 16)
                        elif o.marked:
                            ins.then_inc(esem[o.eng], 1)
                    if ename == final_wait_eng:
                        for k, n in self.nkeys.items():
                            eng.wait_ge(ksem[k], 16 * n)
                return body

            block.sync(run_engine("sp"))
            block.tensor(run_engine("pe"))
            block.scalar(run_engine("act"))
            block.vector(run_engine("dve"))
            block.gpsimd(run_engine("pool"))


class Rot:
    def __init__(self, tiles):
        self.tiles = tiles
        self.i = -1

    def next(self):
        self.i = (self.i + 1) % len(self.tiles)
        return self.tiles[self.i]

    def cur(self):
        return self.tiles[self.i]

    def prev(self):
        return self.tiles[(self.i - 1) % len(self.tiles)]


def build(NOWN, DO_SAMPLE=True):
    NPRE = 3 * NOWN
    NSLOT = 1 + NPRE
    nc = bass.Bass("TRN2", target_bir_lowering=False)

    def din(name, shape):
        return nc.dram_tensor(name, list(shape), F32, kind="ExternalInput").ap()

    def dout(name, shape):
        return nc.dram_tensor(name, list(shape), F32, kind="ExternalOutput").ap()

    xown = din("xown", [NOWN * 128, D])
    xpre = din("xpre", [NSLOT * 128, D])
    rvalid = din("rvalid", [4, NSLOT * 128])
    rneg = din("rneg", [4, NSLOT * 128])
    pm1 = din("pm1", [128, 512])
    cid = din("cid", [128, 128])
    ctri = din("ctri", [128, 512])
    cprev = din("cprev", [128, 512])
    w_in = din("w_in", [D, NIN])
    b_in = din("b_in", [1, NIN])
    w_out = din("w_out", [D, D])
    vecs = din("vecs", [4, D])
    mng = din("mng", [1, 512])
    sinks = din("sinks", [1, 8])

    y_o = dout("y", [NOWN * 128, D])
    pk_o = dout("pk", [144, 128])
    pv_o = dout("pv", [144, 128])
    pc_o = dout("pc", [4, 128, 128])
    pn_o = dout("pn", [4, 128])
    pm_o = dout("pm", [4, 1])

    if DO_SAMPLE:
        xs = din("xs", [64, D])
        ck_i = din("ck", [16, 144, 128])
        cv_i = din("cv", [16, 144, 128])
        sc_i = din("sc", [16, 4, 128, 128])
        sn_i = din("sn", [64, 128])
        sm_i = din("sm", [4, 16])
        cseqcol = din("cseqcol", [128, 16])
        cblk4 = din("cblk4", [128, 512])
        cwin = din("cwin", [128, 256])
        cnew = din("cnew", [128, 256])
        sinks16 = din("sinks16", [16, 2])
        ys_o = dout("ys", [64, D])
        sk_o = dout("sk", [16, 144, 128])
        sv_o = dout("sv", [16, 144, 128])
        sc_o = dout("sco", [16, 4, 128, 128])
        sn_o = dout("sno", [64, 128])
        sm_o = dout("smo", [4, 16])

    P = Prog(nc)
    es = contextlib.ExitStack()
    KDBG = False
    dbg_out = {}

    def DBGDUMP(name, t, ap, shape):
        if not KDBG:
            return
        d = nc.dram_tensor("dbg_" + name, list(shape), t.dtype if hasattr(t, "dtype") else F32, kind="ExternalOutput").ap()
        dbg_out[name] = d
        P.dma("sp", lambda: nc.sync.dma_start(out=d, in_=ap), reads=[t], key="D_" + name)

    def sb(name, shape, dt=F32):
        return es.enter_context(nc.sbuf_tensor(name, list(shape), dt))

    def psum(name, shape, dt=F32):
        t = es.enter_context(nc.psum_tensor(name, list(shape), dt))
        P.excl.add(id(t))
        return t

    def rot(name, shape, dt=F32, n=2):
        return Rot([sb("%s%d" % (name, i), shape, dt) for i in range(n)])

    V = lambda fn, r=(), w=(): P.op("dve", fn, r, w)
    A = lambda fn, r=(), w=(): P.op("act", fn, r, w)
    G = lambda fn, r=(), w=(): P.op("pool", fn, r, w)
    T = lambda fn, r=(), w=(): P.op("pe", fn, r, w)
    kctr = [0]

    def LD(out_t, out_ap, in_ap, lat=3.0, **kw):
        P.dma("sp", lambda: nc.sync.dma_start(out=out_ap, in_=in_ap, **kw), writes=[out_t], key="L_" + out_t.name, lat=lat)

    def ST(out_ap, in_t, in_ap, **kw):
        P.dma("sp", lambda: nc.sync.dma_start(out=out_ap, in_=in_ap, **kw), reads=[in_t], key="S_" + in_t.name)

    with es:
        GB = Rot([psum("g%d" % i, [128, 512]) for i in range(6)])
        GB6 = GB
        pDum = GB.tiles[5]
        WARM_PRE, WARM_OWN = 18, 0


        def warm(n):
            for _ in range(n):
                T(lambda: nc.tensor.matmul(pDum[:], lhsT=idb[:], rhs=tri4[:], start=True, stop=True), [], [pDum])
        pT = psum("pT", [128, 1024], BF16)
        pS = psum("pS", [128, 512])

        wst = rot("wst", [128, D], F32, 2)
        xR = rot("xt", [128, D], F32, 2)
        hpR = rot("hp", [128, D], F32, 2)
        cstage = wst.tiles[0]
        idf = sb("idf", [128, 128]); idb = sb("idb", [128, 128], BF16)
        ones4 = sb("ones4", [4, 128]); ones1b = sb("ones1b", [128, 128], BF16)
        onescol = sb("onescol", [128, 1], BF16); negid4 = sb("negid4", [4, 4]); zer4 = sb("zer4", [4, 128])
        tri4 = sb("tri4", [128, 512], BF16); prev4 = sb("prev4", [128, 512], BF16); pm1b = sb("pm1b", [128, 512], BF16)
        srow = sb("srow", [1, 8])
        binb1 = sb("binb1", [128, N1], BF16); binb2 = sb("binb2", [128, N2], BF16)
        gcol = sb("gcol", [128, 8]); b0col = sb("b0col", [128, 8]); b0g = sb("b0g", [128, 8], BF16); growf = sb("growf", [1, 8])
        ln0g = sb("ln0g", [128, D]); lng = sb("lng", [128, D]); lnb = sb("lnb", [128, D]); nmrR = rot("nmr", [128, 1], F32, 2); nmr = nmrR.next()
        esink = sb("esink", [128, 8]); biasg = sb("biasg", [128, 8]); mngcol = sb("mngcol", [128, 4])
        bob = sb("bob", [128, D], BF16)
        w1 = sb("w1", [128, 8, N1], BF16); w2 = sb("w2", [128, 8, N2], BF16)

        def wsl(k, c0, n):
            if c0 + n <= N1:
                return w1, w1[:, k, c0:c0 + n]
            assert c0 >= N1
            return w2, w2[:, k, c0 - N1:c0 - N1 + n]

        def bsl(c0, n):
            if c0 + n <= N1:
                return binb1, binb1[:, c0:c0 + n]
            assert c0 >= N1
            return binb2, binb2[:, c0 - N1:c0 - N1 + n]
        wo_bf = sb("wo_bf", [128, 8, D], BF16)
        NWQ = 4
        WH = NIN // NWQ
        rvR = rot("rvt", [4, 128], F32, 2); rnR = rot("rnt", [4, 128], F32, 2)

        LD(idf, idf[:], cid)
        A(lambda: nc.scalar.copy(out=idb[:], in_=idf[:]), [idf], [idb])
        G(lambda: nc.gpsimd.memset(ones4[:], 1.0), [], [ones4])
        G(lambda: nc.gpsimd.memset(ones1b[:], 0.0), [], [ones1b])
        G(lambda: nc.gpsimd.memset(ones1b[0:1, :], 1.0), [], [ones1b])
        G(lambda: nc.gpsimd.memset(binb1[:], 0.0), [], [binb1])
        G(lambda: nc.gpsimd.memset(binb2[:], 0.0), [], [binb2])
        LD(gcol, gcol[:], vecs[0:1, :].rearrange("o (k p) -> p (o k)", p=128), allow_slow_non_contiguous=True)
        LD(b0col, b0col[:], vecs[1:2, :].rearrange("o (k p) -> p (o k)", p=128), allow_slow_non_contiguous=True)
        rgc = sb("rgc", [128, 8])
        V(lambda: nc.vector.reciprocal(out=rgc[:], in_=gcol[:]), [gcol], [rgc])
        V(lambda: nc.vector.tensor_tensor(out=b0g[:], in0=b0col[:], in1=rgc[:], op=ALU.mult), [b0col, rgc], [b0g])
        G(lambda: nc.gpsimd.memset(onescol[:], 1.0), [], [onescol])
        G(lambda: nc.gpsimd.memset(zer4[:], 0.0), [], [zer4])
        V(lambda: nc.vector.tensor_scalar(out=negid4[:], in0=idf[0:4, 0:4], scalar1=-1.0, scalar2=None, op0=ALU.mult), [idf], [negid4])
        cast_eng = [("pool", lambda o, i: nc.gpsimd.tensor_copy(out=o, in_=i)),
                    ("dve", lambda o, i: nc.vector.tensor_copy(out=o, in_=i)),
                    ("act", lambda o, i: nc.scalar.copy(out=o, in_=i))]
        scl_eng = [("pool", lambda o, i, sc: nc.gpsimd.tensor_scalar(out=o, in0=i, scalar1=sc, scalar2=None, op0=ALU.mult)),
                   ("dve", lambda o, i, sc: nc.vector.tensor_scalar(out=o, in0=i, scalar1=sc, scalar2=None, op0=ALU.mult)),
                   ("act", lambda o, i, sc: nc.scalar.mul(out=o, in_=i, mul=sc))]
        ci = 0

        def WLD(st, out_ap, in_ap, dq="sp"):
            if dq == "sp":
                P.dma("sp", lambda: nc.sync.dma_start(out=out_ap, in_=in_ap), writes=[st], key="L_" + st.name, lat=14.0)
            else:
                P.dma("pool", lambda: nc.gpsimd.dma_start(out=out_ap, in_=in_ap), writes=[st], key="L_" + st.name, lat=20.0)

        def load_w(wt, c_lo, c_hi, piece, engs, stg=None, dq="sp"):
            nonlocal ci
            for k in range(8):
                c = c_lo
                while c < c_hi:
                    n_ = min(piece, c_hi - c)
                    st = (stg or wst).next()
                    WLD(st, st[:, 0:n_], w_in[k * 128:(k + 1) * 128, c:c + n_], dq)
                    en, f = engs[ci % len(engs)]; ci += 1
                    tl, ap = wsl(k, c, n_)
                    P.op(en, lambda f=f, st=st, ap=ap, n_=n_, k=k: f(ap, st[:, 0:n_], gcol[:, k:k + 1]), [st, gcol], [tl])
                    c += n_

        def bias_rows(c_lo, c_hi):
            c = c_lo
            while c < c_hi:
                n_ = min(512, c_hi - c)
                st = wst.next()
                WLD(st, st[0:1, 0:n_], b_in[:, c:c + n_])
                pb = GB.next()
                for k in range(8):
                    tl, ap = wsl(k, c, n_)
                    T(lambda k=k, pb=pb, ap=ap, n_=n_: nc.tensor.matmul(pb[0:1, 0:n_], lhsT=b0g[:, k:k + 1], rhs=ap, start=(k == 0), stop=(k == 7)),
                      [b0g, tl], [pb])
                bt, bap = bsl(c, n_)
                V(lambda pb=pb, st=st, bap=bap, n_=n_: nc.vector.tensor_tensor(out=bap[0:1, :], in0=pb[0:1, 0:n_], in1=st[0:1, 0:n_], op=ALU.add),
                  [pb, st], [bt])
                if c <= O_MI and O_MI + 8 <= c + n_:
                    o_ = O_MI - c
                    V(lambda pb=pb, st=st, o_=o_: nc.vector.tensor_tensor(out=growf[:], in0=pb[0:1, o_:o_ + 8], in1=st[0:1, o_:o_ + 8], op=ALU.add),
                      [pb, st], [growf])
                c += n_

        load_w(w1, 0, N1, 644, scl_eng[1:3], stg=Rot(wst.tiles + xR.tiles + hpR.tiles))
        for (src, dst) in ((ctri, tri4), (cprev, prev4), (pm1, pm1b)):
            LD(cstage, cstage[:, 0:512], src)
            V(lambda dst=dst: nc.vector.tensor_copy(out=dst[:], in_=cstage[:, 0:512]), [cstage], [dst])
        LD(srow, srow[:], sinks)
        ones1f = sb("ones1f", [1, 128])
        G(lambda: nc.gpsimd.memset(ones1f[:], 1.0), [], [ones1f])

        def bcast_row(dst, dst_ap, row_t, row_ap, n, func=None):
            pb = GB.next()
            T(lambda pb=pb: nc.tensor.matmul(pb[:, 0:n], lhsT=ones1f[:], rhs=row_ap, start=True, stop=True), [ones1f, row_t], [pb])
            if func is None:
                V(lambda pb=pb: nc.vector.tensor_copy(out=dst_ap, in_=pb[:, 0:n]), [pb], [dst])
            else:
                A(lambda pb=pb: nc.scalar.activation(out=dst_ap, in_=pb[:, 0:n], func=func), [pb], [dst])

        G(lambda: nc.gpsimd.memset(bob[:], 0.0), [], [bob])
        for i, dst in enumerate((ln0g, None, lng, lnb)):
            for hf in range(2):
                rs_ = cstage
                LD(rs_, rs_[0:1, 0:512], vecs[i:i + 1, hf * 512:(hf + 1) * 512])
                if dst is None:
                    A(lambda hf=hf: nc.scalar.mul(out=bob[0:1, hf * 512:(hf + 1) * 512], in_=cstage[0:1, 0:512], mul=DN_ALPHA), [cstage], [bob])
                else:
                    bcast_row(dst, dst[:, hf * 512:(hf + 1) * 512], rs_, rs_[0:1, 0:512], 512)
        A(lambda: nc.scalar.mul(out=ln0g[:], in_=ln0g[:], mul=DN_ALPHA), [ln0g], [ln0g])
        rs_ = cstage
        LD(mngcol, mngcol[:], mng.rearrange("o (k p) -> p (o k)", p=128), allow_slow_non_contiguous=True)
        bcast_row(esink, esink[:], srow, srow[:], 8, func=AF.Exp)

        bias_rows(0, N1)
        load_w(w2, N1, NIN, 640, scl_eng[1:3], dq="pool")
        for k in range(8):
            st = wst.next()
            WLD(st, st[:, 0:D], w_out[k * 128:(k + 1) * 128, :], "pool")
            if k < 4:
                en, f = scl_eng[1 + ci % 2]; ci += 1
                P.op(en, lambda f=f, st=st, k=k: f(wo_bf[:, k, :], st[:, 0:D], mngcol[:, k:k + 1]), [st, mngcol], [wo_bf])
            else:
                en, f = cast_eng[1 + ci % 2]; ci += 1
                P.op(en, lambda f=f, st=st, k=k: f(wo_bf[:, k, :], st[:, 0:D]), [st], [wo_bf])
        bcast_row(biasg, biasg[:], growf, growf[:], 8)

        Cst = sb("Cst", [128, 4, 128]); nst = sb("nst", [128, 4])
        Cbf = sb("Cbf", [128, 4, 128], BF16); nbf = sb("nbf", [128, 4], BF16)
        V(lambda: nc.vector.memset(Cst[:], 0.0), [], [Cst])
        V(lambda: nc.vector.memset(nst[:], 0.0), [], [nst])
        V(lambda: nc.vector.memset(Cbf[:], 0.0), [], [Cbf])
        V(lambda: nc.vector.memset(nbf[:], 0.0), [], [nbf])
        bnegR = rot("bneg", [4, 128], F32, 2)
        UR = rot("Urow", [4, 128], F32, 2)
        UendR = rot("Uend", [128, 4], F32, 2)
        for t in bnegR.tiles + UR.tiles + UendR.tiles:
            V(lambda t=t: nc.vector.memset(t[:], 0.0), [], [t])
        bnegR.next(); UR.next(); UendR.next()

        st6R = rot("st6", [128, 2, 6], F32, 2); st6 = st6R.next(); mvR = rot("mv", [128, 2], F32, 2); mv = mvR.next(); rstdR = rot("rstd", [128, 1], F32, 2); rstd = rstdR.next()
        hbR = rot("hb", [128, D], BF16, 2); hb = hbR.next()
        xTR = rot("xT", [128, 8, 128], BF16, 2)
        ktmR = rot("ktm", [128, 4, 128], BF16, 2); ktm = ktmR.next(); vbfR = rot("vbf", [128, 4, 128], BF16, 2); vbf = vbfR.next()
        kwR = rot("kw", [128, 4, 128], BF16, 2); kw = kwR.next()
        gtR = rot("gt", [128, 8], F32, 2); gt = gtR.next(); speR = rot("spe", [4, 128], F32, 2); spe = speR.next(); sprR = rot("spr", [4, 128], F32, 2); spr = sprR.next(); spmR = rot("spm", [4, 128], F32, 2); spm = spmR.next()
        limR = rot("lim", [4, 128], F32, 2); lim = limR.next(); u_rR = rot("u_r", [4, 128], F32, 2); u_r = u_rR.next(); m_rR = rot("m_r", [4, 128], F32, 2); m_r = m_rR.next()
        colsR = rot("cols", [128, 12], F32, 2); cols = colsR.next(); diagUR = rot("diagU", [4, 4], F32, 2); diagU = diagUR.next(); exinR = rot("exin", [128, 16], F32, 2); exin = exinR.next(); exR = rot("ex", [128, 16], F32, 2); ex = exR.next()
        akTR = rot("akT", [128, 128], BF16, 3); vaR = rot("va", [128, 2, 65], BF16, 3)
        akTm = sb("akTm", [128, 16], BF16); vam = sb("vam", [16, 2, 65], BF16)
        kavf = sb("kavf", [128, 256])
        for t in vaR.tiles + [vam]:
            V(lambda t=t: nc.vector.memset(t[:], 1.0), [], [t])
        eo = sb("eo", [128, 512]); ez = sb("ez", [128, 512]); eaz = sb("eaz", [128, 512])
        qT = sb("qT", [128, 4, 128], BF16); kT = sb("kT", [128, 4, 128], BF16); aqT = sb("aqT", [128, 4, 128], BF16)
        bd = sb("bd", [4, 4, 128]); Wt = sb("Wt", [128, 4, 128], BF16); Wm = sb("Wm", [128, 4, 128], BF16)
        PTt = sb("PTt", [128, 4, 128], BF16); hi = sb("hi", [128, 4, 128]); hn = hi
        d1 = sb("d1", [128, 4]); d2 = sb("d2", [128, 4]); rden = sb("rden", [128, 4])
        so = eo; hg = hi; st4 = sb("st4", [128, 4, 6]); mv4 = sb("mv4", [128, 4, 2])
        rs4 = sb("rs4", [128, 4]); sz = ez; ybf = sb("ybf", [128, D], BF16)
        Eown = sb("Eown", [128, 512], BF16); Eprev = sb("Eprev", [128, 512], BF16); Emeta = sb("Emeta", [16, 512], BF16)
        Eown2 = Eown; Eprev2 = Eprev
        dsum = sb("dsum", [128, 4]); rsa = sb("rsa", [128, 4]); ya = sb("ya", [128, 4, 64]); saz = eaz
        yT = sb("yT", [128, 8, 128], BF16)

        def layer_norm(src, dst_f, gbc, bbc, dst_b=None):
            for hf in range(2):
                V(lambda hf=hf, st6=st6: nc.vector.bn_stats(out=st6[:, hf, :], in_=src[:, hf * 512:(hf + 1) * 512]), [src], [st6])
            V(lambda st6=st6, mv=mv: nc.vector.bn_aggr(out=mv[:], in_=st6[:]), [st6], [mv])
            A(lambda mv=mv, rstd=rstd: nc.scalar.activation(out=rstd[:], in_=mv[:, 1:2], func=AF.Ln, bias=LN_EPS), [mv], [rstd])
            A(lambda rstd=rstd: nc.scalar.activation(out=rstd[:], in_=rstd[:], func=AF.Exp, scale=-0.5), [rstd], [rstd])
            V(lambda mv=mv, rstd=rstd, nmr=nmr: nc.vector.tensor_scalar(out=nmr[:], in0=mv[:, 0:1], scalar1=rstd[:, 0:1], scalar2=-1.0, op0=ALU.mult, op1=ALU.mult),
              [mv, rstd], [nmr])
            A(lambda rstd=rstd, nmr=nmr: nc.scalar.activation(out=dst_f[:], in_=src[:], func=AF.Identity, scale=rstd[:, 0:1], bias=nmr[:, 0:1]),
              [src, rstd, nmr], [dst_f])
            V(lambda: nc.vector.tensor_tensor(out=dst_f[:], in0=dst_f[:], in1=gbc[:], op=ALU.mult), [dst_f, gbc], [dst_f])
            V(lambda: nc.vector.tensor_tensor(out=dst_f[:], in0=dst_f[:], in1=bbc[:], op=ALU.add), [dst_f, bbc], [dst_f])
            if dst_b is not None:
                A(lambda: nc.scalar.copy(out=dst_b[:], in_=dst_f[:]), [dst_f], [dst_b])

        def ln0(src, dst_b, hp_f=None):
            for hf in range(2):
                V(lambda hf=hf, st6=st6: nc.vector.bn_stats(out=st6[:, hf, :], in_=src[:, hf * 512:(hf + 1) * 512]), [src], [st6])
            V(lambda st6=st6, mv=mv: nc.vector.bn_aggr(out=mv[:], in_=st6[:]), [st6], [mv])
            A(lambda mv=mv, rstd=rstd: nc.scalar.activation(out=rstd[:], in_=mv[:, 1:2], func=AF.Ln, bias=LN_EPS), [mv], [rstd])
            A(lambda rstd=rstd: nc.scalar.activation(out=rstd[:], in_=rstd[:], func=AF.Exp, scale=-0.5), [rstd], [rstd])
            V(lambda mv=mv, rstd=rstd: nc.vector.tensor_scalar(out=dst_b[:], in0=src[:], scalar1=mv[:, 0:1], scalar2=rstd[:, 0:1],
                                              op0=ALU.subtract, op1=ALU.mult), [src, mv, rstd], [dst_b])
            if hp_f is not None:
                V(lambda mv=mv, rstd=rstd, nmr=nmr: nc.vector.tensor_scalar(out=nmr[:], in0=mv[:, 0:1], scalar1=rstd[:, 0:1], scalar2=-1.0, op0=ALU.mult, op1=ALU.mult),
                  [mv, rstd], [nmr])
                A(lambda rstd=rstd, nmr=nmr: nc.scalar.activation(out=hp_f[:], in_=src[:], func=AF.Identity, scale=rstd[:, 0:1], bias=nmr[:, 0:1]),
                  [src, rstd, nmr], [hp_f])

        def transpose8(src_b, dstT, np_=128):
            for k in range(8):
                T(lambda k=k: nc.tensor.transpose(out=pT[:, k * 128:k * 128 + np_], in_=src_b[0:np_, k * 128:(k + 1) * 128],
                                                  identity=idb[0:np_, 0:np_]), [src_b, idb], [pT])
            A(lambda: nc.scalar.copy(out=dstT[:, :, 0:np_], in_=pT[:].rearrange("p (k t) -> p k t", k=8)[:, :, 0:np_]), [pT], [dstT])

        NOBIAS = False

        def proj_tm(xT, c0, n, pb, nt=128, with_bias=True):
            if NOBIAS:
                with_bias = False
            if with_bias:
                bt, bap = bsl(c0, n)
                T(lambda pb=pb, bap=bap: nc.tensor.matmul(pb[0:nt, 0:n], lhsT=ones1b[:, 0:nt], rhs=bap, start=True, stop=False),
                  [ones1b, bt], [pb])
            for k in range(8):
                tl, wap = wsl(k, c0, n)
                T(lambda k=k, pb=pb, xT=xT, wap=wap: nc.tensor.matmul(pb[0:nt, 0:n], lhsT=xT[:, k, 0:nt], rhs=wap,
                                               start=(k == 0 and not with_bias), stop=(k == 7)), [xT, tl], [pb])

        def proj_fm(xT, c0, m, out_ap, pb, nt=128):
            bt, bap = bsl(c0, m)
            T(lambda pb=pb, bap=bap: nc.tensor.matmul(out_ap, lhsT=bap, rhs=ones1b[:, 0:nt], start=True, stop=False), [ones1b, bt], [pb])
            for k in range(8):
                tl, wap = wsl(k, c0, m)
                T(lambda k=k, pb=pb, xT=xT, wap=wap: nc.tensor.matmul(out_ap, lhsT=wap, rhs=xT[:, k, 0:nt], start=False, stop=(k == 7)),
                  [xT, tl], [pb])

        def gate_rows(xT, slot, own):
            for k in range(8):
                T(lambda k=k, xT=xT: nc.tensor.matmul(pS[:, 280:288], lhsT=xT[:, k, :], rhs=w1[:, k, O_MI:O_MI + 8],
                                               start=(k == 0), stop=(k == 7)), [xT, w1], [pS])
            V(lambda gt=gt: nc.vector.tensor_tensor(out=gt[:], in0=pS[:, 280:288], in1=biasg[:], op=ALU.add), [pS, biasg], [gt])
            T(lambda gt=gt: nc.tensor.transpose(out=pS[0:4, 0:128], in_=gt[:, 0:4], identity=idf[:]), [gt, idf], [pS])
            T(lambda gt=gt: nc.tensor.transpose(out=pS[0:4, 128:256], in_=gt[:, 4:8], identity=idf[:]), [gt, idf], [pS])
            A(lambda spe=spe: nc.scalar.activation(out=spe[:], in_=pS[0:4, 128:256], func=AF.Exp, scale=-1.0), [pS], [spe])
            A(lambda spe=spe, spr=spr: nc.scalar.activation(out=spr[:], in_=spe[:], func=AF.Ln, bias=1.0), [spe], [spr])
            if own:
                V(lambda lim=lim: nc.vector.tensor_copy(out=lim[:], in_=pS[0:4, 0:128]), [pS], [lim])
                spsrc = spr
            else:
                sl = slice(slot * 128, (slot + 1) * 128)
                rvt = rvR.next(); rnt = rnR.next()
                LD(rvt, rvt[:], rvalid[:, sl])
                LD(rnt, rnt[:], rneg[:, sl])
                V(lambda rvt=rvt, spr=spr, spm=spm: nc.vector.tensor_tensor(out=spm[:], in0=spr[:], in1=rvt[:], op=ALU.mult), [spr, rvt], [spm])
                V(lambda rnt=rnt, lim=lim: nc.vector.tensor_tensor(out=lim[:], in0=pS[0:4, 0:128], in1=rnt[:], op=ALU.add), [pS, rnt], [lim])
                spsrc = spm
            bp = bnegR.cur(); bc = bnegR.next()
            V(lambda: nc.vector.tensor_tensor_scan(out=bc[:], data0=spsrc[:], data1=zer4[:], initial=bp[:, 127:128],
                                                   op0=ALU.add, op1=ALU.add), [spsrc, zer4, bp], [bc])
            V(lambda lim=lim, u_r=u_r: nc.vector.tensor_tensor(out=u_r[:], in0=lim[:], in1=bc[:], op=ALU.add), [lim, bc], [u_r])
            Up = UR.cur(); Uc = UR.next()
            V(lambda Uc=Uc, u_r=u_r: nc.vector.tensor_tensor_scan(out=Uc[:], data0=u_r[:], data1=u_r[:], initial=Up[:, 127:128],
                                                   op0=ALU.max, op1=ALU.max), [u_r, Up], [Uc])
            T(lambda u_r=u_r: nc.tensor.transpose(out=pS[:, 256:260], in_=u_r[:], identity=idf[0:4, 0:4]), [u_r, idf], [pS])
            ncol = 4
            if own:
                V(lambda Uc=Uc, m_r=m_r: nc.vector.tensor_tensor(out=m_r[:], in0=Uc[:], in1=bc[:], op=ALU.subtract), [Uc, bc], [m_r])
                T(lambda Uc=Uc: nc.tensor.transpose(out=pS[:, 260:264], in_=Uc[:], identity=idf[0:4, 0:4]), [Uc, idf], [pS])
                T(lambda m_r=m_r: nc.tensor.transpose(out=pS[:, 264:268], in_=m_r[:], identity=idf[0:4, 0:4]), [m_r, idf], [pS])
                ncol = 12
            V(lambda cols=cols: nc.vector.tensor_copy(out=cols[:, 0:ncol], in_=pS[:, 256:256 + ncol]), [pS], [cols])
            V(lambda Uc=Uc, diagU=diagU: nc.vector.tensor_scalar(out=diagU[:], in0=idf[0:4, 0:4], scalar1=Uc[:, 127:128], scalar2=None, op0=ALU.mult),
              [idf, Uc], [diagU])
            T(lambda diagU=diagU: nc.tensor.matmul(pS[:, 272:276], lhsT=ones4[:], rhs=diagU[:], start=True, stop=True), [ones4, diagU], [pS])
            Uprev = UendR.cur(); Uend = UendR.next()
            V(lambda: nc.vector.tensor_copy(out=Uend[:], in_=pS[:, 272:276]), [pS], [Uend])
            V(lambda cols=cols, exin=exin: nc.vector.tensor_tensor(out=exin[:, 0:4], in0=cols[:, 0:4], in1=Uend[:], op=ALU.subtract), [cols, Uend], [exin])
            V(lambda exin=exin: nc.vector.tensor_tensor(out=exin[:, 4:8], in0=Uprev[:], in1=Uend[:], op=ALU.subtract), [Uprev, Uend], [exin])
            ne = 8
            if own:
                V(lambda cols=cols, exin=exin: nc.vector.tensor_tensor(out=exin[:, 8:12], in0=Uprev[:], in1=cols[:, 4:8], op=ALU.subtract), [Uprev, cols], [exin])
                V(lambda cols=cols, exin=exin: nc.vector.tensor_scalar(out=exin[:, 12:16], in0=cols[:, 8:12], scalar1=-1.0, scalar2=None, op0=ALU.mult), [cols], [exin])
                ne = 16
            A(lambda exin=exin, ex=ex: nc.scalar.activation(out=ex[:, 0:ne], in_=exin[:, 0:ne], func=AF.Exp), [exin], [ex])
            return Uc

        def state_update():
            V(lambda ktm=ktm, kw=kw, ex=ex: nc.vector.tensor_tensor(out=kw[:], in0=ktm[:], in1=ex[:, 0:4].unsqueeze(2).to_broadcast([128, 4, 128]),
                                                                    op=ALU.mult), [ktm, ex], [kw])
            pb = GB.next()
            for h in range(4):
                T(lambda h=h, pb=pb, vbf=vbf, kw=kw: nc.tensor.matmul(pb[:, h * 128:(h + 1) * 128], lhsT=kw[:, h, :], rhs=vbf[:, h, :], start=True, stop=True),
                  [kw, vbf], [pb])
            for h in range(4):
                T(lambda h=h, kw=kw: nc.tensor.matmul(pS[:, 296 + h:297 + h], lhsT=kw[:, h, :], rhs=onescol[:], start=True, stop=True),
                  [kw, onescol], [pS])
            V(lambda ex=ex: nc.vector.tensor_tensor(out=Cst[:], in0=Cst[:], in1=ex[:, 4:8].unsqueeze(2).to_broadcast([128, 4, 128]), op=ALU.mult),
              [Cst, ex], [Cst])
            V(lambda pb=pb: nc.vector.tensor_tensor(out=Cst[:].rearrange("p h d -> p (h d)"), in0=Cst[:].rearrange("p h d -> p (h d)"), in1=pb[:], op=ALU.add),
              [Cst, pb], [Cst])
            V(lambda ex=ex: nc.vector.tensor_tensor(out=nst[:], in0=nst[:], in1=ex[:, 4:8], op=ALU.mult), [nst, ex], [nst])
            V(lambda: nc.vector.tensor_tensor(out=nst[:], in0=nst[:], in1=pS[:, 296:300], op=ALU.add), [nst, pS], [nst])

        def swa_kv(xT, want_f32):
            akT = akTR.next(); va = vaR.next()
            pb = GB.next()
            proj_fm(xT, O_AK, 128, pb[:, 0:128], pb)
            A(lambda pb=pb, akT=akT: nc.scalar.copy(out=akT[:], in_=pb[:, 0:128]), [pb], [akT])
            pb2 = GB.next()
            proj_tm(xT, O_AK, 256, pb2)
            V(lambda pb2=pb2, va=va: nc.vector.tensor_copy(out=va[:, :, 0:64], in_=pb2[:, 128:256].rearrange("p (k d) -> p k d", k=2)), [pb2], [va])
            if want_f32:
                A(lambda pb2=pb2: nc.scalar.copy(out=kavf[:], in_=pb2[:, 0:256]), [pb2], [kavf])
            return akT, va

        def chunk(kind, src_ap, slot=None, c=None):
            own = kind == "own"
            nonlocal hb, ktm, vbf, kw, gt, spe, spr, spm, lim, u_r, m_r, cols, diagU, exin, ex, st6, mv, rstd, nmr
            hb = hbR.next()
            ktm = ktmR.next()
            vbf = vbfR.next()
            kw = kwR.next()
            gt = gtR.next()
            spe = speR.next()
            spr = sprR.next()
            spm = spmR.next()
            lim = limR.next()
            u_r = u_rR.next()
            m_r = m_rR.next()
            cols = colsR.next()
            diagU = diagUR.next()
            exin = exinR.next()
            ex = exR.next()
            st6 = st6R.next()
            mv = mvR.next()
            rstd = rstdR.next()
            nmr = nmrR.next()
            xt = xR.next()
            LD(xt, xt[:], src_ap, lat=6.0)
            hp = hpR.next() if own else None
            ln0(xt, hb, hp)
            xT = xTR.next()
            transpose8(hb, xT)
            warm(WARM_PRE if not own else WARM_OWN)
            pb = GB.next(); proj_tm(xT, O_MK, 512, pb)
            A(lambda pb=pb, ktm=ktm: nc.scalar.mul(out=ktm[:].rearrange("p h d -> p (h d)"), in_=pb[:], mul=KSCALE), [pb], [ktm])
            pb = GB.next(); proj_tm(xT, O_MV, 512, pb)
            A(lambda pb=pb, vbf=vbf: nc.scalar.copy(out=vbf[:].rearrange("p h d -> p (h d)"), in_=pb[:]), [pb], [vbf])
            Uc = gate_rows(xT, slot, own)
            if kind == "meta":
                akT, va = swa_kv(xT, True)
                V(lambda akT=akT: nc.vector.tensor_copy(out=akTm[:], in_=akT[:, 0:16]), [akT], [akTm])
                V(lambda va=va: nc.vector.tensor_copy(out=vam[:, :, 0:64], in_=va[0:16, :, 0:64]), [va], [vam])
                ST(pk_o[0:16, :], kavf, kavf[0:16, 0:128])
                ST(pv_o[0:16, :], kavf, kavf[0:16, 128:256])
            elif kind == "prelast":
                swa_kv(xT, False)
            if own and c == NOWN - 1:
                DBGDUMP("cols", cols, cols[:], [128, 12])
                DBGDUMP("ex", ex, ex[:], [128, 16])
                DBGDUMP("gt", gt, gt[:], [128, 8])
                DBGDUMP("ktm", ktm, ktm[:].rearrange("p h d -> p (h d)"), [128, 512])
                DBGDUMP("vbf", vbf, vbf[:].rearrange("p h d -> p (h d)"), [128, 512])
                DBGDUMP("hp", hp, hp[:], [128, 1024])
                DBGDUMP("Cpre", Cst, Cst[:].rearrange("p h d -> p (h d)"), [128, 512])
            if own:
                own_chunk(xT, hp, c, Uc)
            state_update()
            if own:
                A(lambda: nc.scalar.copy(out=Cbf[:], in_=Cst[:]), [Cst], [Cbf])
                A(lambda: nc.scalar.copy(out=nbf[:], in_=nst[:]), [nst], [nbf])
                if c == NOWN - 1:
                    ST(pc_o.rearrange("h k v -> k h v"), Cst, Cst[:])
                    ST(pn_o.rearrange("h k -> k h"), nst, nst[:], allow_slow_non_contiguous=True)
                    ST(pm_o, m_r, m_r[:, 127:128])
            elif kind == "prelast" or (kind == "meta" and NPRE == 0):
                A(lambda: nc.scalar.copy(out=Cbf[:], in_=Cst[:]), [Cst], [Cbf])
                A(lambda: nc.scalar.copy(out=nbf[:], in_=nst[:]), [nst], [nbf])

        def own_chunk(xT, hp, c, Uc):
            akTp, vap = akTR.cur(), vaR.cur()
            last = (c == NOWN - 1)
            pb = GB.next()
            for h in range(4):
                proj_fm(xT, O_MQ + h * 128, 128, pb[:, h * 128:(h + 1) * 128], pb)
            A(lambda pb=pb, h=h: nc.scalar.copy(out=qT[:].rearrange("p h t -> p (h t)"), in_=pb[:]), [pb], [qT])
            for h in range(4):
                T(lambda h=h, ktm=ktm: nc.tensor.transpose(out=pT[:, h * 128:(h + 1) * 128], in_=ktm[:, h, :], identity=idb[:]), [ktm, idb], [pT])
            A(lambda: nc.scalar.copy(out=kT[:].rearrange("p h t -> p (h t)"), in_=pT[:, 0:512]), [pT], [kT])
            warm(WARM_OWN)
            for h in range(4):
                V(lambda h=h, Uc=Uc: nc.vector.tensor_scalar(out=bd[:, h, :], in0=Uc[:], scalar1=negid4[:, h:h + 1], scalar2=None, op0=ALU.mult),
                  [Uc, negid4], [bd])
            pU = GB.next()
            T(lambda pU=pU, h=h: nc.tensor.matmul(pU[:], lhsT=ones4[:], rhs=bd[:].rearrange("p h t -> p (h t)"), start=True, stop=True), [ones4, bd], [pU])
            for h in range(4):
                A(lambda h=h, pU=pU, cols=cols: nc.scalar.activation(out=Wt[:, h, :], in_=pU[:, h * 128:(h + 1) * 128], func=AF.Exp, bias=cols[:, h:h + 1]),
                  [pU, cols], [Wt])
            V(lambda h=h: nc.vector.tensor_tensor(out=Wm[:].rearrange("p h t -> p (h t)"), in0=Wt[:].rearrange("p h t -> p (h t)"),
                                              in1=tri4[:], op=ALU.mult), [Wt, tri4], [Wm])
            pSc = GB.next()
            for h in range(4):
                T(lambda h=h, pSc=pSc: nc.tensor.matmul(pSc[:, h * 128:(h + 1) * 128], lhsT=kT[:, h, :], rhs=qT[:, h, :], start=True, stop=True),
                  [kT, qT], [pSc])
            V(lambda pSc=pSc, h=h: nc.vector.tensor_tensor(out=PTt[:].rearrange("p h t -> p (h t)"), in0=pSc[:], in1=Wm[:].rearrange("p h t -> p (h t)"),
                                              op=ALU.mult), [pSc, Wm], [PTt])
            pN = GB.next(); pI = GB.next()
            for h in range(4):
                T(lambda h=h, pN=pN, vbf=vbf: nc.tensor.matmul(pN[:, h * 128:(h + 1) * 128], lhsT=PTt[:, h, :], rhs=vbf[:, h, :], start=True, stop=True),
                  [PTt, vbf], [pN])
            for h in range(4):
                T(lambda h=h, pI=pI: nc.tensor.matmul(pI[:, h * 128:(h + 1) * 128], lhsT=qT[:, h, :], rhs=Cbf[:, h, :], start=True, stop=True),
                  [qT, Cbf], [pI])
            for h in range(4):
                T(lambda h=h: nc.tensor.matmul(pS[:, 288 + h:289 + h], lhsT=PTt[:, h, :], rhs=onescol[:], start=True, stop=True),
                  [PTt, onescol], [pS])
            for h in range(4):
                T(lambda h=h: nc.tensor.matmul(pS[:, 292 + h:293 + h], lhsT=qT[:, h, :], rhs=nbf[:, h:h + 1], start=True, stop=True),
                  [qT, nbf], [pS])
            for h in range(4):
                A(lambda h=h, pI=pI, ex=ex: nc.scalar.mul(out=hi[:, h, :], in_=pI[:, h * 128:(h + 1) * 128], mul=ex[:, 8 + h:9 + h]),
                  [pI, ex], [hi])
            V(lambda pN=pN, h=h: nc.vector.tensor_tensor(out=hn[:].rearrange("p h t -> p (h t)"), in0=hi[:].rearrange("p h t -> p (h t)"), in1=pN[:],
                                              op=ALU.add), [hi, pN], [hn])
            V(lambda ex=ex: nc.vector.tensor_tensor(out=d1[:], in0=pS[:, 292:296], in1=ex[:, 8:12], op=ALU.mult), [pS, ex], [d1])
            V(lambda: nc.vector.tensor_tensor(out=d2[:], in0=d1[:], in1=pS[:, 288:292], op=ALU.add), [d1, pS], [d2])
            V(lambda: nc.vector.scalar_tensor_tensor(out=d1[:], in0=d2[:], scalar=-1.0, in1=d2[:], op0=ALU.mult, op1=ALU.max), [d2], [d1])
            V(lambda ex=ex: nc.vector.tensor_tensor(out=d2[:], in0=d1[:], in1=ex[:, 12:16], op=ALU.max), [d1, ex], [d2])
            V(lambda: nc.vector.reciprocal(out=rden[:], in_=d2[:]), [d2], [rden])
            pb = GB.next(); proj_tm(xT, O_MO, 512, pb)
            A(lambda pb=pb: nc.scalar.activation(out=eo[:], in_=pb[:], func=AF.Exp, scale=-1.0), [pb], [eo])
            A(lambda: nc.scalar.activation(out=so[:], in_=eo[:], func=AF.Ln, bias=1.0), [eo], [so])
            A(lambda: nc.scalar.activation(out=so[:], in_=so[:], func=AF.Exp, scale=-1.0), [so], [so])
            V(lambda: nc.vector.tensor_tensor(out=hg[:], in0=hn[:], in1=rden[:].unsqueeze(2).to_broadcast([128, 4, 128]), op=ALU.mult),
              [hn, rden], [hg])
            V(lambda h=h: nc.vector.tensor_tensor(out=hg[:].rearrange("p h t -> p (h t)"), in0=hg[:].rearrange("p h t -> p (h t)"), in1=so[:],
                                              op=ALU.mult), [hg, so], [hg])
            for h in range(4):
                V(lambda h=h: nc.vector.bn_stats(out=st4[:, h, :], in_=hg[:, h, :]), [hg], [st4])
            for h in range(4):
                V(lambda h=h: nc.vector.bn_aggr(out=mv4[:, h, :], in_=st4[:, h, :]), [st4], [mv4])
            A(lambda: nc.scalar.activation(out=rs4[:].unsqueeze(2), in_=mv4[:, :, 1:2], func=AF.Ln, bias=LN_EPS), [mv4], [rs4])
            A(lambda: nc.scalar.activation(out=rs4[:], in_=rs4[:], func=AF.Exp, scale=-0.5), [rs4], [rs4])
            for h in range(4):
                V(lambda h=h: nc.vector.tensor_scalar(out=hg[:, h, :], in0=hg[:, h, :], scalar1=mv4[:, h, 0:1], scalar2=rs4[:, h:h + 1],
                                                      op0=ALU.subtract, op1=ALU.mult), [hg, mv4, rs4], [hg])
            pb = GB.next(); proj_tm(xT, O_MZ, 512, pb)
            A(lambda pb=pb: nc.scalar.activation(out=ez[:], in_=pb[:], func=AF.Exp, scale=-1.0), [pb], [ez])
            A(lambda: nc.scalar.activation(out=ez[:], in_=ez[:], func=AF.Ln, bias=1.0), [ez], [ez])
            A(lambda: nc.scalar.activation(out=ez[:], in_=ez[:], func=AF.Exp, scale=-1.0), [ez], [ez])
            V(lambda pb=pb: nc.vector.tensor_tensor(out=ez[:], in0=ez[:], in1=pb[:], op=ALU.mult), [ez, pb], [ez])
            V(lambda h=h: nc.vector.tensor_tensor(out=ybf[:, 0:512], in0=hg[:].rearrange("p h t -> p (h t)"), in1=sz[:], op=ALU.mult),
              [hg, sz], [ybf])

            akT, va = swa_kv(xT, last)
            if last:
                ST(pk_o[16:144, :], kavf, kavf[:, 0:128])
                ST(pv_o[16:144, :], kavf, kavf[:, 128:256])
            pb = GB.next(); proj_tm(xT, O_AQ, 512, pb)
            A(lambda pb=pb: nc.scalar.copy(out=Wt[:].rearrange("p g (k d) -> p g k d", k=2),
                                           in_=pb[:].rearrange("p (k g d) -> p g k d", k=2, g=4)), [pb], [Wt])
            for g in range(4):
                T(lambda g=g: nc.tensor.transpose(out=pT[:, g * 128:(g + 1) * 128], in_=Wt[:, g, :], identity=idb[:]), [Wt, idb], [pT])
            V(lambda: nc.vector.tensor_copy(out=aqT[:].rearrange("p g t -> p (g t)"), in_=pT[:, 0:512]), [pT], [aqT])

            pb = GB.next(); proj_tm(xT, O_AZ, 512, pb)
            A(lambda pb=pb: nc.scalar.activation(out=eaz[:], in_=pb[:], func=AF.Exp, scale=-1.0), [pb], [eaz])
            A(lambda: nc.scalar.activation(out=eaz[:], in_=eaz[:], func=AF.Ln, bias=1.0), [eaz], [eaz])
            A(lambda: nc.scalar.activation(out=eaz[:], in_=eaz[:], func=AF.Exp, scale=-1.0), [eaz], [eaz])
            V(lambda pb=pb: nc.vector.tensor_tensor(out=eaz[:], in0=eaz[:], in1=pb[:], op=ALU.mult), [eaz, pb], [eaz])
            warm(WARM_OWN)
            pmask = pm1b if c == 0 else prev4
            for kap in range(2):
                ks = slice(64 * kap, 64 * kap + 64)
                pb = GB.next()
                for g in range(4):
                    T(lambda g=g, pb=pb, akT=akT, ks=ks: nc.tensor.matmul(pb[:, g * 128:(g + 1) * 128], lhsT=akT[ks, :], rhs=aqT[ks, g, :], start=True, stop=True),
                      [akT, aqT], [pb])
                A(lambda pb=pb: nc.scalar.activation(out=Eown[:], in_=pb[:], func=AF.Exp, scale=ASCALE), [pb], [Eown])
                V(lambda: nc.vector.tensor_tensor(out=Eown2[:], in0=Eown[:], in1=tri4[:], op=ALU.mult), [Eown, tri4], [Eown2])
                pb = GB.next()
                for g in range(4):
                    T(lambda g=g, pb=pb, akTp=akTp, ks=ks: nc.tensor.matmul(pb[:, g * 128:(g + 1) * 128], lhsT=akTp[ks, :], rhs=aqT[ks, g, :], start=True, stop=True),
                      [akTp, aqT], [pb])
                A(lambda pb=pb: nc.scalar.activation(out=Eprev[:], in_=pb[:], func=AF.Exp, scale=ASCALE), [pb], [Eprev])
                V(lambda pmask=pmask: nc.vector.tensor_tensor(out=Eprev2[:], in0=Eprev[:], in1=pmask[:], op=ALU.mult), [Eprev, pmask], [Eprev2])
                pb = GB.next()
                for g in range(4):
                    T(lambda g=g, pb=pb, ks=ks: nc.tensor.matmul(pb[0:16, g * 128:(g + 1) * 128], lhsT=akTm[ks, :], rhs=aqT[ks, g, :], start=True, stop=True),
                      [akTm, aqT], [pb])
                A(lambda pb=pb: nc.scalar.activation(out=Emeta[:], in_=pb[0:16, :], func=AF.Exp, scale=ASCALE), [pb], [Emeta])
                po = GB.next()
                for g in range(4):
                    oap = po[:, g * 65:(g + 1) * 65]
                    T(lambda g=g, oap=oap, po=po, kap=kap: nc.tensor.matmul(oap, lhsT=Emeta[:, g * 128:(g + 1) * 128], rhs=vam[:, kap, :], start=True, stop=False),
                      [Emeta, vam], [po])
                    T(lambda g=g, oap=oap, po=po, vap=vap, kap=kap: nc.tensor.matmul(oap, lhsT=Eprev2[:, g * 128:(g + 1) * 128], rhs=vap[:, kap, :], start=False, stop=False),
                      [Eprev2, vap], [po])
                    T(lambda g=g, oap=oap, po=po, va=va, kap=kap: nc.tensor.matmul(oap, lhsT=Eown2[:, g * 128:(g + 1) * 128], rhs=va[:, kap, :], start=False, stop=True),
                      [Eown2, va], [po])
                po3 = po[:, 0:260].rearrange("p (g e) -> p g e", g=4)
                V(lambda po3=po3, po=po, kap=kap: nc.vector.tensor_tensor(out=dsum[:].unsqueeze(2), in0=po3[:, :, 64:65],
                                                          in1=esink[:, 4 * kap:4 * kap + 4].unsqueeze(2), op=ALU.add),
                  [po, esink], [dsum])
                V(lambda: nc.vector.reciprocal(out=rsa[:], in_=dsum[:]), [dsum], [rsa])
                V(lambda po3=po3, po=po: nc.vector.tensor_tensor(out=ya[:], in0=po3[:, :, 0:64], in1=rsa[:].unsqueeze(2).to_broadcast([128, 4, 64]),
                                                          op=ALU.mult), [po, rsa], [ya])
                V(lambda kap=kap, g=g: nc.vector.tensor_tensor(out=ybf[:, 512 + 256 * kap:768 + 256 * kap], in0=ya[:].rearrange("p g d -> p (g d)"),
                                                  in1=saz[:, 256 * kap:256 * kap + 256], op=ALU.mult), [ya, saz], [ybf])

            if c == NOWN - 1:
                DBGDUMP("ybf", ybf, ybf[:], [128, 1024])
                DBGDUMP("hn", hi, hi[:].rearrange("p h d -> p (h d)"), [128, 512])
            transpose8(ybf, yT)
            warm(WARM_OWN)
            pm_ = [GB.next(), GB.next()]
            for hf in range(2):
                T(lambda hf=hf: nc.tensor.matmul(pm_[hf][:], lhsT=ones1b[:, 0:128], rhs=bob[:, hf * 512:(hf + 1) * 512], start=True, stop=False),
                  [ones1b, bob], [pm_[hf]])
                for k in range(8):
                    T(lambda k=k, hf=hf: nc.tensor.matmul(pm_[hf][:], lhsT=yT[:, k, :], rhs=wo_bf[:, k, hf * 512:(hf + 1) * 512],
                                                          start=False, stop=(k == 7)), [yT, wo_bf], [pm_[hf]])
            V(lambda hp=hp: nc.vector.tensor_tensor(out=hp[:], in0=hp[:], in1=ln0g[:], op=ALU.mult), [hp, ln0g], [hp])
            for hf in range(2):
                V(lambda hf=hf, hp=hp: nc.vector.tensor_tensor(out=hp[:, hf * 512:(hf + 1) * 512], in0=hp[:, hf * 512:(hf + 1) * 512],
                                                               in1=pm_[hf][:], op=ALU.add),
                  [hp, pm_[hf]], [hp])
            layer_norm(hp, hp, lng, lnb)
            P.dma("pool", lambda hp=hp: nc.gpsimd.dma_start(out=y_o[c * 128:(c + 1) * 128, :], in_=hp[:]), reads=[hp], key="S_" + hp.name)


        def sample_program():
            nonlocal hb, ktm, vbf, kw, gt, spe, spr, spm, lim, u_r, m_r, cols, diagU, exin, ex, st6, mv, rstd, nmr
            hb = hbR.next()
            ktm = ktmR.next()
            vbf = vbfR.next()
            kw = kwR.next()
            gt = gtR.next()
            spe = speR.next()
            spr = sprR.next()
            spm = spmR.next()
            lim = limR.next()
            u_r = u_rR.next()
            m_r = m_rR.next()
            cols = colsR.next()
            diagU = diagUR.next()
            exin = exinR.next()
            ex = exR.next()
            st6 = st6R.next()
            mv = mvR.next()
            rstd = rstdR.next()
            nmr = nmrR.next()
            seqcol = sb("seqcol", [128, 16])
            blk4 = sb("blk4", [128, 512], BF16); winm = sb("winm", [128, 256], BF16); newm = sb("newm", [128, 256], BF16)
            esk16 = sb("esk16", [16, 2])
            LD(seqcol, seqcol[:], cseqcol)
            LD(cstage, cstage[:, 0:512], cblk4)
            V(lambda: nc.vector.tensor_copy(out=blk4[:], in_=cstage[:, 0:512]), [cstage], [blk4])
            LD(cstage, cstage[:, 0:256], cwin)
            V(lambda: nc.vector.tensor_copy(out=winm[:], in_=cstage[:, 0:256]), [cstage], [winm])
            LD(cstage, cstage[:, 0:256], cnew)
            V(lambda: nc.vector.tensor_copy(out=newm[:], in_=cstage[:, 0:256]), [cstage], [newm])
            LD(esk16, esk16[:], sinks16)
            A(lambda: nc.scalar.activation(out=esk16[:], in_=esk16[:], func=AF.Exp), [esk16], [esk16])

            xt = xR.next()
            V(lambda xt=xt: nc.vector.memset(xt[:], 0.0), [], [xt])
            LD(xt, xt[0:64, :], xs)
            hp = hpR.next()
            ln0(xt, hb, hp)
            xT = xTR.next()
            transpose8(hb, xT)
            vaug = sb("vaug", [128, 4, 129], BF16)
            V(lambda: nc.vector.memset(vaug[:], 1.0), [], [vaug])
            pb = GB.next(); proj_tm(xT, O_MK, 512, pb)
            A(lambda pb=pb, ktm=ktm: nc.scalar.mul(out=ktm[:].rearrange("p h d -> p (h d)"), in_=pb[:], mul=KSCALE), [pb], [ktm])
            pb = GB.next(); proj_tm(xT, O_MV, 512, pb)
            V(lambda pb=pb: nc.vector.tensor_copy(out=vaug[:, :, 0:128], in_=pb[:].rearrange("p (h d) -> p h d", h=4)), [pb], [vaug])

            for k in range(8):
                T(lambda k=k, xT=xT: nc.tensor.matmul(pS[:, 280:288], lhsT=xT[:, k, :], rhs=w1[:, k, O_MI:O_MI + 8],
                                                      start=(k == 0), stop=(k == 7)), [xT, w1], [pS])
            V(lambda gt=gt: nc.vector.tensor_tensor(out=gt[:], in0=pS[:, 280:288], in1=biasg[:], op=ALU.add), [pS, biasg], [gt])
            T(lambda gt=gt: nc.tensor.transpose(out=pS[0:4, 0:128], in_=gt[:, 0:4], identity=idf[:]), [gt, idf], [pS])
            T(lambda gt=gt: nc.tensor.transpose(out=pS[0:4, 128:256], in_=gt[:, 4:8], identity=idf[:]), [gt, idf], [pS])
            A(lambda spe=spe: nc.scalar.activation(out=spe[:], in_=pS[0:4, 128:256], func=AF.Exp, scale=-1.0), [pS], [spe])
            A(lambda spe=spe, spr=spr: nc.scalar.activation(out=spr[:], in_=spe[:], func=AF.Ln, bias=1.0), [spe], [spr])
            V(lambda lim=lim: nc.vector.tensor_copy(out=lim[:], in_=pS[0:4, 0:128]), [pS], [lim])
            m0 = sb("m0", [4, 16]); bs = sb("bs", [4, 128]); Us = sb("Us", [4, 128]); ueb = sb("ueb", [4, 128]); m0b = sb("m0b", [4, 128])
            decr = sb("decr", [4, 16]); bdd = sb("bdd", [4, 16, 4]); decbc = sb("decbc", [128, 64])
            LD(m0, m0[:], sm_i)
            for t_ in (bs, Us, ueb, m0b):
                V(lambda t_=t_: nc.vector.memset(t_[:], 0.0), [], [t_])
            v3 = lambda t_: t_[:, 0:64].rearrange("p (j l) -> p j l", l=4)
            V(lambda: nc.vector.tensor_copy(out=v3(bs)[:, :, 0:1], in_=v3(spr)[:, :, 0:1]), [spr], [bs])
            for l in range(1, 4):
                V(lambda l=l: nc.vector.tensor_tensor(out=v3(bs)[:, :, l:l + 1], in0=v3(bs)[:, :, l - 1:l], in1=v3(spr)[:, :, l:l + 1], op=ALU.add),
                  [bs, spr], [bs])
            V(lambda lim=lim, u_r=u_r: nc.vector.tensor_tensor(out=u_r[:], in0=lim[:], in1=bs[:], op=ALU.add), [lim, bs], [u_r])
            V(lambda: nc.vector.tensor_tensor(out=v3(Us)[:, :, 0:1], in0=v3(u_r)[:, :, 0:1], in1=m0[:].unsqueeze(2), op=ALU.max), [u_r, m0], [Us])
            for l in range(1, 4):
                V(lambda l=l: nc.vector.tensor_tensor(out=v3(Us)[:, :, l:l + 1], in0=v3(Us)[:, :, l - 1:l], in1=v3(u_r)[:, :, l:l + 1], op=ALU.max),
                  [Us, u_r], [Us])
            V(lambda m_r=m_r: nc.vector.tensor_tensor(out=m_r[:], in0=Us[:], in1=bs[:], op=ALU.subtract), [Us, bs], [m_r])
            V(lambda: nc.vector.tensor_copy(out=v3(ueb), in_=v3(Us)[:, :, 3:4].to_broadcast([4, 16, 4])), [Us], [ueb])
            V(lambda: nc.vector.tensor_copy(out=v3(m0b), in_=m0[:].unsqueeze(2).to_broadcast([4, 16, 4])), [m0], [m0b])
            mnew = sb("mnew", [4, 16])
            V(lambda: nc.vector.tensor_copy(out=mnew[:].unsqueeze(2), in_=v3(m_r)[:, :, 3:4]), [m_r], [mnew])
            ST(sm_o, mnew, mnew[:])
            cols5 = sb("cols5", [128, 20])
            for i_, rt in enumerate((u_r, Us, m_r, ueb, m0b)):
                T(lambda i_=i_, rt=rt: nc.tensor.transpose(out=pS[:, 256 + 4 * i_:260 + 4 * i_], in_=rt[:], identity=idf[0:4, 0:4]), [rt, idf], [pS])
            V(lambda: nc.vector.tensor_copy(out=cols5[:], in_=pS[:, 256:276]), [pS], [cols5])
            V(lambda cols=cols: nc.vector.tensor_copy(out=cols[:, 0:4], in_=cols5[:, 0:4]), [cols5], [cols])
            V(lambda exin=exin: nc.vector.tensor_tensor(out=exin[:, 0:4], in0=cols5[:, 0:4], in1=cols5[:, 12:16], op=ALU.subtract), [cols5], [exin])
            V(lambda exin=exin: nc.vector.tensor_tensor(out=exin[:, 8:12], in0=cols5[:, 16:20], in1=cols5[:, 4:8], op=ALU.subtract), [cols5], [exin])
            V(lambda exin=exin: nc.vector.tensor_scalar(out=exin[:, 12:16], in0=cols5[:, 8:12], scalar1=-1.0, scalar2=None, op0=ALU.mult), [cols5], [exin])
            V(lambda exin=exin: nc.vector.memset(exin[:, 4:8], 0.0), [], [exin])
            A(lambda exin=exin, ex=ex: nc.scalar.activation(out=ex[:, 0:16], in_=exin[:, 0:16], func=AF.Exp), [exin], [ex])
            V(lambda: nc.vector.tensor_tensor(out=decr[:].unsqueeze(2), in0=m0[:].unsqueeze(2), in1=v3(Us)[:, :, 3:4], op=ALU.subtract), [m0, Us], [decr])
            A(lambda: nc.scalar.activation(out=decr[:], in_=decr[:], func=AF.Exp), [decr], [decr])
            for h in range(4):
                V(lambda h=h: nc.vector.tensor_scalar(out=bdd[:, :, h:h + 1], in0=decr[:].unsqueeze(2), scalar1=idf[0:4, h:h + 1], scalar2=None, op0=ALU.mult),
                  [decr, idf], [bdd])
            T(lambda: nc.tensor.matmul(pS[:, 300:364], lhsT=ones4[:], rhs=bdd[:].rearrange("p j h -> p (j h)"), start=True, stop=True),
              [ones4, bdd], [pS])
            V(lambda: nc.vector.tensor_copy(out=decbc[:], in_=pS[:, 300:364]), [pS], [decbc])

            akT, va = swa_kv(xT, True)
            pb = GB.next(); proj_tm(xT, O_MO, 512, pb)
            A(lambda pb=pb: nc.scalar.activation(out=eo[:], in_=pb[:], func=AF.Exp, scale=-1.0), [pb], [eo])
            pb = GB.next(); proj_tm(xT, O_MZ, 512, pb)
            A(lambda pb=pb: nc.scalar.activation(out=ez[:], in_=pb[:], func=AF.Exp, scale=-1.0), [pb], [ez])
            A(lambda: nc.scalar.activation(out=ez[:], in_=ez[:], func=AF.Ln, bias=1.0), [ez], [ez])
            A(lambda: nc.scalar.activation(out=ez[:], in_=ez[:], func=AF.Exp, scale=-1.0), [ez], [ez])
            V(lambda pb=pb: nc.vector.tensor_tensor(out=ez[:], in0=ez[:], in1=pb[:], op=ALU.mult), [ez, pb], [ez])
            pb = GB.next(); proj_tm(xT, O_AZ, 512, pb)
            A(lambda pb=pb: nc.scalar.activation(out=eaz[:], in_=pb[:], func=AF.Exp, scale=-1.0), [pb], [eaz])
            A(lambda: nc.scalar.activation(out=eaz[:], in_=eaz[:], func=AF.Ln, bias=1.0), [eaz], [eaz])
            A(lambda: nc.scalar.activation(out=eaz[:], in_=eaz[:], func=AF.Exp, scale=-1.0), [eaz], [eaz])
            V(lambda pb=pb: nc.vector.tensor_tensor(out=eaz[:], in0=eaz[:], in1=pb[:], op=ALU.mult), [eaz, pb], [eaz])
            pb = GB.next()
            for h in range(4):
                proj_fm(xT, O_MQ + h * 128, 128, pb[:, h * 128:(h + 1) * 128], pb)
            A(lambda pb=pb: nc.scalar.copy(out=qT[:].rearrange("p h t -> p (h t)"), in_=pb[:]), [pb], [qT])
            for h in range(4):
                T(lambda h=h, ktm=ktm: nc.tensor.transpose(out=pT[:, h * 128:(h + 1) * 128], in_=ktm[:, h, :], identity=idb[:]), [ktm, idb], [pT])
            A(lambda: nc.scalar.copy(out=kT[:].rearrange("p h t -> p (h t)"), in_=pT[:, 0:512]), [pT], [kT])
            pb = GB.next(); proj_tm(xT, O_AQ, 512, pb)
            A(lambda pb=pb: nc.scalar.copy(out=Wt[:].rearrange("p g (k d) -> p g k d", k=2),
                                           in_=pb[:].rearrange("p (k g d) -> p g k d", k=2, g=4)), [pb], [Wt])
            for g in range(4):
                T(lambda g=g: nc.tensor.transpose(out=pT[:, g * 128:(g + 1) * 128], in_=Wt[:, g, :], identity=idb[:]), [Wt, idb], [pT])
            V(lambda: nc.vector.tensor_copy(out=aqT[:].rearrange("p g t -> p (g t)"), in_=pT[:, 0:512]), [pT], [aqT])

            for h in range(4):
                V(lambda h=h: nc.vector.tensor_scalar(out=bd[:, h, :], in0=Us[:], scalar1=negid4[:, h:h + 1], scalar2=None, op0=ALU.mult),
                  [Us, negid4], [bd])
            pU = GB6.tiles[4]
            T(lambda: nc.tensor.matmul(pU[:], lhsT=ones4[:], rhs=bd[:].rearrange("p h t -> p (h t)"), start=True, stop=True), [ones4, bd], [pU])
            for h in range(4):
                A(lambda h=h, cols=cols: nc.scalar.activation(out=Wt[:, h, :], in_=pU[:, h * 128:(h + 1) * 128], func=AF.Exp, bias=cols[:, h:h + 1]),
                  [pU, cols], [Wt])
            V(lambda: nc.vector.tensor_tensor(out=Wm[:].rearrange("p h t -> p (h t)"), in0=Wt[:].rearrange("p h t -> p (h t)"),
                                              in1=blk4[:], op=ALU.mult), [Wt, blk4], [Wm])
            pSc = GB6.tiles[5]
            for h in range(4):
                T(lambda h=h: nc.tensor.matmul(pSc[:, h * 128:(h + 1) * 128], lhsT=kT[:, h, :], rhs=qT[:, h, :], start=True, stop=True),
                  [kT, qT], [pSc])
            V(lambda: nc.vector.tensor_tensor(out=PTt[:].rearrange("p h t -> p (h t)"), in0=pSc[:], in1=Wm[:].rearrange("p h t -> p (h t)"),
                                              op=ALU.mult), [pSc, Wm], [PTt])
            pN = GB6.tiles[4]
            for h in range(4):
                T(lambda h=h: nc.tensor.matmul(pN[:, h * 128:(h + 1) * 128], lhsT=PTt[:, h, :], rhs=vaug[:, h, 0:128], start=True, stop=True),
                  [PTt, vaug], [pN])
            for h in range(4):
                T(lambda h=h: nc.tensor.matmul(pS[:, 288 + h:289 + h], lhsT=PTt[:, h, :], rhs=onescol[:], start=True, stop=True),
                  [PTt, onescol], [pS])
            xsp = xR.next()
            hnum_ap = xsp[:, 0:512]
            V(lambda: nc.vector.tensor_copy(out=hnum_ap, in_=pN[:]), [pN], [xsp])
            V(lambda: nc.vector.tensor_copy(out=d2[:], in_=pS[:, 288:292]), [pS], [d2])

            snl = sb("snl", [64, 128]); nT = sb("nT", [128, 64]); nTn = sb("nTn", [128, 64]); snout = snl
            LD(snl, snl[:], sn_i)
            T(lambda: nc.tensor.transpose(out=pS[:, 364:428], in_=snl[:], identity=idf[0:64, 0:64]), [snl, idf], [pS])
            V(lambda: nc.vector.tensor_copy(out=nT[:], in_=pS[:, 364:428]), [pS], [nT])
            CbR = rot("Cb", [128, 4, 129], BF16, 2)
            qmR = rot("qm", [128, 4, 64], BF16, 2); kwmR = rot("kwm", [64, 4, 128], BF16, 2)
            for t_ in qmR.tiles:
                V(lambda t_=t_: nc.vector.memset(t_[:], 0.0), [], [t_])
            for h in range(4):
                V(lambda h=h, ktm=ktm, kw=kw, ex=ex: nc.vector.tensor_scalar(out=kw[:, h, :], in0=ktm[:, h, :], scalar1=ex[:, h:h + 1], scalar2=None, op0=ALU.mult),
                  [ktm, ex], [kw])
            pIs = GB6.tiles[0:4]
            pUp = [GB6.tiles[4], GB6.tiles[5]]
            clR = Rot(wst.tiles + [t_ for t_ in hpR.tiles if t_ is not hp])
            for j in range(16):
                Clt = clR.next(); Cb = CbR.next(); qm = qmR.next(); kwm = kwmR.next()
                Cl = Clt[:, 0:516].rearrange("p (h e) -> p h e", e=129)
                LD(Clt, Cl[:, :, 0:128], sc_i[j].rearrange("h k v -> k h v"))
                G(lambda Cl=Cl, j=j: nc.gpsimd.tensor_copy(out=Cl[:, :, 128:129], in_=nT[:, 4 * j:4 * j + 4].unsqueeze(2)), [nT], [Clt])
                A(lambda Cl=Cl, Cb=Cb: nc.scalar.copy(out=Cb[:], in_=Cl), [Clt], [Cb])
                if j >= 2:
                    G(lambda qm=qm, j=j: nc.gpsimd.memset(qm[:, :, 4 * (j - 2):4 * (j - 2) + 4], 0.0), [], [qm])
                G(lambda qm=qm, j=j: nc.gpsimd.tensor_copy(out=qm[:, :, 4 * j:4 * j + 4], in_=qT[:, :, 4 * j:4 * j + 4]), [qT], [qm])
                for h in range(4):
                    T(lambda h=h, qm=qm, Cb=Cb, j=j: nc.tensor.matmul(pIs[h][0:64, 0:129], lhsT=qm[:, h, :], rhs=Cb[:, h, :],
                                                                      start=(j == 0), stop=(j == 15)), [qm, Cb], [pIs[h]])
                V(lambda kwm=kwm, j=j, kw=kw: nc.vector.tensor_scalar(out=kwm[:].rearrange("p h d -> p (h d)"), in0=kw[0:64].rearrange("p h d -> p (h d)"),
                                                              scalar1=seqcol[0:64, j:j + 1], scalar2=None, op0=ALU.mult), [kw, seqcol], [kwm])
                for h in range(4):
                    pu = pUp[h // 2]
                    T(lambda h=h, pu=pu, kwm=kwm: nc.tensor.matmul(pu[:, (h % 2) * 129:(h % 2) * 129 + 129], lhsT=kwm[:, h, :], rhs=vaug[0:64, h, :],
                                                                   start=True, stop=True), [kwm, vaug], [pu])
                for h in range(4):
                    pu = pUp[h // 2]
                    V(lambda h=h, pu=pu, Cl=Cl, j=j: nc.vector.scalar_tensor_tensor(
                        out=Cl[:, h, :], in0=Cl[:, h, :], scalar=decbc[:, 4 * j + h:4 * j + h + 1],
                        in1=pu[:, (h % 2) * 129:(h % 2) * 129 + 129], op0=ALU.mult, op1=ALU.add), [Clt, decbc, pu], [Clt])
                G(lambda Cl=Cl, j=j: nc.gpsimd.tensor_copy(out=nTn[:, 4 * j:4 * j + 4].unsqueeze(2), in_=Cl[:, :, 128:129]), [Clt], [nTn])
                ST(sc_o[j].rearrange("h k v -> k h v"), Clt, Cl[:, :, 0:128])
            T(lambda: nc.tensor.transpose(out=pS[0:64, 0:128], in_=nTn[:], identity=idf[:]), [nTn, idf], [pS])
            V(lambda: nc.vector.tensor_copy(out=snout[:], in_=pS[0:64, 0:128]), [pS], [snout])
            ST(sn_o, snout, snout[:])

            for h in range(4):
                A(lambda h=h, ex=ex: nc.scalar.mul(out=hi[0:64, h, :], in_=pIs[h][0:64, 0:128], mul=ex[0:64, 8 + h:9 + h]), [pIs[h], ex], [hi])
                V(lambda h=h, ex=ex: nc.vector.tensor_tensor(out=d1[0:64, h:h + 1], in0=pIs[h][0:64, 128:129], in1=ex[0:64, 8 + h:9 + h], op=ALU.mult),
                  [pIs[h], ex], [d1])
            V(lambda: nc.vector.tensor_tensor(out=hi[0:64].rearrange("p h t -> p (h t)"), in0=hi[0:64].rearrange("p h t -> p (h t)"),
                                              in1=xsp[0:64, 0:512], op=ALU.add), [hi, xsp], [hi])
            V(lambda: nc.vector.tensor_tensor(out=d2[0:64], in0=d1[0:64], in1=d2[0:64], op=ALU.add), [d1, d2], [d2])
            V(lambda: nc.vector.scalar_tensor_tensor(out=d1[0:64], in0=d2[0:64], scalar=-1.0, in1=d2[0:64], op0=ALU.mult, op1=ALU.max), [d2], [d1])
            V(lambda ex=ex: nc.vector.tensor_tensor(out=d2[0:64], in0=d1[0:64], in1=ex[0:64, 12:16], op=ALU.max), [d1, ex], [d2])
            V(lambda: nc.vector.reciprocal(out=rden[0:64], in_=d2[0:64]), [d2], [rden])
            A(lambda: nc.scalar.activation(out=eo[:], in_=eo[:], func=AF.Ln, bias=1.0), [eo], [eo])
            A(lambda: nc.scalar.activation(out=eo[:], in_=eo[:], func=AF.Exp, scale=-1.0), [eo], [eo])
            V(lambda: nc.vector.tensor_tensor(out=hi[0:64], in0=hi[0:64], in1=rden[0:64].unsqueeze(2).to_broadcast([64, 4, 128]), op=ALU.mult),
              [hi, rden], [hi])
            V(lambda: nc.vector.tensor_tensor(out=hi[0:64].rearrange("p h t -> p (h t)"), in0=hi[0:64].rearrange("p h t -> p (h t)"),
                                              in1=eo[0:64], op=ALU.mult), [hi, eo], [hi])
            for h in range(4):
                V(lambda h=h: nc.vector.bn_stats(out=st4[0:64, h, :], in_=hi[0:64, h, :]), [hi], [st4])
            for h in range(4):
                V(lambda h=h: nc.vector.bn_aggr(out=mv4[0:64, h, :], in_=st4[0:64, h, :]), [st4], [mv4])
            A(lambda: nc.scalar.activation(out=rs4[0:64].unsqueeze(2), in_=mv4[0:64, :, 1:2], func=AF.Ln, bias=LN_EPS), [mv4], [rs4])
            A(lambda: nc.scalar.activation(out=rs4[0:64], in_=rs4[0:64], func=AF.Exp, scale=-0.5), [rs4], [rs4])
            for h in range(4):
                V(lambda h=h: nc.vector.tensor_scalar(out=hi[0:64, h, :], in0=hi[0:64, h, :], scalar1=mv4[0:64, h, 0:1], scalar2=rs4[0:64, h:h + 1],
                                                      op0=ALU.subtract, op1=ALU.mult), [hi, mv4, rs4], [hi])
            V(lambda: nc.vector.memset(ybf[:], 0.0), [], [ybf])
            V(lambda: nc.vector.tensor_tensor(out=ybf[0:64, 0:512], in0=hi[0:64].rearrange("p h t -> p (h t)"), in1=ez[0:64], op=ALU.mult),
              [hi, ez], [ybf])

            kst = rot("kst", [128, 128], F32, 2); kmst = rot("kmst", [16, 128], F32, 2)
            vst = rot("vst", [128, 128], F32, 2); vmst = rot("vmst", [16, 128], F32, 2)
            KwT = rot("KwT", [128, 128], BF16, 2); KmT = rot("KmT", [128, 16], BF16, 2)
            VwR = rot("Vw", [128, 2, 65], BF16, 2); VmR = rot("Vm", [16, 2, 65], BF16, 2)
            for t_ in VwR.tiles + VmR.tiles:
                V(lambda t_=t_: nc.vector.memset(t_[:], 1.0), [], [t_])
            Sw = [GB6.tiles[0], GB6.tiles[1]]; Sm = [GB6.tiles[2], GB6.tiles[3]]; Sn = [GB6.tiles[4], GB6.tiles[5]]
            for j in range(16):
                ks_ = kst.next(); km_ = kmst.next(); kw_ = KwT.next(); kmT_ = KmT.next()
                LD(ks_, ks_[:], ck_i[j, 16:144, :]); LD(km_, km_[:], ck_i[j, 0:16, :])
                T(lambda ks_=ks_: nc.tensor.transpose(out=pS[:, 0:128], in_=ks_[:], identity=idf[:]), [ks_, idf], [pS])
                T(lambda km_=km_: nc.tensor.transpose(out=pS[:, 128:144], in_=km_[:], identity=idf[0:16, 0:16]), [km_, idf], [pS])
                A(lambda kw_=kw_: nc.scalar.copy(out=kw_[:], in_=pS[:, 0:128]), [pS], [kw_])
                A(lambda kmT_=kmT_: nc.scalar.copy(out=kmT_[:], in_=pS[:, 128:144]), [pS], [kmT_])
                for kap in range(2):
                    ksl = slice(64 * kap, 64 * kap + 64)
                    qv = aqT[ksl, :, 4 * j:4 * j + 4]
                    T(lambda kap=kap, ksl=ksl, qv=qv, kw_=kw_, j=j: nc.tensor.matmul(Sw[kap][:, 16 * j:16 * j + 16], lhsT=kw_[ksl, :], rhs=qv,
                                                                                  start=True, stop=True), [kw_, aqT], [Sw[kap]])
                    T(lambda kap=kap, ksl=ksl, qv=qv, kmT_=kmT_, j=j: nc.tensor.matmul(Sm[kap][0:16, 16 * j:16 * j + 16], lhsT=kmT_[ksl, :], rhs=qv,
                                                                                    start=True, stop=True), [kmT_, aqT], [Sm[kap]])
                    T(lambda kap=kap, ksl=ksl, qv=qv, akT=akT, j=j: nc.tensor.matmul(Sn[kap][0:64, 16 * j:16 * j + 16], lhsT=akT[ksl, 0:64], rhs=qv,
                                                                                  start=True, stop=True), [akT, aqT], [Sn[kap]])
            Ew = sb("Ew", [128, 2, 256], BF16); Em = sb("Em", [16, 2, 256], BF16); En = sb("En", [64, 2, 256], BF16)
            for kap in range(2):
                A(lambda kap=kap: nc.scalar.activation(out=Ew[:, kap, :], in_=Sw[kap][:, 0:256], func=AF.Exp, scale=ASCALE), [Sw[kap]], [Ew])
                A(lambda kap=kap: nc.scalar.activation(out=Em[:, kap, :], in_=Sm[kap][0:16, 0:256], func=AF.Exp, scale=ASCALE), [Sm[kap]], [Em])
                A(lambda kap=kap: nc.scalar.activation(out=En[:, kap, :], in_=Sn[kap][0:64, 0:256], func=AF.Exp, scale=ASCALE), [Sn[kap]], [En])
                V(lambda kap=kap: nc.vector.tensor_tensor(out=Ew[:, kap, :], in0=Ew[:, kap, :], in1=winm[:], op=ALU.mult), [Ew, winm], [Ew])
                V(lambda kap=kap: nc.vector.tensor_tensor(out=En[:, kap, :], in0=En[:, kap, :], in1=newm[0:64, :], op=ALU.mult), [En, newm], [En])
            osb = sb("osb", [16, 16, 64]); rso = sb("rso", [16, 2, 16])
            yas_t = xsp
            yas = xsp[0:64, 512:1024].rearrange("p (k g d) -> p k g d", k=2, g=4)
            scr_t = nc.dram_tensor("scr", [16, 4, 2, 4, 64], F32)
            scr = scr_t.ap()
            for j in range(16):
                vs_ = vst.next(); vm_ = vmst.next(); Vw = VwR.next(); Vm = VmR.next()
                LD(vs_, vs_[:], cv_i[j, 16:144, :]); LD(vm_, vm_[:], cv_i[j, 0:16, :])
                G(lambda vs_=vs_, Vw=Vw: nc.gpsimd.tensor_copy(out=Vw[:, :, 0:64], in_=vs_[:].rearrange("p (k d) -> p k d", k=2)), [vs_], [Vw])
                G(lambda vm_=vm_, Vm=Vm: nc.gpsimd.tensor_copy(out=Vm[:, :, 0:64], in_=vm_[:].rearrange("p (k d) -> p k d", k=2)), [vm_], [Vm])
                for kap in range(2):
                    po = GB6.tiles[3 * kap + j // 7]
                    oap = po[0:16, (j % 7) * 65:(j % 7) * 65 + 65]
                    cs = slice(16 * j, 16 * j + 16)
                    T(lambda kap=kap, j=j, oap=oap, cs=cs, Vm=Vm: nc.tensor.matmul(oap, lhsT=Em[:, kap, cs], rhs=Vm[:, kap, :], start=True, stop=False),
                      [Em, Vm], [po])
                    T(lambda kap=kap, j=j, oap=oap, cs=cs, Vw=Vw: nc.tensor.matmul(oap, lhsT=Ew[:, kap, cs], rhs=Vw[:, kap, :], start=False, stop=False),
                      [Ew, Vw], [po])
                    T(lambda kap=kap, j=j, oap=oap, cs=cs, va=va: nc.tensor.matmul(oap, lhsT=En[:, kap, cs], rhs=va[0:64, kap, :], start=False, stop=True),
                      [En, va], [po])
            for kap in range(2):
                for b3 in range(3):
                    nj = 7 if b3 < 2 else 2
                    po = GB6.tiles[3 * kap + b3]
                    pv_ = po[0:16, 0:65 * nj].rearrange("p (j e) -> p j e", e=65)
                    V(lambda kap=kap, b3=b3, nj=nj, pv_=pv_: nc.vector.tensor_scalar(
                        out=rso[:, kap, 7 * b3:7 * b3 + nj].unsqueeze(2), in0=pv_[:, :, 64:65], scalar1=esk16[:, kap:kap + 1],
                        scalar2=None, op0=ALU.add), [po, esk16], [rso])
                    V(lambda kap=kap, b3=b3, nj=nj: nc.vector.reciprocal(out=rso[:, kap, 7 * b3:7 * b3 + nj], in_=rso[:, kap, 7 * b3:7 * b3 + nj]),
                      [rso], [rso])
                    V(lambda kap=kap, b3=b3, nj=nj, pv_=pv_: nc.vector.tensor_tensor(
                        out=osb[:, 7 * b3:7 * b3 + nj, :], in0=pv_[:, :, 0:64],
                        in1=rso[:, kap, 7 * b3:7 * b3 + nj].unsqueeze(2).to_broadcast([16, nj, 64]), op=ALU.mult), [po, rso], [osb])
                for g in range(4):
                    P.dma("sp", lambda kap=kap, g=g: nc.sync.dma_start(out=scr[:, :, kap, g, :].rearrange("j l d -> l j d"),
                                                                       in_=osb[4 * g:4 * g + 4, :, :]),
                          reads=[osb], writes=[scr_t], key="S_scr")
                P.dma("sp", lambda kap=kap: nc.sync.dma_start(out=yas[:, kap, :, :],
                                                              in_=scr[:, :, kap, :, :].rearrange("j l g d -> (j l) g d")),
                      reads=[scr_t], writes=[yas_t], key="L_yas")
            V(lambda: nc.vector.tensor_tensor(out=ybf[0:64, 512:1024], in0=xsp[0:64, 512:1024], in1=eaz[0:64], op=ALU.mult),
              [xsp, eaz], [ybf])

            transpose8(ybf, yT)
            pm_ = [GB.next(), GB.next()]
            for hf in range(2):
                T(lambda hf=hf: nc.tensor.matmul(pm_[hf][:], lhsT=ones1b[:, 0:128], rhs=bob[:, hf * 512:(hf + 1) * 512], start=True, stop=False),
                  [ones1b, bob], [pm_[hf]])
                for k in range(8):
                    T(lambda k=k, hf=hf: nc.tensor.matmul(pm_[hf][:], lhsT=yT[:, k, :], rhs=wo_bf[:, k, hf * 512:(hf + 1) * 512],
                                                          start=False, stop=(k == 7)), [yT, wo_bf], [pm_[hf]])
            V(lambda hp=hp: nc.vector.tensor_tensor(out=hp[:], in0=hp[:], in1=ln0g[:], op=ALU.mult), [hp, ln0g], [hp])
            for hf in range(2):
                V(lambda hf=hf, hp=hp: nc.vector.tensor_tensor(out=hp[:, hf * 512:(hf + 1) * 512], in0=hp[:, hf * 512:(hf + 1) * 512],
                                                               in1=pm_[hf][:], op=ALU.add),
                  [hp, pm_[hf]], [hp])
            layer_norm(hp, hp, lng, lnb)
            ST(ys_o, hp, hp[0:64, :])
            for (ci_, co_, off) in ((ck_i, sk_o, 0), (cv_i, sv_o, 128)):
                P.dma("sp", lambda ci_=ci_, co_=co_: nc.sync.dma_start(out=co_[:, 0:16, :], in_=ci_[:, 0:16, :]), key="X_c0")
                P.dma("sp", lambda ci_=ci_, co_=co_: nc.sync.dma_start(out=co_[:, 16:140, :], in_=ci_[:, 20:144, :]), key="X_c1")
                for j in range(16):
                    ST(co_[j, 140:144, :], kavf, kavf[4 * j:4 * j + 4, off:off + 128])

        GB = Rot(GB6.tiles[0:5])
        for s in range(NSLOT):
            kind = "meta" if s == 0 else ("prelast" if s == NSLOT - 1 else "pre")
            chunk(kind, xpre[s * 128:(s + 1) * 128, :], slot=s)
        bias_rows(N1, NIN)
        for c in range(NOWN):
            chunk("own", xown[c * 128:(c + 1) * 128, :], c=c)
        if DO_SAMPLE:
            sample_program()

        P.emit()
    return nc


def _consts():
    s = np.arange(128)[:, None]
    t = np.arange(128)[None, :]
    tri = (s <= t).astype(np.float32)
    prv = (s > t).astype(np.float32)
    return np.eye(128, dtype=np.float32), np.tile(tri, (1, 4)), np.tile(prv, (1, 4))


def prep_prompt(inputs, NOWN):
    NPRE = 3 * NOWN
    NSLOT = 1 + NPRE
    xp = np.asarray(inputs["x_prompt"], np.float32)
    meta = np.asarray(inputs["meta_tokens"], np.float32)
    cid, ctri, cprev = _consts()
    vecs = np.stack([np.asarray(inputs[k], np.float32).reshape(-1) for k in ("ln0_g", "ln0_b")] +
                    [np.asarray(inputs[k], np.float32).reshape(-1) for k in ("ln_g", "ln_b")])
    common = dict(cid=cid, ctri=ctri, cprev=cprev,
                  w_in=np.ascontiguousarray(np.concatenate([np.asarray(inputs["w_in"], np.float32)[0][:, a:b] for a, b in COL_PERM], 1)),
                  b_in=np.ascontiguousarray(np.concatenate([np.asarray(inputs["b_in"], np.float32).reshape(NIN)[a:b] for a, b in COL_PERM]).reshape(1, NIN)),
                  w_out=np.ascontiguousarray(np.asarray(inputs["w_out"], np.float32)[0]),
                  vecs=vecs, mng=np.asarray(inputs["m_norm_g"], np.float32).reshape(1, 512),
                  sinks=np.asarray(inputs["a_sinks"], np.float32).reshape(1, 8))
    maps = []
    for core in range(8):
        b, r = core // 4, core % 4
        xown = np.ascontiguousarray(xp[b, r * NOWN * 128:(r + 1) * NOWN * 128])
        xpre = np.zeros((NSLOT * 128, D), np.float32)
        valid = np.zeros((NSLOT * 128,), np.float32)
        xpre[0:NMETA] = meta
        valid[0:NMETA] = 1.0
        npre = r * NOWN
        if npre:
            xpre[(NSLOT - npre) * 128:] = xp[b, 0:npre * 128]
            valid[(NSLOT - npre) * 128:] = 1.0
        rvalid = np.tile(valid[None, :], (4, 1))
        rneg = np.where(rvalid > 0, 0.0, NEG).astype(np.float32)
        pm1 = cprev if r > 0 else np.zeros_like(cprev)
        m = dict(common)
        m.update(xown=xown, xpre=xpre, rvalid=rvalid, rneg=rneg, pm1=pm1)
        maps.append(m)
    return maps


def prep_sample(inputs, maps):
    xs = np.asarray(inputs["x_sample"], np.float32)
    ck = np.asarray(inputs["cache_swa_k"], np.float32)[0].reshape(128, 144, 128)
    cv = np.asarray(inputs["cache_swa_v"], np.float32)[0].reshape(128, 144, 128)
    sc = np.asarray(inputs["state_mlstm_c"], np.float32)[0]
    sn = np.asarray(inputs["state_mlstm_n"], np.float32)[0]
    sm = np.asarray(inputs["state_mlstm_m"], np.float32)[0]
    sinks = np.asarray(inputs["a_sinks"], np.float32).reshape(8)
    s_ = np.arange(128)
    j_ = np.arange(16)
    cseqcol = ((s_[:, None] // 4 == j_[None, :]) & (s_[:, None] < 64)).astype(np.float32)
    t_ = np.arange(64)
    cseqbc = np.broadcast_to((t_[None, :] // 4 == j_[:, None]).astype(np.float32).reshape(1, 1024), (128, 1024)).copy()
    t128 = np.arange(128)
    blk = ((s_[:, None] // 4 == t128[None, :] // 4) & (s_[:, None] <= t128[None, :]) & (s_[:, None] < 64) & (t128[None, :] < 64))
    cblk4 = np.tile(blk.astype(np.float32), (1, 4))
    l_ = np.arange(4)
    win = (s_[:, None] > l_[None, :]).astype(np.float32)
    cwin = np.broadcast_to(win[:, None, None, :], (128, 16, 4, 4)).reshape(128, 256).copy()
    new = ((s_[:, None, None] // 4 == j_[None, :, None]) & (s_[:, None, None] % 4 <= l_[None, None, :]) & (s_[:, None, None] < 64))
    cnew = np.broadcast_to(new[:, :, None, :], (128, 16, 4, 4)).astype(np.float32).reshape(128, 256).copy()
    sinks16 = np.zeros((16, 2), np.float32)
    for kap in range(2):
        for g in range(4):
            sinks16[4 * g:4 * g + 4, kap] = sinks[4 * kap + g]
    for core in range(8):
        sl = slice(16 * core, 16 * core + 16)
        maps[core].update(
            xs=np.ascontiguousarray(xs[sl].reshape(64, D)), ck=np.ascontiguousarray(ck[sl]), cv=np.ascontiguousarray(cv[sl]),
            sc=np.ascontiguousarray(sc[sl]), sn=np.ascontiguousarray(sn[sl].reshape(64, 128)),
            sm=np.ascontiguousarray(sm[sl].T), cseqcol=cseqcol, cblk4=cblk4, cwin=cwin, cnew=cnew, sinks16=sinks16)
    return maps


_NC_CACHE = {}


def kernel(**inputs):
    NOWN = np.asarray(inputs["x_prompt"]).shape[1] // 512
    B = 2
    maps = prep_sample(inputs, prep_prompt(inputs, NOWN))
    nc = build(NOWN, DO_SAMPLE=True)
    res = run_bass_kernel_spmd(nc, maps, core_ids=list(range(8)))
    R = res.results
    S = NOWN * 512
    y = np.zeros((B, S, D), np.float32)
    for core in range(8):
        b, r = core // 4, core % 4
        y[b, r * NOWN * 128:(r + 1) * NOWN * 128] = R[core]["y"]
    last = [3, 7]
    pk = np.stack([R[c]["pk"].reshape(144, 2, 64) for c in last])[None]
    pv = np.stack([R[c]["pv"].reshape(144, 2, 64) for c in last])[None]
    pc = np.stack([R[c]["pc"] for c in last])[None]
    pn = np.stack([R[c]["pn"] for c in last])[None]
    pm = np.stack([R[c]["pm"].reshape(4) for c in last])[None]
    ys = np.concatenate([R[c]["ys"].reshape(16, 4, D) for c in range(8)], 0)
    sk = np.concatenate([R[c]["sk"].reshape(16, 144, 2, 64) for c in range(8)], 0)[None]
    sv = np.concatenate([R[c]["sv"].reshape(16, 144, 2, 64) for c in range(8)], 0)[None]
    sco = np.concatenate([R[c]["sco"] for c in range(8)], 0)[None]
    sno = np.concatenate([R[c]["sno"].reshape(16, 4, 128) for c in range(8)], 0)[None]
    smo = np.concatenate([R[c]["smo"].T for c in range(8)], 0)[None]
    f = lambda a: np.ascontiguousarray(a, dtype=np.float32)
    return tuple(f(a) for a in (y, ys, pk, pv, pc, pn, pm, sk, sv, sco, sno, smo))
```

```python
import contextlib
import numpy as np
import concourse.bass as bass
import concourse.mybir as mybir
from concourse.bass_utils import run_bass_kernel_spmd

F32 = mybir.dt.float32
BF16 = mybir.dt.bfloat16
ALU = mybir.AluOpType
AF = mybir.ActivationFunctionType

COMPUTE = ("pe", "act", "dve", "pool")
STRICT_SAME_ENGINE = False

D = 1024
NIN = 3848
NMETA = 16
O_MK, O_MV, O_AK, O_AV, O_MI, O_MF, O_MQ, O_MO, O_MZ, O_AQ, O_AZ = 0, 512, 1024, 1152, 1280, 1284, 1288, 1800, 2312, 2824, 3336
N1 = 1288
N2 = NIN - N1
COL_PERM = [(512, 1024), (1024, 1536), (3080, 3208), (3208, 3336), (2560, 2568), (0, 512), (1536, 2048), (2048, 2560), (2568, 3080), (3336, 3848)]
LN_EPS = 1e-5
DN_ALPHA = 2.0 ** 0.25
KSCALE = 128.0 ** -0.5
ASCALE = 64.0 ** -0.5
NEG = -1.0e30


class Op:
    __slots__ = ("eng", "fn", "reads", "writes", "dma", "key", "idx", "deps", "marked", "kcount", "alld", "fin", "lat")

    def __init__(self, eng, fn, reads, writes, dma, key):
        self.eng, self.fn, self.reads, self.writes, self.dma, self.key = eng, fn, reads, writes, dma, key
        self.deps = []
        self.marked = False
        self.kcount = 0


class Prog:
    def __init__(self, nc):
        self.nc = nc
        self.ops = []
        self.nkeys = {}
        self.filler = None
        self.fill_frac = 0.7
        self.fill_on = lambda o: True
        self.excl = set()

    def op(self, eng, fn, reads=(), writes=()):
        writes = tuple(writes) + tuple(r for r in reads if id(r) in self.excl and not any(r is w for w in writes))
        o = Op(eng, fn, tuple(reads), tuple(writes), False, None)
        self.ops.append(o)
        return o

    def dma(self, eng, fn, reads=(), writes=(), key=None, lat=3.0):
        o = Op(eng, fn, tuple(reads), tuple(writes), True, key)
        o.lat = lat
        self.nkeys[key] = self.nkeys.get(key, 0) + 1
        o.kcount = self.nkeys[key]
        self.ops.append(o)
        return o

    COST = {"pe": 0.25, "act": 0.45, "dve": 0.5, "pool": 0.9, "sp": 0.05}

    def _schedule(self, window=96):
        self._analyze(mark=False)
        per = {}
        for o in self.ops:
            per.setdefault(o.eng, []).append(o)
            o.fin = None
        free = {e: 0.0 for e in per}
        order = []
        nleft = len(self.ops)
        while nleft:
            best = None
            for e, lst in per.items():
                cnt = 0
                for o in lst:
                    if o.fin is not None:
                        continue
                    cnt += 1
                    if cnt > window:
                        break
                    rdy = 0.0
                    ok = True
                    for p in o.alld:
                        if p.fin is None:
                            ok = False
                            break
                        f = p.fin + (0.0 if (p.eng == e and not p.dma) else 0.25)
                        if f > rdy:
                            rdy = f
                    if not ok:
                        continue
                    st = max(free[e], rdy)
                    if best is None or (st, o.idx) < (best[0], best[1].idx):
                        best = (st, o)
                    if st <= free[e]:
                        break
            st, o = best
            if self.filler is not None and any(t is self.filler[1] for t in o.reads + o.writes):
                self.filler = None
            if self.filler is not None and o.eng == "pe" and self.fill_on(o):
                gap = st - free["pe"]
                if gap > 0.6:
                    nf = min(int(gap * self.fill_frac / 0.25), 24)
                    for _ in range(nf):
                        f = Op("pe", self.filler[0], (), (self.filler[1],), False, None)
                        f.idx = -1
                        f.alld = []
                        f.fin = free["pe"] + 0.25
                        free["pe"] = f.fin
                        order.append(f)
                    st = max(st, free["pe"])
            c = self.COST[o.eng]
            free[o.eng] = st + c
            o.fin = st + c + (o.lat if o.dma else 0.0)
            order.append(o)
            nleft -= 1
            lst = per[o.eng]
            while lst and lst[0].fin is not None:
                lst.pop(0)
        self.ops = order
        self.nkeys = {}
        for o in self.ops:
            o.marked = False
            if o.dma:
                self.nkeys[o.key] = self.nkeys.get(o.key, 0) + 1
                o.kcount = self.nkeys[o.key]
        self.sim_time = max(free.values())

    def _analyze(self, mark=True):
        wr, rd = {}, {}
        SERIAL = False

        def add(lst, o):
            if not o.dma:
                for i, x in enumerate(lst):
                    if (not x.dma) and x.eng == o.eng:
                        lst[i] = o
                        return
            lst.append(o)

        for idx, o in enumerate(self.ops):
            o.idx = idx
            deps = {}
            for r in o.reads:
                for p in wr.get(id(r), ()):
                    deps[p.idx] = (p, "raw")
            for w in o.writes:
                for p in rd.get(id(w), ()):
                    deps.setdefault(p.idx, (p, "war"))
                for p in wr.get(id(w), ()):
                    deps.setdefault(p.idx, (p, "waw"))
            wids = set(id(w) for w in o.writes)
            for w in o.writes:
                if rd.get(id(w)):
                    wr[id(w)] = [o]
                    rd[id(w)] = []
                else:
                    add(wr.setdefault(id(w), []), o)
            for r in o.reads:
                if id(r) not in wids:
                    add(rd.setdefault(id(r), []), o)
            if SERIAL and idx > 0:
                pp = self.ops[idx - 1]
                deps.setdefault(pp.idx, (pp, "raw"))
            o.deps = []
            o.alld = []
            for p, kind in deps.values():
                if p is o:
                    continue
                o.alld.append(p)
                if (not p.dma) and (not o.dma) and p.eng == o.eng:
                    if p.eng == "pe" or (kind != "raw" and not STRICT_SAME_ENGINE):
                        continue
                o.deps.append(p)
                if mark:
                    p.marked = True

    def emit(self, final_wait_eng="sp", schedule=True):
        nc = self.nc
        if schedule:
            self._schedule()
        self._analyze()
        with contextlib.ExitStack() as es:
            esem = {e: es.enter_context(nc.semaphore("s_" + e)) for e in COMPUTE}
            ksem = {k: es.enter_context(nc.semaphore("k_%s" % (str(k),))) for k in self.nkeys}
            cnt = {e: 0 for e in COMPUTE}
            val = {}
            for o in self.ops:
                if o.dma:
                    val[o.idx] = (ksem[o.key], 16 * o.kcount)
                elif o.marked:
                    cnt[o.eng] += 1
                    val[o.idx] = (esem[o.eng], cnt[o.eng])
            block = es.enter_context(nc.Block())

            def run_engine(ename):
                def body(eng):
                    waited = {}
                    for o in self.ops:
                        if o.eng != ename:
                            continue
                        for p in o.deps:
                            sem, v = val[p.idx]
                            if waited.get(id(sem), 0) >= v:
                                continue
                            waited[id(sem)] = v
                            eng.wait_ge(sem, v)
                        ins = o.fn()
                        if o.dma:
                            ins.then_inc(ksem[o.key], 16)
                        elif o.marked:
                            ins.then_inc(esem[o.eng], 1)
                    if ename == final_wait_eng:
                        for k, n in self.nkeys.items():
                            eng.wait_ge(ksem[k], 16 * n)
                return body

            block.sync(run_engine("sp"))
            block.tensor(run_engine("pe"))
            block.scalar(run_engine("act"))
            block.vector(run_engine("dve"))
            block.gpsimd(run_engine("pool"))


class Rot:
    def __init__(self, tiles):
        self.tiles = tiles
        self.i = -1

    def next(self):
        self.i = (self.i + 1) % len(self.tiles)
        return self.tiles[self.i]

    def cur(self):
        return self.tiles[self.i]

    def prev(self):
        return self.tiles[(self.i - 1) % len(self.tiles)]


def build(NOWN, DO_SAMPLE=True):
    NPRE = 3 * NOWN
    NSLOT = 1 + NPRE
    nc = bass.Bass("TRN2", target_bir_lowering=False)

    def din(name, shape):
        return nc.dram_tensor(name, list(shape), F32, kind="ExternalInput").ap()

    def dout(name, shape):
        return nc.dram_tensor(name, list(shape), F32, kind="ExternalOutput").ap()

    xown = din("xown", [NOWN * 128, D])
    xpre = din("xpre", [NSLOT * 128, D])
    rvalid = din("rvalid", [4, NSLOT * 128])
    rneg = din("rneg", [4, NSLOT * 128])
    pm1 = din("pm1", [128, 512])
    cid = din("cid", [128, 128])
    ctri = din("ctri", [128, 512])
    cprev = din("cprev", [128, 512])
    w_in = din("w_in", [D, NIN])
    b_in = din("b_in", [1, NIN])
    w_out = din("w_out", [D, D])
    vecs = din("vecs", [4, D])
    mng = din("mng", [1, 512])
    sinks = din("sinks", [1, 8])

    y_o = dout("y", [NOWN * 128, D])
    pk_o = dout("pk", [144, 128])
    pv_o = dout("pv", [144, 128])
    pc_o = dout("pc", [4, 128, 128])
    pn_o = dout("pn", [4, 128])
    pm_o = dout("pm", [4, 1])

    if DO_SAMPLE:
        xs = din("xs", [64, D])
        ck_i = din("ck", [16, 144, 128])
        cv_i = din("cv", [16, 144, 128])
        sc_i = din("sc", [16, 4, 128, 128])
        sn_i = din("sn", [64, 128])
        sm_i = din("sm", [4, 16])
        cseqcol = din("cseqcol", [128, 16])
        cblk4 = din("cblk4", [128, 512])
        cwin = din("cwin", [128, 256])
        cnew = din("cnew", [128, 256])
        sinks16 = din("sinks16", [16, 2])
        ys_o = dout("ys", [64, D])
        sk_o = dout("sk", [16, 144, 128])
        sv_o = dout("sv", [16, 144, 128])
        sc_o = dout("sco", [16, 4, 128, 128])
        sn_o = dout("sno", [64, 128])
        sm_o = dout("smo", [4, 16])

    P = Prog(nc)
    es = contextlib.ExitStack()
    KDBG = False
    dbg_out = {}

    def DBGDUMP(name, t, ap, shape):
        if not KDBG:
            return
        d = nc.dram_tensor("dbg_" + name, list(shape), t.dtype if hasattr(t, "dtype") else F32, kind="ExternalOutput").ap()
        dbg_out[name] = d
        P.dma("sp", lambda: nc.sync.dma_start(out=d, in_=ap), reads=[t], key="D_" + name)

    def sb(name, shape, dt=F32):
        return es.enter_context(nc.sbuf_tensor(name, list(shape), dt))

    def psum(name, shape, dt=F32):
        t = es.enter_context(nc.psum_tensor(name, list(shape), dt))
        P.excl.add(id(t))
        return t

    def rot(name, shape, dt=F32, n=2):
        return Rot([sb("%s%d" % (name, i), shape, dt) for i in range(n)])

    V = lambda fn, r=(), w=(): P.op("dve", fn, r, w)
    A = lambda fn, r=(), w=(): P.op("act", fn, r, w)
    G = lambda fn, r=(), w=(): P.op("pool", fn, r, w)
    T = lambda fn, r=(), w=(): P.op("pe", fn, r, w)
    kctr = [0]

    def LD(out_t, out_ap, in_ap, lat=3.0, **kw):
        P.dma("sp", lambda: nc.sync.dma_start(out=out_ap, in_=in_ap, **kw), writes=[out_t], key="L_" + out_t.name, lat=lat)

    def ST(out_ap, in_t, in_ap, **kw):
        P.dma("sp", lambda: nc.sync.dma_start(out=out_ap, in_=in_ap, **kw), reads=[in_t], key="S_" + in_t.name)

    with es:
        GB = Rot([psum("g%d" % i, [128, 512]) for i in range(6)])
        GB6 = GB
        pDum = GB.tiles[5]
        WARM_PRE, WARM_OWN = 18, 0


        def warm(n):
            for _ in range(n):
                T(lambda: nc.tensor.matmul(pDum[:], lhsT=idb[:], rhs=tri4[:], start=True, stop=True), [], [pDum])
        pT = psum("pT", [128, 1024], BF16)
        pS = psum("pS", [128, 512])

        wst = rot("wst", [128, D], F32, 2)
        xR = rot("xt", [128, D], F32, 2)
        hpR = rot("hp", [128, D], F32, 2)
        cstage = wst.tiles[0]
        idf = sb("idf", [128, 128]); idb = sb("idb", [128, 128], BF16)
        ones4 = sb("ones4", [4, 128]); ones1b = sb("ones1b", [128, 128], BF16)
        onescol = sb("onescol", [128, 1], BF16); negid4 = sb("negid4", [4, 4]); zer4 = sb("zer4", [4, 128])
        tri4 = sb("tri4", [128, 512], BF16); prev4 = sb("prev4", [128, 512], BF16); pm1b = sb("pm1b", [128, 512], BF16)
        srow = sb("srow", [1, 8])
        binb1 = sb("binb1", [128, N1], BF16); binb2 = sb("binb2", [128, N2], BF16)
        gcol = sb("gcol", [128, 8]); b0col = sb("b0col", [128, 8]); b0g = sb("b0g", [128, 8], BF16); growf = sb("growf", [1, 8])
        ln0g = sb("ln0g", [128, D]); lng = sb("lng", [128, D]); lnb = sb("lnb", [128, D]); nmrR = rot("nmr", [128, 1], F32, 2); nmr = nmrR.next()
        esink = sb("esink", [128, 8]); biasg = sb("biasg", [128, 8]); mngcol = sb("mngcol", [128, 4])
        bob = sb("bob", [128, D], BF16)
        w1 = sb("w1", [128, 8, N1], BF16); w2 = sb("w2", [128, 8, N2], BF16)

        def wsl(k, c0, n):
            if c0 + n <= N1:
                return w1, w1[:, k, c0:c0 + n]
            assert c0 >= N1
            return w2, w2[:, k, c0 - N1:c0 - N1 + n]

        def bsl(c0, n):
            if c0 + n <= N1:
                return binb1, binb1[:, c0:c0 + n]
            assert c0 >= N1
            return binb2, binb2[:, c0 - N1:c0 - N1 + n]
        wo_bf = sb("wo_bf", [128, 8, D], BF16)
        NWQ = 4
        WH = NIN // NWQ
        rvR = rot("rvt", [4, 128], F32, 2); rnR = rot("rnt", [4, 128], F32, 2)

        LD(idf, idf[:], cid)
        A(lambda: nc.scalar.copy(out=idb[:], in_=idf[:]), [idf], [idb])
        G(lambda: nc.gpsimd.memset(ones4[:], 1.0), [], [ones4])
        G(lambda: nc.gpsimd.memset(ones1b[:], 0.0), [], [ones1b])
        G(lambda: nc.gpsimd.memset(ones1b[0:1, :], 1.0), [], [ones1b])
        G(lambda: nc.gpsimd.memset(binb1[:], 0.0), [], [binb1])
        G(lambda: nc.gpsimd.memset(binb2[:], 0.0), [], [binb2])
        LD(gcol, gcol[:], vecs[0:1, :].rearrange("o (k p) -> p (o k)", p=128), allow_slow_non_contiguous=True)
        LD(b0col, b0col[:], vecs[1:2, :].rearrange("o (k p) -> p (o k)", p=128), allow_slow_non_contiguous=True)
        rgc = sb("rgc", [128, 8])
        V(lambda: nc.vector.reciprocal(out=rgc[:], in_=gcol[:]), [gcol], [rgc])
        V(lambda: nc.vector.tensor_tensor(out=b0g[:], in0=b0col[:], in1=rgc[:], op=ALU.mult), [b0col, rgc], [b0g])
        G(lambda: nc.gpsimd.memset(onescol[:], 1.0), [], [onescol])
        G(lambda: nc.gpsimd.memset(zer4[:], 0.0), [], [zer4])
        V(lambda: nc.vector.tensor_scalar(out=negid4[:], in0=idf[0:4, 0:4], scalar1=-1.0, scalar2=None, op0=ALU.mult), [idf], [negid4])
        cast_eng = [("pool", lambda o, i: nc.gpsimd.tensor_copy(out=o, in_=i)),
                    ("dve", lambda o, i: nc.vector.tensor_copy(out=o, in_=i)),
                    ("act", lambda o, i: nc.scalar.copy(out=o, in_=i))]
        scl_eng = [("pool", lambda o, i, sc: nc.gpsimd.tensor_scalar(out=o, in0=i, scalar1=sc, scalar2=None, op0=ALU.mult)),
                   ("dve", lambda o, i, sc: nc.vector.tensor_scalar(out=o, in0=i, scalar1=sc, scalar2=None, op0=ALU.mult)),
                   ("act", lambda o, i, sc: nc.scalar.mul(out=o, in_=i, mul=sc))]
        ci = 0

        def WLD(st, out_ap, in_ap, dq="sp"):
            if dq == "sp":
                P.dma("sp", lambda: nc.sync.dma_start(out=out_ap, in_=in_ap), writes=[st], key="L_" + st.name, lat=14.0)
            else:
                P.dma("pool", lambda: nc.gpsimd.dma_start(out=out_ap, in_=in_ap), writes=[st], key="L_" + st.name, lat=20.0)

        def load_w(wt, c_lo, c_hi, piece, engs, stg=None, dq="sp"):
            nonlocal ci
            for k in range(8):
                c = c_lo
                while c < c_hi:
                    n_ = min(piece, c_hi - c)
                    st = (stg or wst).next()
                    WLD(st, st[:, 0:n_], w_in[k * 128:(k + 1) * 128, c:c + n_], dq)
                    en, f = engs[ci % len(engs)]; ci += 1
                    tl, ap = wsl(k, c, n_)
                    P.op(en, lambda f=f, st=st, ap=ap, n_=n_, k=k: f(ap, st[:, 0:n_], gcol[:, k:k + 1]), [st, gcol], [tl])
                    c += n_

        def bias_rows(c_lo, c_hi):
            c = c_lo
            while c < c_hi:
                n_ = min(512, c_hi - c)
                st = wst.next()
                WLD(st, st[0:1, 0:n_], b_in[:, c:c + n_])
                pb = GB.next()
                for k in range(8):
                    tl, ap = wsl(k, c, n_)
                    T(lambda k=k, pb=pb, ap=ap, n_=n_: nc.tensor.matmul(pb[0:1, 0:n_], lhsT=b0g[:, k:k + 1], rhs=ap, start=(k == 0), stop=(k == 7)),
                      [b0g, tl], [pb])
                bt, bap = bsl(c, n_)
                V(lambda pb=pb, st=st, bap=bap, n_=n_: nc.vector.tensor_tensor(out=bap[0:1, :], in0=pb[0:1, 0:n_], in1=st[0:1, 0:n_], op=ALU.add),
                  [pb, st], [bt])
                if c <= O_MI and O_MI + 8 <= c + n_:
                    o_ = O_MI - c
                    V(lambda pb=pb, st=st, o_=o_: nc.vector.tensor_tensor(out=growf[:], in0=pb[0:1, o_:o_ + 8], in1=st[0:1, o_:o_ + 8], op=ALU.add),
                      [pb, st], [growf])
                c += n_

        load_w(w1, 0, N1, 644, scl_eng[1:3], stg=Rot(wst.tiles + xR.tiles + hpR.tiles))
        for (src, dst) in ((ctri, tri4), (cprev, prev4), (pm1, pm1b)):
            LD(cstage, cstage[:, 0:512], src)
            V(lambda dst=dst: nc.vector.tensor_copy(out=dst[:], in_=cstage[:, 0:512]), [cstage], [dst])
        LD(srow, srow[:], sinks)
        ones1f = sb("ones1f", [1, 128])
        G(lambda: nc.gpsimd.memset(ones1f[:], 1.0), [], [ones1f])

        def bcast_row(dst, dst_ap, row_t, row_ap, n, func=None):
            pb = GB.next()
            T(lambda pb=pb: nc.tensor.matmul(pb[:, 0:n], lhsT=ones1f[:], rhs=row_ap, start=True, stop=True), [ones1f, row_t], [pb])
            if func is None:
                V(lambda pb=pb: nc.vector.tensor_copy(out=dst_ap, in_=pb[:, 0:n]), [pb], [dst])
            else:
                A(lambda pb=pb: nc.scalar.activation(out=dst_ap, in_=pb[:, 0:n], func=func), [pb], [dst])

        G(lambda: nc.gpsimd.memset(bob[:], 0.0), [], [bob])
        for i, dst in enumerate((ln0g, None, lng, lnb)):
            for hf in range(2):
                rs_ = cstage
                LD(rs_, rs_[0:1, 0:512], vecs[i:i + 1, hf * 512:(hf + 1) * 512])
                if dst is None:
                    A(lambda hf=hf: nc.scalar.mul(out=bob[0:1, hf * 512:(hf + 1) * 512], in_=cstage[0:1, 0:512], mul=DN_ALPHA), [cstage], [bob])
                else:
                    bcast_row(dst, dst[:, hf * 512:(hf + 1) * 512], rs_, rs_[0:1, 0:512], 512)
        A(lambda: nc.scalar.mul(out=ln0g[:], in_=ln0g[:], mul=DN_ALPHA), [ln0g], [ln0g])
        rs_ = cstage
        LD(mngcol, mngcol[:], mng.rearrange("o (k p) -> p (o k)", p=128), allow_slow_non_contiguous=True)
        bcast_row(esink, esink[:], srow, srow[:], 8, func=AF.Exp)

        bias_rows(0, N1)
        load_w(w2, N1, NIN, 640, scl_eng[1:3], dq="pool")
        for k in range(8):
            st = wst.next()
            WLD(st, st[:, 0:D], w_out[k * 128:(k + 1) * 128, :], "pool")
            if k < 4:
                en, f = scl_eng[1 + ci % 2]; ci += 1
                P.op(en, lambda f=f, st=st, k=k: f(wo_bf[:, k, :], st[:, 0:D], mngcol[:, k:k + 1]), [st, mngcol], [wo_bf])
            else:
                en, f = cast_eng[1 + ci % 2]; ci += 1
                P.op(en, lambda f=f, st=st, k=k: f(wo_bf[:, k, :], st[:, 0:D]), [st], [wo_bf])
        bcast_row(biasg, biasg[:], growf, growf[:], 8)

        Cst = sb("Cst", [128, 4, 128]); nst = sb("nst", [128, 4])
        Cbf = sb("Cbf", [128, 4, 128], BF16); nbf = sb("nbf", [128, 4], BF16)
        V(lambda: nc.vector.memset(Cst[:], 0.0), [], [Cst])
        V(lambda: nc.vector.memset(nst[:], 0.0), [], [nst])
        V(lambda: nc.vector.memset(Cbf[:], 0.0), [], [Cbf])
        V(lambda: nc.vector.memset(nbf[:], 0.0), [], [nbf])
        bnegR = rot("bneg", [4, 128], F32, 2)
        UR = rot("Urow", [4, 128], F32, 2)
        UendR = rot("Uend", [128, 4], F32, 2)
        for t in bnegR.tiles + UR.tiles + UendR.tiles:
            V(lambda t=t: nc.vector.memset(t[:], 0.0), [], [t])
        bnegR.next(); UR.next(); UendR.next()

        st6R = rot("st6", [128, 2, 6], F32, 2); st6 = st6R.next(); mvR = rot("mv", [128, 2], F32, 2); mv = mvR.next(); rstdR = rot("rstd", [128, 1], F32, 2); rstd = rstdR.next()
        hbR = rot("hb", [128, D], BF16, 2); hb = hbR.next()
        xTR = rot("xT", [128, 8, 128], BF16, 2)
        ktmR = rot("ktm", [128, 4, 128], BF16, 2); ktm = ktmR.next(); vbfR = rot("vbf", [128, 4, 128], BF16, 2); vbf = vbfR.next()
        kwR = rot("kw", [128, 4, 128], BF16, 2); kw = kwR.next()
        gtR = rot("gt", [128, 8], F32, 2); gt = gtR.next(); speR = rot("spe", [4, 128], F32, 2); spe = speR.next(); sprR = rot("spr", [4, 128], F32, 2); spr = sprR.next(); spmR = rot("spm", [4, 128], F32, 2); spm = spmR.next()
        limR = rot("lim", [4, 128], F32, 2); lim = limR.next(); u_rR = rot("u_r", [4, 128], F32, 2); u_r = u_rR.next(); m_rR = rot("m_r", [4, 128], F32, 2); m_r = m_rR.next()
        colsR = rot("cols", [128, 12], F32, 2); cols = colsR.next(); diagUR = rot("diagU", [4, 4], F32, 2); diagU = diagUR.next(); exinR = rot("exin", [128, 16], F32, 2); exin = exinR.next(); exR = rot("ex", [128, 16], F32, 2); ex = exR.next()
        akTR = rot("akT", [128, 128], BF16, 3); vaR = rot("va", [128, 2, 65], BF16, 3)
        akTm = sb("akTm", [128, 16], BF16); vam = sb("vam", [16, 2, 65], BF16)
        kavf = sb("kavf", [128, 256])
        for t in vaR.tiles + [vam]:
            V(lambda t=t: nc.vector.memset(t[:], 1.0), [], [t])
        eo = sb("eo", [128, 512]); ez = sb("ez", [128, 512]); eaz = sb("eaz", [128, 512])
        qT = sb("qT", [128, 4, 128], BF16); kT = sb("kT", [128, 4, 128], BF16); aqT = sb("aqT", [128, 4, 128], BF16)
        bd = sb("bd", [4, 4, 128]); Wt = sb("Wt", [128, 4, 128], BF16); Wm = sb("Wm", [128, 4, 128], BF16)
        PTt = sb("PTt", [128, 4, 128], BF16); hi = sb("hi", [128, 4, 128]); hn = hi
        d1 = sb("d1", [128, 4]); d2 = sb("d2", [128, 4]); rden = sb("rden", [128, 4])
        so = eo; hg = hi; st4 = sb("st4", [128, 4, 6]); mv4 = sb("mv4", [128, 4, 2])
        rs4 = sb("rs4", [128, 4]); sz = ez; ybf = sb("ybf", [128, D], BF16)
        Eown = sb("Eown", [128, 512], BF16); Eprev = sb("Eprev", [128, 512], BF16); Emeta = sb("Emeta", [16, 512], BF16)
        Eown2 = Eown; Eprev2 = Eprev
        dsum = sb("dsum", [128, 4]); rsa = sb("rsa", [128, 4]); ya = sb("ya", [128, 4, 64]); saz = eaz
        yT = sb("yT", [128, 8, 128], BF16)

        def layer_norm(src, dst_f, gbc, bbc, dst_b=None):
            for hf in range(2):
                V(lambda hf=hf, st6=st6: nc.vector.bn_stats(out=st6[:, hf, :], in_=src[:, hf * 512:(hf + 1) * 512]), [src], [st6])
            V(lambda st6=st6, mv=mv: nc.vector.bn_aggr(out=mv[:], in_=st6[:]), [st6], [mv])
            A(lambda mv=mv, rstd=rstd: nc.scalar.activation(out=rstd[:], in_=mv[:, 1:2], func=AF.Ln, bias=LN_EPS), [mv], [rstd])
            A(lambda rstd=rstd: nc.scalar.activation(out=rstd[:], in_=rstd[:], func=AF.Exp, scale=-0.5), [rstd], [rstd])
            V(lambda mv=mv, rstd=rstd, nmr=nmr: nc.vector.tensor_scalar(out=nmr[:], in0=mv[:, 0:1], scalar1=rstd[:, 0:1], scalar2=-1.0, op0=ALU.mult, op1=ALU.mult),
              [mv, rstd], [nmr])
            A(lambda rstd=rstd, nmr=nmr: nc.scalar.activation(out=dst_f[:], in_=src[:], func=AF.Identity, scale=rstd[:, 0:1], bias=nmr[:, 0:1]),
              [src, rstd, nmr], [dst_f])
            V(lambda: nc.vector.tensor_tensor(out=dst_f[:], in0=dst_f[:], in1=gbc[:], op=ALU.mult), [dst_f, gbc], [dst_f])
            V(lambda: nc.vector.tensor_tensor(out=dst_f[:], in0=dst_f[:], in1=bbc[:], op=ALU.add), [dst_f, bbc], [dst_f])
            if dst_b is not None:
                A(lambda: nc.scalar.copy(out=dst_b[:], in_=dst_f[:]), [dst_f], [dst_b])

        def ln0(src, dst_b, hp_f=None):
            for hf in range(2):
                V(lambda hf=hf, st6=st6: nc.vector.bn_stats(out=st6[:, hf, :], in_=src[:, hf * 512:(hf + 1) * 512]), [src], [st6])
            V(lambda st6=st6, mv=mv: nc.vector.bn_aggr(out=mv[:], in_=st6[:]), [st6], [mv])
            A(lambda mv=mv, rstd=rstd: nc.scalar.activation(out=rstd[:], in_=mv[:, 1:2], func=AF.Ln, bias=LN_EPS), [mv], [rstd])
            A(lambda rstd=rstd: nc.scalar.activation(out=rstd[:], in_=rstd[:], func=AF.Exp, scale=-0.5), [rstd], [rstd])
            V(lambda mv=mv, rstd=rstd: nc.vector.tensor_scalar(out=dst_b[:], in0=src[:], scalar1=mv[:, 0:1], scalar2=rstd[:, 0:1],
                                              op0=ALU.subtract, op1=ALU.mult), [src, mv, rstd], [dst_b])
            if hp_f is not None:
                V(lambda mv=mv, rstd=rstd, nmr=nmr: nc.vector.tensor_scalar(out=nmr[:], in0=mv[:, 0:1], scalar1=rstd[:, 0:1], scalar2=-1.0, op0=ALU.mult, op1=ALU.mult),
                  [mv, rstd], [nmr])
                A(lambda rstd=rstd, nmr=nmr: nc.scalar.activation(out=hp_f[:], in_=src[:], func=AF.Identity, scale=rstd[:, 0:1], bias=nmr[:, 0:1]),
                  [src, rstd, nmr], [hp_f])

        def transpose8(src_b, dstT, np_=128):
            for k in range(8):
                T(lambda k=k: nc.tensor.transpose(out=pT[:, k * 128:k * 128 + np_], in_=src_b[0:np_, k * 128:(k + 1) * 128],
                                                  identity=idb[0:np_, 0:np_]), [src_b, idb], [pT])
            A(lambda: nc.scalar.copy(out=dstT[:, :, 0:np_], in_=pT[:].rearrange("p (k t) -> p k t", k=8)[:, :, 0:np_]), [pT], [dstT])

        NOBIAS = False

        def proj_tm(xT, c0, n, pb, nt=128, with_bias=True):
            if NOBIAS:
                with_bias = False
            if with_bias:
                bt, bap = bsl(c0, n)
                T(lambda pb=pb, bap=bap: nc.tensor.matmul(pb[0:nt, 0:n], lhsT=ones1b[:, 0:nt], rhs=bap, start=True, stop=False),
                  [ones1b, bt], [pb])
            for k in range(8):
                tl, wap = wsl(k, c0, n)
                T(lambda k=k, pb=pb, xT=xT, wap=wap: nc.tensor.matmul(pb[0:nt, 0:n], lhsT=xT[:, k, 0:nt], rhs=wap,
                                               start=(k == 0 and not with_bias), stop=(k == 7)), [xT, tl], [pb])

        def proj_fm(xT, c0, m, out_ap, pb, nt=128):
            bt, bap = bsl(c0, m)
            T(lambda pb=pb, bap=bap: nc.tensor.matmul(out_ap, lhsT=bap, rhs=ones1b[:, 0:nt], start=True, stop=False), [ones1b, bt], [pb])
            for k in range(8):
                tl, wap = wsl(k, c0, m)
                T(lambda k=k, pb=pb, xT=xT, wap=wap: nc.tensor.matmul(out_ap, lhsT=wap, rhs=xT[:, k, 0:nt], start=False, stop=(k == 7)),
                  [xT, tl], [pb])

        def gate_rows(xT, slot, own):
            for k in range(8):
                T(lambda k=k, xT=xT: nc.tensor.matmul(pS[:, 280:288], lhsT=xT[:, k, :], rhs=w1[:, k, O_MI:O_MI + 8],
                                               start=(k == 0), stop=(k == 7)), [xT, w1], [pS])
            V(lambda gt=gt: nc.vector.tensor_tensor(out=gt[:], in0=pS[:, 280:288], in1=biasg[:], op=ALU.add), [pS, biasg], [gt])
            T(lambda gt=gt: nc.tensor.transpose(out=pS[0:4, 0:128], in_=gt[:, 0:4], identity=idf[:]), [gt, idf], [pS])
            T(lambda gt=gt: nc.tensor.transpose(out=pS[0:4, 128:256], in_=gt[:, 4:8], identity=idf[:]), [gt, idf], [pS])
            A(lambda spe=spe: nc.scalar.activation(out=spe[:], in_=pS[0:4, 128:256], func=AF.Exp, scale=-1.0), [pS], [spe])
            A(lambda spe=spe, spr=spr: nc.scalar.activation(out=spr[:], in_=spe[:], func=AF.Ln, bias=1.0), [spe], [spr])
            if own:
                V(lambda lim=lim: nc.vector.tensor_copy(out=lim[:], in_=pS[0:4, 0:128]), [pS], [lim])
                spsrc = spr
            else:
                sl = slice(slot * 128, (slot + 1) * 128)
                rvt = rvR.next(); rnt = rnR.next()
                LD(rvt, rvt[:], rvalid[:, sl])
                LD(rnt, rnt[:], rneg[:, sl])
                V(lambda rvt=rvt, spr=spr, spm=spm: nc.vector.tensor_tensor(out=spm[:], in0=spr[:], in1=rvt[:], op=ALU.mult), [spr, rvt], [spm])
                V(lambda rnt=rnt, lim=lim: nc.vector.tensor_tensor(out=lim[:], in0=pS[0:4, 0:128], in1=rnt[:], op=ALU.add), [pS, rnt], [lim])
                spsrc = spm
            bp = bnegR.cur(); bc = bnegR.next()
            V(lambda: nc.vector.tensor_tensor_scan(out=bc[:], data0=spsrc[:], data1=zer4[:], initial=bp[:, 127:128],
                                                   op0=ALU.add, op1=ALU.add), [spsrc, zer4, bp], [bc])
            V(lambda lim=lim, u_r=u_r: nc.vector.tensor_tensor(out=u_r[:], in0=lim[:], in1=bc[:], op=ALU.add), [lim, bc], [u_r])
            Up = UR.cur(); Uc = UR.next()
            V(lambda Uc=Uc, u_r=u_r: nc.vector.tensor_tensor_scan(out=Uc[:], data0=u_r[:], data1=u_r[:], initial=Up[:, 127:128],
                                                   op0=ALU.max, op1=ALU.max), [u_r, Up], [Uc])
            T(lambda u_r=u_r: nc.tensor.transpose(out=pS[:, 256:260], in_=u_r[:], identity=idf[0:4, 0:4]), [u_r, idf], [pS])
            ncol = 4
            if own:
                V(lambda Uc=Uc, m_r=m_r: nc.vector.tensor_tensor(out=m_r[:], in0=Uc[:], in1=bc[:], op=ALU.subtract), [Uc, bc], [m_r])
                T(lambda Uc=Uc: nc.tensor.transpose(out=pS[:, 260:264], in_=Uc[:], identity=idf[0:4, 0:4]), [Uc, idf], [pS])
                T(lambda m_r=m_r: nc.tensor.transpose(out=pS[:, 264:268], in_=m_r[:], identity=idf[0:4, 0:4]), [m_r, idf], [pS])
                ncol = 12
            V(lambda cols=cols: nc.vector.tensor_copy(out=cols[:, 0:ncol], in_=pS[:, 256:256 + ncol]), [pS], [cols])
            V(lambda Uc=Uc, diagU=diagU: nc.vector.tensor_scalar(out=diagU[:], in0=idf[0:4, 0:4], scalar1=Uc[:, 127:128], scalar2=None, op0=ALU.mult),
              [idf, Uc], [diagU])
            T(lambda diagU=diagU: nc.tensor.matmul(pS[:, 272:276], lhsT=ones4[:], rhs=diagU[:], start=True, stop=True), [ones4, diagU], [pS])
            Uprev = UendR.cur(); Uend = UendR.next()
            V(lambda: nc.vector.tensor_copy(out=Uend[:], in_=pS[:, 272:276]), [pS], [Uend])
            V(lambda cols=cols, exin=exin: nc.vector.tensor_tensor(out=exin[:, 0:4], in0=cols[:, 0:4], in1=Uend[:], op=ALU.subtract), [cols, Uend], [exin])
            V(lambda exin=exin: nc.vector.tensor_tensor(out=exin[:, 4:8], in0=Uprev[:], in1=Uend[:], op=ALU.subtract), [Uprev, Uend], [exin])
            ne = 8
            if own:
                V(lambda cols=cols, exin=exin: nc.vector.tensor_tensor(out=exin[:, 8:12], in0=Uprev[:], in1=cols[:, 4:8], op=ALU.subtract), [Uprev, cols], [exin])
                V(lambda cols=cols, exin=exin: nc.vector.tensor_scalar(out=exin[:, 12:16], in0=cols[:, 8:12], scalar1=-1.0, scalar2=None, op0=ALU.mult), [cols], [exin])
                ne = 16
            A(lambda exin=exin, ex=ex: nc.scalar.activation(out=ex[:, 0:ne], in_=exin[:, 0:ne], func=AF.Exp), [exin], [ex])
            return Uc

        def state_update():
            V(lambda ktm=ktm, kw=kw, ex=ex: nc.vector.tensor_tensor(out=kw[:], in0=ktm[:], in1=ex[:, 0:4].unsqueeze(2).to_broadcast([128, 4, 128]),
                                                                    op=ALU.mult), [ktm, ex], [kw])
            pb = GB.next()
            for h in range(4):
                T(lambda h=h, pb=pb, vbf=vbf, kw=kw: nc.tensor.matmul(pb[:, h * 128:(h + 1) * 128], lhsT=kw[:, h, :], rhs=vbf[:, h, :], start=True, stop=True),
                  [kw, vbf], [pb])
            for h in range(4):
                T(lambda h=h, kw=kw: nc.tensor.matmul(pS[:, 296 + h:297 + h], lhsT=kw[:, h, :], rhs=onescol[:], start=True, stop=True),
                  [kw, onescol], [pS])
            V(lambda ex=ex: nc.vector.tensor_tensor(out=Cst[:], in0=Cst[:], in1=ex[:, 4:8].unsqueeze(2).to_broadcast([128, 4, 128]), op=ALU.mult),
              [Cst, ex], [Cst])
            V(lambda pb=pb: nc.vector.tensor_tensor(out=Cst[:].rearrange("p h d -> p (h d)"), in0=Cst[:].rearrange("p h d -> p (h d)"), in1=pb[:], op=ALU.add),
              [Cst, pb], [Cst])
            V(lambda ex=ex: nc.vector.tensor_tensor(out=nst[:], in0=nst[:], in1=ex[:, 4:8], op=ALU.mult), [nst, ex], [nst])
            V(lambda: nc.vector.tensor_tensor(out=nst[:], in0=nst[:], in1=pS[:, 296:300], op=ALU.add), [nst, pS], [nst])

        def swa_kv(xT, want_f32):
            akT = akTR.next(); va = vaR.next()
            pb = GB.next()
            proj_fm(xT, O_AK, 128, pb[:, 0:128], pb)
            A(lambda pb=pb, akT=akT: nc.scalar.copy(out=akT[:], in_=pb[:, 0:128]), [pb], [akT])
            pb2 = GB.next()
            proj_tm(xT, O_AK, 256, pb2)
            V(lambda pb2=pb2, va=va: nc.vector.tensor_copy(out=va[:, :, 0:64], in_=pb2[:, 128:256].rearrange("p (k d) -> p k d", k=2)), [pb2], [va])
            if want_f32:
                A(lambda pb2=pb2: nc.scalar.copy(out=kavf[:], in_=pb2[:, 0:256]), [pb2], [kavf])
            return akT, va

        def chunk(kind, src_ap, slot=None, c=None):
            own = kind == "own"
            nonlocal hb, ktm, vbf, kw, gt, spe, spr, spm, lim, u_r, m_r, cols, diagU, exin, ex, st6, mv, rstd, nmr
            hb = hbR.next()
            ktm = ktmR.next()
            vbf = vbfR.next()
            kw = kwR.next()
            gt = gtR.next()
            spe = speR.next()
            spr = sprR.next()
            spm = spmR.next()
            lim = limR.next()
            u_r = u_rR.next()
            m_r = m_rR.next()
            cols = colsR.next()
            diagU = diagUR.next()
            exin = exinR.next()
            ex = exR.next()
            st6 = st6R.next()
            mv = mvR.next()
            rstd = rstdR.next()
            nmr = nmrR.next()
            xt = xR.next()
            LD(xt, xt[:], src_ap, lat=6.0)
            hp = hpR.next() if own else None
            ln0(xt, hb, hp)
            xT = xTR.next()
            transpose8(hb, xT)
            warm(WARM_PRE if not own else WARM_OWN)
            pb = GB.next(); proj_tm(xT, O_MK, 512, pb)
            A(lambda pb=pb, ktm=ktm: nc.scalar.mul(out=ktm[:].rearrange("p h d -> p (h d)"), in_=pb[:], mul=KSCALE), [pb], [ktm])
            pb = GB.next(); proj_tm(xT, O_MV, 512, pb)
            A(lambda pb=pb, vbf=vbf: nc.scalar.copy(out=vbf[:].rearrange("p h d -> p (h d)"), in_=pb[:]), [pb], [vbf])
            Uc = gate_rows(xT, slot, own)
            if kind == "meta":
                akT, va = swa_kv(xT, True)
                V(lambda akT=akT: nc.vector.tensor_copy(out=akTm[:], in_=akT[:, 0:16]), [akT], [akTm])
                V(lambda va=va: nc.vector.tensor_copy(out=vam[:, :, 0:64], in_=va[0:16, :, 0:64]), [va], [vam])
                ST(pk_o[0:16, :], kavf, kavf[0:16, 0:128])
                ST(pv_o[0:16, :], kavf, kavf[0:16, 128:256])
            elif kind == "prelast":
                swa_kv(xT, False)
            if own and c == NOWN - 1:
                DBGDUMP("cols", cols, cols[:], [128, 12])
                DBGDUMP("ex", ex, ex[:], [128, 16])
                DBGDUMP("gt", gt, gt[:], [128, 8])
                DBGDUMP("ktm", ktm, ktm[:].rearrange("p h d -> p (h d)"), [128, 512])
                DBGDUMP("vbf", vbf, vbf[:].rearrange("p h d -> p (h d)"), [128, 512])
                DBGDUMP("hp", hp, hp[:], [128, 1024])
                DBGDUMP("Cpre", Cst, Cst[:].rearrange("p h d -> p (h d)"), [128, 512])
            if own:
                own_chunk(xT, hp, c, Uc)
            state_update()
            if own:
                A(lambda: nc.scalar.copy(out=Cbf[:], in_=Cst[:]), [Cst], [Cbf])
                A(lambda: nc.scalar.copy(out=nbf[:], in_=nst[:]), [nst], [nbf])
                if c == NOWN - 1:
                    ST(pc_o.rearrange("h k v -> k h v"), Cst, Cst[:])
                    ST(pn_o.rearrange("h k -> k h"), nst, nst[:], allow_slow_non_contiguous=True)
                    ST(pm_o, m_r, m_r[:, 127:128])
            elif kind == "prelast" or (kind == "meta" and NPRE == 0):
                A(lambda: nc.scalar.copy(out=Cbf[:], in_=Cst[:]), [Cst], [Cbf])
                A(lambda: nc.scalar.copy(out=nbf[:], in_=nst[:]), [nst], [nbf])

        def own_chunk(xT, hp, c, Uc):
            akTp, vap = akTR.cur(), vaR.cur()
            last = (c == NOWN - 1)
            pb = GB.next()
            for h in range(4):
                proj_fm(xT, O_MQ + h * 128, 128, pb[:, h * 128:(h + 1) * 128], pb)
            A(lambda pb=pb, h=h: nc.scalar.copy(out=qT[:].rearrange("p h t -> p (h t)"), in_=pb[:]), [pb], [qT])
            for h in range(4):
                T(lambda h=h, ktm=ktm: nc.tensor.transpose(out=pT[:, h * 128:(h + 1) * 128], in_=ktm[:, h, :], identity=idb[:]), [ktm, idb], [pT])
            A(lambda: nc.scalar.copy(out=kT[:].rearrange("p h t -> p (h t)"), in_=pT[:, 0:512]), [pT], [kT])
            warm(WARM_OWN)
            for h in range(4):
                V(lambda h=h, Uc=Uc: nc.vector.tensor_scalar(out=bd[:, h, :], in0=Uc[:], scalar1=negid4[:, h:h + 1], scalar2=None, op0=ALU.mult),
                  [Uc, negid4], [bd])
            pU = GB.next()
            T(lambda pU=pU, h=h: nc.tensor.matmul(pU[:], lhsT=ones4[:], rhs=bd[:].rearrange("p h t -> p (h t)"), start=True, stop=True), [ones4, bd], [pU])
            for h in range(4):
                A(lambda h=h, pU=pU, cols=cols: nc.scalar.activation(out=Wt[:, h, :], in_=pU[:, h * 128:(h + 1) * 128], func=AF.Exp, bias=cols[:, h:h + 1]),
                  [pU, cols], [Wt])
            V(lambda h=h: nc.vector.tensor_tensor(out=Wm[:].rearrange("p h t -> p (h t)"), in0=Wt[:].rearrange("p h t -> p (h t)"),
                                              in1=tri4[:], op=ALU.mult), [Wt, tri4], [Wm])
            pSc = GB.next()
            for h in range(4):
                T(lambda h=h, pSc=pSc: nc.tensor.matmul(pSc[:, h * 128:(h + 1) * 128], lhsT=kT[:, h, :], rhs=qT[:, h, :], start=True, stop=True),
                  [kT, qT], [pSc])
            V(lambda pSc=pSc, h=h: nc.vector.tensor_tensor(out=PTt[:].rearrange("p h t -> p (h t)"), in0=pSc[:], in1=Wm[:].rearrange("p h t -> p (h t)"),
                                              op=ALU.mult), [pSc, Wm], [PTt])
            pN = GB.next(); pI = GB.next()
            for h in range(4):
                T(lambda h=h, pN=pN, vbf=vbf: nc.tensor.matmul(pN[:, h * 128:(h + 1) * 128], lhsT=PTt[:, h, :], rhs=vbf[:, h, :], start=True, stop=True),
                  [PTt, vbf], [pN])
            for h in range(4):
                T(lambda h=h, pI=pI: nc.tensor.matmul(pI[:, h * 128:(h + 1) * 128], lhsT=qT[:, h, :], rhs=Cbf[:, h, :], start=True, stop=True),
                  [qT, Cbf], [pI])
            for h in range(4):
                T(lambda h=h: nc.tensor.matmul(pS[:, 288 + h:289 + h], lhsT=PTt[:, h, :], rhs=onescol[:], start=True, stop=True),
                  [PTt, onescol], [pS])
            for h in range(4):
                T(lambda h=h: nc.tensor.matmul(pS[:, 292 + h:293 + h], lhsT=qT[:, h, :], rhs=nbf[:, h:h + 1], start=True, stop=True),
                  [qT, nbf], [pS])
            for h in range(4):
                A(lambda h=h, pI=pI, ex=ex: nc.scalar.mul(out=hi[:, h, :], in_=pI[:, h * 128:(h + 1) * 128], mul=ex[:, 8 + h:9 + h]),
                  [pI, ex], [hi])
            V(lambda pN=pN, h=h: nc.vector.tensor_tensor(out=hn[:].rearrange("p h t -> p (h t)"), in0=hi[:].rearrange("p h t -> p (h t)"), in1=pN[:],
                                              op=ALU.add), [hi, pN], [hn])
            V(lambda ex=ex: nc.vector.tensor_tensor(out=d1[:], in0=pS[:, 292:296], in1=ex[:, 8:12], op=ALU.mult), [pS, ex], [d1])
            V(lambda: nc.vector.tensor_tensor(out=d2[:], in0=d1[:], in1=pS[:, 288:292], op=ALU.add), [d1, pS], [d2])
            V(lambda: nc.vector.scalar_tensor_tensor(out=d1[:], in0=d2[:], scalar=-1.0, in1=d2[:], op0=ALU.mult, op1=ALU.max), [d2], [d1])
            V(lambda ex=ex: nc.vector.tensor_tensor(out=d2[:], in0=d1[:], in1=ex[:, 12:16], op=ALU.max), [d1, ex], [d2])
            V(lambda: nc.vector.reciprocal(out=rden[:], in_=d2[:]), [d2], [rden])
            pb = GB.next(); proj_tm(xT, O_MO, 512, pb)
            A(lambda pb=pb: nc.scalar.activation(out=eo[:], in_=pb[:], func=AF.Exp, scale=-1.0), [pb], [eo])
            A(lambda: nc.scalar.activation(out=so[:], in_=eo[:], func=AF.Ln, bias=1.0), [eo], [so])
            A(lambda: nc.scalar.activation(out=so[:], in_=so[:], func=AF.Exp, scale=-1.0), [so], [so])
            V(lambda: nc.vector.tensor_tensor(out=hg[:], in0=hn[:], in1=rden[:].unsqueeze(2).to_broadcast([128, 4, 128]), op=ALU.mult),
              [hn, rden], [hg])
            V(lambda h=h: nc.vector.tensor_tensor(out=hg[:].rearrange("p h t -> p (h t)"), in0=hg[:].rearrange("p h t -> p (h t)"), in1=so[:],
                                              op=ALU.mult), [hg, so], [hg])
            for h in range(4):
                V(lambda h=h: nc.vector.bn_stats(out=st4[:, h, :], in_=hg[:, h, :]), [hg], [st4])
            for h in range(4):
                V(lambda h=h: nc.vector.bn_aggr(out=mv4[:, h, :], in_=st4[:, h, :]), [st4], [mv4])
            A(lambda: nc.scalar.activation(out=rs4[:].unsqueeze(2), in_=mv4[:, :, 1:2], func=AF.Ln, bias=LN_EPS), [mv4], [rs4])
            A(lambda: nc.scalar.activation(out=rs4[:], in_=rs4[:], func=AF.Exp, scale=-0.5), [rs4], [rs4])
            for h in range(4):
                V(lambda h=h: nc.vector.tensor_scalar(out=hg[:, h, :], in0=hg[:, h, :], scalar1=mv4[:, h, 0:1], scalar2=rs4[:, h:h + 1],
                                                      op0=ALU.subtract, op1=ALU.mult), [hg, mv4, rs4], [hg])
            pb = GB.next(); proj_tm(xT, O_MZ, 512, pb)
            A(lambda pb=pb: nc.scalar.activation(out=ez[:], in_=pb[:], func=AF.Exp, scale=-1.0), [pb], [ez])
            A(lambda: nc.scalar.activation(out=ez[:], in_=ez[:], func=AF.Ln, bias=1.0), [ez], [ez])
            A(lambda: nc.scalar.activation(out=ez[:], in_=ez[:], func=AF.Exp, scale=-1.0), [ez], [ez])
            V(lambda pb=pb: nc.vector.tensor_tensor(out=ez[:], in0=ez[:], in1=pb[:], op=ALU.mult), [ez, pb], [ez])
            V(lambda h=h: nc.vector.tensor_tensor(out=ybf[:, 0:512], in0=hg[:].rearrange("p h t -> p (h t)"), in1=sz[:], op=ALU.mult),
              [hg, sz], [ybf])

            akT, va = swa_kv(xT, last)
            if last:
                ST(pk_o[16:144, :], kavf, kavf[:, 0:128])
                ST(pv_o[16:144, :], kavf, kavf[:, 128:256])
            pb = GB.next(); proj_tm(xT, O_AQ, 512, pb)
            A(lambda pb=pb: nc.scalar.copy(out=Wt[:].rearrange("p g (k d) -> p g k d", k=2),
                                           in_=pb[:].rearrange("p (k g d) -> p g k d", k=2, g=4)), [pb], [Wt])
            for g in range(4):
                T(lambda g=g: nc.tensor.transpose(out=pT[:, g * 128:(g + 1) * 128], in_=Wt[:, g, :], identity=idb[:]), [Wt, idb], [pT])
            V(lambda: nc.vector.tensor_copy(out=aqT[:].rearrange("p g t -> p (g t)"), in_=pT[:, 0:512]), [pT], [aqT])

            pb = GB.next(); proj_tm(xT, O_AZ, 512, pb)
            A(lambda pb=pb: nc.scalar.activation(out=eaz[:], in_=pb[:], func=AF.Exp, scale=-1.0), [pb], [eaz])
            A(lambda: nc.scalar.activation(out=eaz[:], in_=eaz[:], func=AF.Ln, bias=1.0), [eaz], [eaz])
            A(lambda: nc.scalar.activation(out=eaz[:], in_=eaz[:], func=AF.Exp, scale=-1.0), [eaz], [eaz])
            V(lambda pb=pb: nc.vector.tensor_tensor(out=eaz[:], in0=eaz[:], in1=pb[:], op=ALU.mult), [eaz, pb], [eaz])
            warm(WARM_OWN)
            pmask = pm1b if c == 0 else prev4
            for kap in range(2):
                ks = slice(64 * kap, 64 * kap + 64)
                pb = GB.next()
                for g in range(4):
                    T(lambda g=g, pb=pb, akT=akT, ks=ks: nc.tensor.matmul(pb[:, g * 128:(g + 1) * 128], lhsT=akT[ks, :], rhs=aqT[ks, g, :], start=True, stop=True),
                      [akT, aqT], [pb])
                A(lambda pb=pb: nc.scalar.activation(out=Eown[:], in_=pb[:], func=AF.Exp, scale=ASCALE), [pb], [Eown])
                V(lambda: nc.vector.tensor_tensor(out=Eown2[:], in0=Eown[:], in1=tri4[:], op=ALU.mult), [Eown, tri4], [Eown2])
                pb = GB.next()
                for g in range(4):
                    T(lambda g=g, pb=pb, akTp=akTp, ks=ks: nc.tensor.matmul(pb[:, g * 128:(g + 1) * 128], lhsT=akTp[ks, :], rhs=aqT[ks, g, :], start=True, stop=True),
                      [akTp, aqT], [pb])
                A(lambda pb=pb: nc.scalar.activation(out=Eprev[:], in_=pb[:], func=AF.Exp, scale=ASCALE), [pb], [Eprev])
                V(lambda pmask=pmask: nc.vector.tensor_tensor(out=Eprev2[:], in0=Eprev[:], in1=pmask[:], op=ALU.mult), [Eprev, pmask], [Eprev2])
                pb = GB.next()
                for g in range(4):
                    T(lambda g=g, pb=pb, ks=ks: nc.tensor.matmul(pb[0:16, g * 128:(g + 1) * 128], lhsT=akTm[ks, :], rhs=aqT[ks, g, :], start=True, stop=True),
                      [akTm, aqT], [pb])
                A(lambda pb=pb: nc.scalar.activation(out=Emeta[:], in_=pb[0:16, :], func=AF.Exp, scale=ASCALE), [pb], [Emeta])
                po = GB.next()
                for g in range(4):
                    oap = po[:, g * 65:(g + 1) * 65]
                    T(lambda g=g, oap=oap, po=po, kap=kap: nc.tensor.matmul(oap, lhsT=Emeta[:, g * 128:(g + 1) * 128], rhs=vam[:, kap, :], start=True, stop=False),
                      [Emeta, vam], [po])
                    T(lambda g=g, oap=oap, po=po, vap=vap, kap=kap: nc.tensor.matmul(oap, lhsT=Eprev2[:, g * 128:(g + 1) * 128], rhs=vap[:, kap, :], start=False, stop=False),
                      [Eprev2, vap], [po])
                    T(lambda g=g, oap=oap, po=po, va=va, kap=kap: nc.tensor.matmul(oap, lhsT=Eown2[:, g * 128:(g + 1) * 128], rhs=va[:, kap, :], start=False, stop=True),
                      [Eown2, va], [po])
                po3 = po[:, 0:260].rearrange("p (g e) -> p g e", g=4)
                V(lambda po3=po3, po=po, kap=kap: nc.vector.tensor_tensor(out=dsum[:].unsqueeze(2), in0=po3[:, :, 64:65],
                                                          in1=esink[:, 4 * kap:4 * kap + 4].unsqueeze(2), op=ALU.add),
                  [po, esink], [dsum])
                V(lambda: nc.vector.reciprocal(out=rsa[:], in_=dsum[:]), [dsum], [rsa])
                V(lambda po3=po3, po=po: nc.vector.tensor_tensor(out=ya[:], in0=po3[:, :, 0:64], in1=rsa[:].unsqueeze(2).to_broadcast([128, 4, 64]),
                                                          op=ALU.mult), [po, rsa], [ya])
                V(lambda kap=kap, g=g: nc.vector.tensor_tensor(out=ybf[:, 512 + 256 * kap:768 + 256 * kap], in0=ya[:].rearrange("p g d -> p (g d)"),
                                                  in1=saz[:, 256 * kap:256 * kap + 256], op=ALU.mult), [ya, saz], [ybf])

            if c == NOWN - 1:
                DBGDUMP("ybf", ybf, ybf[:], [128, 1024])
                DBGDUMP("hn", hi, hi[:].rearrange("p h d -> p (h d)"), [128, 512])
            transpose8(ybf, yT)
            warm(WARM_OWN)
            pm_ = [GB.next(), GB.next()]
            for hf in range(2):
                T(lambda hf=hf: nc.tensor.matmul(pm_[hf][:], lhsT=ones1b[:, 0:128], rhs=bob[:, hf * 512:(hf + 1) * 512], start=True, stop=False),
                  [ones1b, bob], [pm_[hf]])
                for k in range(8):
                    T(lambda k=k, hf=hf: nc.tensor.matmul(pm_[hf][:], lhsT=yT[:, k, :], rhs=wo_bf[:, k, hf * 512:(hf + 1) * 512],
                                                          start=False, stop=(k == 7)), [yT, wo_bf], [pm_[hf]])
            V(lambda hp=hp: nc.vector.tensor_tensor(out=hp[:], in0=hp[:], in1=ln0g[:], op=ALU.mult), [hp, ln0g], [hp])
            for hf in range(2):
                V(lambda hf=hf, hp=hp: nc.vector.tensor_tensor(out=hp[:, hf * 512:(hf + 1) * 512], in0=hp[:, hf * 512:(hf + 1) * 512],
                                                               in1=pm_[hf][:], op=ALU.add),
                  [hp, pm_[hf]], [hp])
            layer_norm(hp, hp, lng, lnb)
            P.dma("pool", lambda hp=hp: nc.gpsimd.dma_start(out=y_o[c * 128:(c + 1) * 128, :], in_=hp[:]), reads=[hp], key="S_" + hp.name)


        def sample_program():
            nonlocal hb, ktm, vbf, kw, gt, spe, spr, spm, lim, u_r, m_r, cols, diagU, exin, ex, st6, mv, rstd, nmr
            hb = hbR.next()
            ktm = ktmR.next()
            vbf = vbfR.next()
            kw = kwR.next()
            gt = gtR.next()
            spe = speR.next()
            spr = sprR.next()
            spm = spmR.next()
            lim = limR.next()
            u_r = u_rR.next()
            m_r = m_rR.next()
            cols = colsR.next()
            diagU = diagUR.next()
            exin = exinR.next()
            ex = exR.next()
            st6 = st6R.next()
            mv = mvR.next()
            rstd = rstdR.next()
            nmr = nmrR.next()
            seqcol = sb("seqcol", [128, 16])
            blk4 = sb("blk4", [128, 512], BF16); winm = sb("winm", [128, 256], BF16); newm = sb("newm", [128, 256], BF16)
            esk16 = sb("esk16", [16, 2])
            LD(seqcol, seqcol[:], cseqcol)
            LD(cstage, cstage[:, 0:512], cblk4)
            V(lambda: nc.vector.tensor_copy(out=blk4[:], in_=cstage[:, 0:512]), [cstage], [blk4])
            LD(cstage, cstage[:, 0:256], cwin)
            V(lambda: nc.vector.tensor_copy(out=winm[:], in_=cstage[:, 0:256]), [cstage], [winm])
            LD(cstage, cstage[:, 0:256], cnew)
            V(lambda: nc.vector.tensor_copy(out=newm[:], in_=cstage[:, 0:256]), [cstage], [newm])
            LD(esk16, esk16[:], sinks16)
            A(lambda: nc.scalar.activation(out=esk16[:], in_=esk16[:], func=AF.Exp), [esk16], [esk16])

            xt = xR.next()
            V(lambda xt=xt: nc.vector.memset(xt[:], 0.0), [], [xt])
            LD(xt, xt[0:64, :], xs)
            hp = hpR.next()
            ln0(xt, hb, hp)
            xT = xTR.next()
            transpose8(hb, xT)
            vaug = sb("vaug", [128, 4, 129], BF16)
            V(lambda: nc.vector.memset(vaug[:], 1.0), [], [vaug])
            pb = GB.next(); proj_tm(xT, O_MK, 512, pb)
            A(lambda pb=pb, ktm=ktm: nc.scalar.mul(out=ktm[:].rearrange("p h d -> p (h d)"), in_=pb[:], mul=KSCALE), [pb], [ktm])
            pb = GB.next(); proj_tm(xT, O_MV, 512, pb)
            V(lambda pb=pb: nc.vector.tensor_copy(out=vaug[:, :, 0:128], in_=pb[:].rearrange("p (h d) -> p h d", h=4)), [pb], [vaug])

            for k in range(8):
                T(lambda k=k, xT=xT: nc.tensor.matmul(pS[:, 280:288], lhsT=xT[:, k, :], rhs=w1[:, k, O_MI:O_MI + 8],
                                                      start=(k == 0), stop=(k == 7)), [xT, w1], [pS])
            V(lambda gt=gt: nc.vector.tensor_tensor(out=gt[:], in0=pS[:, 280:288], in1=biasg[:], op=ALU.add), [pS, biasg], [gt])
            T(lambda gt=gt: nc.tensor.transpose(out=pS[0:4, 0:128], in_=gt[:, 0:4], identity=idf[:]), [gt, idf], [pS])
            T(lambda gt=gt: nc.tensor.transpose(out=pS[0:4, 128:256], in_=gt[:, 4:8], identity=idf[:]), [gt, idf], [pS])
            A(lambda spe=spe: nc.scalar.activation(out=spe[:], in_=pS[0:4, 128:256], func=AF.Exp, scale=-1.0), [pS], [spe])
            A(lambda spe=spe, spr=spr: nc.scalar.activation(out=spr[:], in_=spe[:], func=AF.Ln, bias=1.0), [spe], [spr])
            V(lambda lim=lim: nc.vector.tensor_copy(out=lim[:], in_=pS[0:4, 0:128]), [pS], [lim])
            m0 = sb("m0", [4, 16]); bs = sb("bs", [4, 128]); Us = sb("Us", [4, 128]); ueb = sb("ueb", [4, 128]); m0b = sb("m0b", [4, 128])
            decr = sb("decr", [4, 16]); bdd = sb("bdd", [4, 16, 4]); decbc = sb("decbc", [128, 64])
            LD(m0, m0[:], sm_i)
            for t_ in (bs, Us, ueb, m0b):
                V(lambda t_=t_: nc.vector.memset(t_[:], 0.0), [], [t_])
            v3 = lambda t_: t_[:, 0:64].rearrange("p (j l) -> p j l", l=4)
            V(lambda: nc.vector.tensor_copy(out=v3(bs)[:, :, 0:1], in_=v3(spr)[:, :, 0:1]), [spr], [bs])
            for l in range(1, 4):
                V(lambda l=l: nc.vector.tensor_tensor(out=v3(bs)[:, :, l:l + 1], in0=v3(bs)[:, :, l - 1:l], in1=v3(spr)[:, :, l:l + 1], op=ALU.add),
                  [bs, spr], [bs])
            V(lambda lim=lim, u_r=u_r: nc.vector.tensor_tensor(out=u_r[:], in0=lim[:], in1=bs[:], op=ALU.add), [lim, bs], [u_r])
            V(lambda: nc.vector.tensor_tensor(out=v3(Us)[:, :, 0:1], in0=v3(u_r)[:, :, 0:1], in1=m0[:].unsqueeze(2), op=ALU.max), [u_r, m0], [Us])
            for l in range(1, 4):
                V(lambda l=l: nc.vector.tensor_tensor(out=v3(Us)[:, :, l:l + 1], in0=v3(Us)[:, :, l - 1:l], in1=v3(u_r)[:, :, l:l + 1], op=ALU.max),
                  [Us, u_r], [Us])
            V(lambda m_r=m_r: nc.vector.tensor_tensor(out=m_r[:], in0=Us[:], in1=bs[:], op=ALU.subtract), [Us, bs], [m_r])
            V(lambda: nc.vector.tensor_copy(out=v3(ueb), in_=v3(Us)[:, :, 3:4].to_broadcast([4, 16, 4])), [Us], [ueb])
            V(lambda: nc.vector.tensor_copy(out=v3(m0b), in_=m0[:].unsqueeze(2).to_broadcast([4, 16, 4])), [m0], [m0b])
            mnew = sb("mnew", [4, 16])
            V(lambda: nc.vector.tensor_copy(out=mnew[:].unsqueeze(2), in_=v3(m_r)[:, :, 3:4]), [m_r], [mnew])
            ST(sm_o, mnew, mnew[:])
            cols5 = sb("cols5", [128, 20])
            for i_, rt in enumerate((u_r, Us, m_r, ueb, m0b)):
                T(lambda i_=i_, rt=rt: nc.tensor.transpose(out=pS[:, 256 + 4 * i_:260 + 4 * i_], in_=rt[:], identity=idf[0:4, 0:4]), [rt, idf], [pS])
            V(lambda: nc.vector.tensor_copy(out=cols5[:], in_=pS[:, 256:276]), [pS], [cols5])
            V(lambda cols=cols: nc.vector.tensor_copy(out=cols[:, 0:4], in_=cols5[:, 0:4]), [cols5], [cols])
            V(lambda exin=exin: nc.vector.tensor_tensor(out=exin[:, 0:4], in0=cols5[:, 0:4], in1=cols5[:, 12:16], op=ALU.subtract), [cols5], [exin])
            V(lambda exin=exin: nc.vector.tensor_tensor(out=exin[:, 8:12], in0=cols5[:, 16:20], in1=cols5[:, 4:8], op=ALU.subtract), [cols5], [exin])
            V(lambda exin=exin: nc.vector.tensor_scalar(out=exin[:, 12:16], in0=cols5[:, 8:12], scalar1=-1.0, scalar2=None, op0=ALU.mult), [cols5], [exin])
            V(lambda exin=exin: nc.vector.memset(exin[:, 4:8], 0.0), [], [exin])
            A(lambda exin=exin, ex=ex: nc.scalar.activation(out=ex[:, 0:16], in_=exin[:, 0:16], func=AF.Exp), [exin], [ex])
            V(lambda: nc.vector.tensor_tensor(out=decr[:].unsqueeze(2), in0=m0[:].unsqueeze(2), in1=v3(Us)[:, :, 3:4], op=ALU.subtract), [m0, Us], [decr])
            A(lambda: nc.scalar.activation(out=decr[:], in_=decr[:], func=AF.Exp), [decr], [decr])
            for h in range(4):
                V(lambda h=h: nc.vector.tensor_scalar(out=bdd[:, :, h:h + 1], in0=decr[:].unsqueeze(2), scalar1=idf[0:4, h:h + 1], scalar2=None, op0=ALU.mult),
                  [decr, idf], [bdd])
            T(lambda: nc.tensor.matmul(pS[:, 300:364], lhsT=ones4[:], rhs=bdd[:].rearrange("p j h -> p (j h)"), start=True, stop=True),
              [ones4, bdd], [pS])
            V(lambda: nc.vector.tensor_copy(out=decbc[:], in_=pS[:, 300:364]), [pS], [decbc])

            akT, va = swa_kv(xT, True)
            pb = GB.next(); proj_tm(xT, O_MO, 512, pb)
            A(lambda pb=pb: nc.scalar.activation(out=eo[:], in_=pb[:], func=AF.Exp, scale=-1.0), [pb], [eo])
            pb = GB.next(); proj_tm(xT, O_MZ, 512, pb)
            A(lambda pb=pb: nc.scalar.activation(out=ez[:], in_=pb[:], func=AF.Exp, scale=-1.0), [pb], [ez])
            A(lambda: nc.scalar.activation(out=ez[:], in_=ez[:], func=AF.Ln, bias=1.0), [ez], [ez])
            A(lambda: nc.scalar.activation(out=ez[:], in_=ez[:], func=AF.Exp, scale=-1.0), [ez], [ez])
            V(lambda pb=pb: nc.vector.tensor_tensor(out=ez[:], in0=ez[:], in1=pb[:], op=ALU.mult), [ez, pb], [ez])
            pb = GB.next(); proj_tm(xT, O_AZ, 512, pb)
            A(lambda pb=pb: nc.scalar.activation(out=eaz[:], in_=pb[:], func=AF.Exp, scale=-1.0), [pb], [eaz])
            A(lambda: nc.scalar.activation(out=eaz[:], in_=eaz[:], func=AF.Ln, bias=1.0), [eaz], [eaz])
            A(lambda: nc.scalar.activation(out=eaz[:], in_=eaz[:], func=AF.Exp, scale=-1.0), [eaz], [eaz])
            V(lambda pb=pb: nc.vector.tensor_tensor(out=eaz[:], in0=eaz[:], in1=pb[:], op=ALU.mult), [eaz, pb], [eaz])
            pb = GB.next()
            for h in range(4):
                proj_fm(xT, O_MQ + h * 128, 128, pb[:, h * 128:(h + 1) * 128], pb)
            A(lambda pb=pb: nc.scalar.copy(out=qT[:].rearrange("p h t -> p (h t)"), in_=pb[:]), [pb], [qT])
            for h in range(4):
                T(lambda h=h, ktm=ktm: nc.tensor.transpose(out=pT[:, h * 128:(h + 1) * 128], in_=ktm[:, h, :], identity=idb[:]), [ktm, idb], [pT])
            A(lambda: nc.scalar.copy(out=kT[:].rearrange("p h t -> p (h t)"), in_=pT[:, 0:512]), [pT], [kT])
            pb = GB.next(); proj_tm(xT, O_AQ, 512, pb)
            A(lambda pb=pb: nc.scalar.copy(out=Wt[:].rearrange("p g (k d) -> p g k d", k=2),
                                           in_=pb[:].rearrange("p (k g d) -> p g k d", k=2, g=4)), [pb], [Wt])
            for g in range(4):
                T(lambda g=g: nc.tensor.transpose(out=pT[:, g * 128:(g + 1) * 128], in_=Wt[:, g, :], identity=idb[:]), [Wt, idb], [pT])
            V(lambda: nc.vector.tensor_copy(out=aqT[:].rearrange("p g t -> p (g t)"), in_=pT[:, 0:512]), [pT], [aqT])

            for h in range(4):
                V(lambda h=h: nc.vector.tensor_scalar(out=bd[:, h, :], in0=Us[:], scalar1=negid4[:, h:h + 1], scalar2=None, op0=ALU.mult),
                  [Us, negid4], [bd])
            pU = GB6.tiles[4]
            T(lambda: nc.tensor.matmul(pU[:], lhsT=ones4[:], rhs=bd[:].rearrange("p h t -> p (h t)"), start=True, stop=True), [ones4, bd], [pU])
            for h in range(4):
                A(lambda h=h, cols=cols: nc.scalar.activation(out=Wt[:, h, :], in_=pU[:, h * 128:(h + 1) * 128], func=AF.Exp, bias=cols[:, h:h + 1]),
                  [pU, cols], [Wt])
            V(lambda: nc.vector.tensor_tensor(out=Wm[:].rearrange("p h t -> p (h t)"), in0=Wt[:].rearrange("p h t -> p (h t)"),
                                              in1=blk4[:], op=ALU.mult), [Wt, blk4], [Wm])
            pSc = GB6.tiles[5]
            for h in range(4):
                T(lambda h=h: nc.tensor.matmul(pSc[:, h * 128:(h + 1) * 128], lhsT=kT[:, h, :], rhs=qT[:, h, :], start=True, stop=True),
                  [kT, qT], [pSc])
            V(lambda: nc.vector.tensor_tensor(out=PTt[:].rearrange("p h t -> p (h t)"), in0=pSc[:], in1=Wm[:].rearrange("p h t -> p (h t)"),
                                              op=ALU.mult), [pSc, Wm], [PTt])
            pN = GB6.tiles[4]
            for h in range(4):
                T(lambda h=h: nc.tensor.matmul(pN[:, h * 128:(h + 1) * 128], lhsT=PTt[:, h, :], rhs=vaug[:, h, 0:128], start=True, stop=True),
                  [PTt, vaug], [pN])
            for h in range(4):
                T(lambda h=h: nc.tensor.matmul(pS[:, 288 + h:289 + h], lhsT=PTt[:, h, :], rhs=onescol[:], start=True, stop=True),
                  [PTt, onescol], [pS])
            xsp = xR.next()
            hnum_ap = xsp[:, 0:512]
            V(lambda: nc.vector.tensor_copy(out=hnum_ap, in_=pN[:]), [pN], [xsp])
            V(lambda: nc.vector.tensor_copy(out=d2[:], in_=pS[:, 288:292]), [pS], [d2])

            snl = sb("snl", [64, 128]); nT = sb("nT", [128, 64]); nTn = sb("nTn", [128, 64]); snout = snl
            LD(snl, snl[:], sn_i)
            T(lambda: nc.tensor.transpose(out=pS[:, 364:428], in_=snl[:], identity=idf[0:64, 0:64]), [snl, idf], [pS])
            V(lambda: nc.vector.tensor_copy(out=nT[:], in_=pS[:, 364:428]), [pS], [nT])
            CbR = rot("Cb", [128, 4, 129], BF16, 2)
            qmR = rot("qm", [128, 4, 64], BF16, 2); kwmR = rot("kwm", [64, 4, 128], BF16, 2)
            for t_ in qmR.tiles:
                V(lambda t_=t_: nc.vector.memset(t_[:], 0.0), [], [t_])
            for h in range(4):
                V(lambda h=h, ktm=ktm, kw=kw, ex=ex: nc.vector.tensor_scalar(out=kw[:, h, :], in0=ktm[:, h, :], scalar1=ex[:, h:h + 1], scalar2=None, op0=ALU.mult),
                  [ktm, ex], [kw])
            pIs = GB6.tiles[0:4]
            pUp = [GB6.tiles[4], GB6.tiles[5]]
            clR = Rot(wst.tiles + [t_ for t_ in hpR.tiles if t_ is not hp])
            for j in range(16):
                Clt = clR.next(); Cb = CbR.next(); qm = qmR.next(); kwm = kwmR.next()
                Cl = Clt[:, 0:516].rearrange("p (h e) -> p h e", e=129)
                LD(Clt, Cl[:, :, 0:128], sc_i[j].rearrange("h k v -> k h v"))
                G(lambda Cl=Cl, j=j: nc.gpsimd.tensor_copy(out=Cl[:, :, 128:129], in_=nT[:, 4 * j:4 * j + 4].unsqueeze(2)), [nT], [Clt])
                A(lambda Cl=Cl, Cb=Cb: nc.scalar.copy(out=Cb[:], in_=Cl), [Clt], [Cb])
                if j >= 2:
                    G(lambda qm=qm, j=j: nc.gpsimd.memset(qm[:, :, 4 * (j - 2):4 * (j - 2) + 4], 0.0), [], [qm])
                G(lambda qm=qm, j=j: nc.gpsimd.tensor_copy(out=qm[:, :, 4 * j:4 * j + 4], in_=qT[:, :, 4 * j:4 * j + 4]), [qT], [qm])
                for h in range(4):
                    T(lambda h=h, qm=qm, Cb=Cb, j=j: nc.tensor.matmul(pIs[h][0:64, 0:129], lhsT=qm[:, h, :], rhs=Cb[:, h, :],
                                                                      start=(j == 0), stop=(j == 15)), [qm, Cb], [pIs[h]])
                V(lambda kwm=kwm, j=j, kw=kw: nc.vector.tensor_scalar(out=kwm[:].rearrange("p h d -> p (h d)"), in0=kw[0:64].rearrange("p h d -> p (h d)"),
                                                              scalar1=seqcol[0:64, j:j + 1], scalar2=None, op0=ALU.mult), [kw, seqcol], [kwm])
                for h in range(4):
                    pu = pUp[h // 2]
                    T(lambda h=h, pu=pu, kwm=kwm: nc.tensor.matmul(pu[:, (h % 2) * 129:(h % 2) * 129 + 129], lhsT=kwm[:, h, :], rhs=vaug[0:64, h, :],
                                                                   start=True, stop=True), [kwm, vaug], [pu])
                for h in range(4):
                    pu = pUp[h // 2]
                    V(lambda h=h, pu=pu, Cl=Cl, j=j: nc.vector.scalar_tensor_tensor(
                        out=Cl[:, h, :], in0=Cl[:, h, :], scalar=decbc[:, 4 * j + h:4 * j + h + 1],
                        in1=pu[:, (h % 2) * 129:(h % 2) * 129 + 129], op0=ALU.mult, op1=ALU.add), [Clt, decbc, pu], [Clt])
                G(lambda Cl=Cl, j=j: nc.gpsimd.tensor_copy(out=nTn[:, 4 * j:4 * j + 4].unsqueeze(2), in_=Cl[:, :, 128:129]), [Clt], [nTn])
                P.dma("pool", lambda j=j, Cl=Cl: nc.gpsimd.dma_start(out=sc_o[j].rearrange("h k v -> k h v"), in_=Cl[:, :, 0:128]),
                      reads=[Clt], key="S_" + Clt.name)
            T(lambda: nc.tensor.transpose(out=pS[0:64, 0:128], in_=nTn[:], identity=idf[:]), [nTn, idf], [pS])
            V(lambda: nc.vector.tensor_copy(out=snout[:], in_=pS[0:64, 0:128]), [pS], [snout])
            ST(sn_o, snout, snout[:])

            for h in range(4):
                A(lambda h=h, ex=ex: nc.scalar.mul(out=hi[0:64, h, :], in_=pIs[h][0:64, 0:128], mul=ex[0:64, 8 + h:9 + h]), [pIs[h], ex], [hi])
                V(lambda h=h, ex=ex: nc.vector.tensor_tensor(out=d1[0:64, h:h + 1], in0=pIs[h][0:64, 128:129], in1=ex[0:64, 8 + h:9 + h], op=ALU.mult),
                  [pIs[h], ex], [d1])
            V(lambda: nc.vector.tensor_tensor(out=hi[0:64].rearrange("p h t -> p (h t)"), in0=hi[0:64].rearrange("p h t -> p (h t)"),
                                              in1=xsp[0:64, 0:512], op=ALU.add), [hi, xsp], [hi])
            V(lambda: nc.vector.tensor_tensor(out=d2[0:64], in0=d1[0:64], in1=d2[0:64], op=ALU.add), [d1, d2], [d2])
            V(lambda: nc.vector.scalar_tensor_tensor(out=d1[0:64], in0=d2[0:64], scalar=-1.0, in1=d2[0:64], op0=ALU.mult, op1=ALU.max), [d2], [d1])
            V(lambda ex=ex: nc.vector.tensor_tensor(out=d2[0:64], in0=d1[0:64], in1=ex[0:64, 12:16], op=ALU.max), [d1, ex], [d2])
            V(lambda: nc.vector.reciprocal(out=rden[0:64], in_=d2[0:64]), [d2], [rden])
            A(lambda: nc.scalar.activation(out=eo[:], in_=eo[:], func=AF.Ln, bias=1.0), [eo], [eo])
            A(lambda: nc.scalar.activation(out=eo[:], in_=eo[:], func=AF.Exp, scale=-1.0), [eo], [eo])
            V(lambda: nc.vector.tensor_tensor(out=hi[0:64], in0=hi[0:64], in1=rden[0:64].unsqueeze(2).to_broadcast([64, 4, 128]), op=ALU.mult),
              [hi, rden], [hi])
            V(lambda: nc.vector.tensor_tensor(out=hi[0:64].rearrange("p h t -> p (h t)"), in0=hi[0:64].rearrange("p h t -> p (h t)"),
                                              in1=eo[0:64], op=ALU.mult), [hi, eo], [hi])
            for h in range(4):
                V(lambda h=h: nc.vector.bn_stats(out=st4[0:64, h, :], in_=hi[0:64, h, :]), [hi], [st4])
            for h in range(4):
                V(lambda h=h: nc.vector.bn_aggr(out=mv4[0:64, h, :], in_=st4[0:64, h, :]), [st4], [mv4])
            A(lambda: nc.scalar.activation(out=rs4[0:64].unsqueeze(2), in_=mv4[0:64, :, 1:2], func=AF.Ln, bias=LN_EPS), [mv4], [rs4])
            A(lambda: nc.scalar.activation(out=rs4[0:64], in_=rs4[0:64], func=AF.Exp, scale=-0.5), [rs4], [rs4])
            for h in range(4):
                V(lambda h=h: nc.vector.tensor_scalar(out=hi[0:64, h, :], in0=hi[0:64, h, :], scalar1=mv4[0:64, h, 0:1], scalar2=rs4[0:64, h:h + 1],
                                                      op0=ALU.subtract, op1=ALU.mult), [hi, mv4, rs4], [hi])
            V(lambda: nc.vector.memset(ybf[:], 0.0), [], [ybf])
            V(lambda: nc.vector.tensor_tensor(out=ybf[0:64, 0:512], in0=hi[0:64].rearrange("p h t -> p (h t)"), in1=ez[0:64], op=ALU.mult),
              [hi, ez], [ybf])

            kst = rot("kst", [128, 128], F32, 2); kmst = rot("kmst", [16, 128], F32, 2)
            vst = rot("vst", [128, 128], F32, 2); vmst = rot("vmst", [16, 128], F32, 2)
            KwT = rot("KwT", [128, 128], BF16, 2); KmT = rot("KmT", [128, 16], BF16, 2)
            VwR = rot("Vw", [128, 2, 65], BF16, 2); VmR = rot("Vm", [16, 2, 65], BF16, 2)
            for t_ in VwR.tiles + VmR.tiles:
                V(lambda t_=t_: nc.vector.memset(t_[:], 1.0), [], [t_])
            Sw = [GB6.tiles[0], GB6.tiles[1]]; Sm = [GB6.tiles[2], GB6.tiles[3]]; Sn = [GB6.tiles[4], GB6.tiles[5]]
            for j in range(16):
                ks_ = kst.next(); km_ = kmst.next(); kw_ = KwT.next(); kmT_ = KmT.next()
                LD(ks_, ks_[:], ck_i[j, 16:144, :]); LD(km_, km_[:], ck_i[j, 0:16, :])
                T(lambda ks_=ks_: nc.tensor.transpose(out=pS[:, 0:128], in_=ks_[:], identity=idf[:]), [ks_, idf], [pS])
                T(lambda km_=km_: nc.tensor.transpose(out=pS[:, 128:144], in_=km_[:], identity=idf[0:16, 0:16]), [km_, idf], [pS])
                A(lambda kw_=kw_: nc.scalar.copy(out=kw_[:], in_=pS[:, 0:128]), [pS], [kw_])
                A(lambda kmT_=kmT_: nc.scalar.copy(out=kmT_[:], in_=pS[:, 128:144]), [pS], [kmT_])
                for kap in range(2):
                    ksl = slice(64 * kap, 64 * kap + 64)
                    qv = aqT[ksl, :, 4 * j:4 * j + 4]
                    T(lambda kap=kap, ksl=ksl, qv=qv, kw_=kw_, j=j: nc.tensor.matmul(Sw[kap][:, 16 * j:16 * j + 16], lhsT=kw_[ksl, :], rhs=qv,
                                                                                  start=True, stop=True), [kw_, aqT], [Sw[kap]])
                    T(lambda kap=kap, ksl=ksl, qv=qv, kmT_=kmT_, j=j: nc.tensor.matmul(Sm[kap][0:16, 16 * j:16 * j + 16], lhsT=kmT_[ksl, :], rhs=qv,
                                                                                    start=True, stop=True), [kmT_, aqT], [Sm[kap]])
                    T(lambda kap=kap, ksl=ksl, qv=qv, akT=akT, j=j: nc.tensor.matmul(Sn[kap][0:64, 16 * j:16 * j + 16], lhsT=akT[ksl, 0:64], rhs=qv,
                                                                                  start=True, stop=True), [akT, aqT], [Sn[kap]])
            Ew = sb("Ew", [128, 2, 256], BF16); Em = sb("Em", [16, 2, 256], BF16); En = sb("En", [64, 2, 256], BF16)
            for kap in range(2):
                A(lambda kap=kap: nc.scalar.activation(out=Ew[:, kap, :], in_=Sw[kap][:, 0:256], func=AF.Exp, scale=ASCALE), [Sw[kap]], [Ew])
                A(lambda kap=kap: nc.scalar.activation(out=Em[:, kap, :], in_=Sm[kap][0:16, 0:256], func=AF.Exp, scale=ASCALE), [Sm[kap]], [Em])
                A(lambda kap=kap: nc.scalar.activation(out=En[:, kap, :], in_=Sn[kap][0:64, 0:256], func=AF.Exp, scale=ASCALE), [Sn[kap]], [En])
                V(lambda kap=kap: nc.vector.tensor_tensor(out=Ew[:, kap, :], in0=Ew[:, kap, :], in1=winm[:], op=ALU.mult), [Ew, winm], [Ew])
                V(lambda kap=kap: nc.vector.tensor_tensor(out=En[:, kap, :], in0=En[:, kap, :], in1=newm[0:64, :], op=ALU.mult), [En, newm], [En])
            osb = sb("osb", [16, 16, 64]); rso = sb("rso", [16, 2, 16])
            yas_t = xsp
            yas = xsp[0:64, 512:1024].rearrange("p (k g d) -> p k g d", k=2, g=4)
            scr_t = nc.dram_tensor("scr", [16, 4, 2, 4, 64], F32)
            scr = scr_t.ap()
            for j in range(16):
                vs_ = vst.next(); vm_ = vmst.next(); Vw = VwR.next(); Vm = VmR.next()
                LD(vs_, vs_[:], cv_i[j, 16:144, :]); LD(vm_, vm_[:], cv_i[j, 0:16, :])
                G(lambda vs_=vs_, Vw=Vw: nc.gpsimd.tensor_copy(out=Vw[:, :, 0:64], in_=vs_[:].rearrange("p (k d) -> p k d", k=2)), [vs_], [Vw])
                G(lambda vm_=vm_, Vm=Vm: nc.gpsimd.tensor_copy(out=Vm[:, :, 0:64], in_=vm_[:].rearrange("p (k d) -> p k d", k=2)), [vm_], [Vm])
                for kap in range(2):
                    po = GB6.tiles[3 * kap + j // 7]
                    oap = po[0:16, (j % 7) * 65:(j % 7) * 65 + 65]
                    cs = slice(16 * j, 16 * j + 16)
                    T(lambda kap=kap, j=j, oap=oap, cs=cs, Vm=Vm: nc.tensor.matmul(oap, lhsT=Em[:, kap, cs], rhs=Vm[:, kap, :], start=True, stop=False),
                      [Em, Vm], [po])
                    T(lambda kap=kap, j=j, oap=oap, cs=cs, Vw=Vw: nc.tensor.matmul(oap, lhsT=Ew[:, kap, cs], rhs=Vw[:, kap, :], start=False, stop=False),
                      [Ew, Vw], [po])
                    T(lambda kap=kap, j=j, oap=oap, cs=cs, va=va: nc.tensor.matmul(oap, lhsT=En[:, kap, cs], rhs=va[0:64, kap, :], start=False, stop=True),
                      [En, va], [po])
            for kap in range(2):
                for b3 in range(3):
                    nj = 7 if b3 < 2 else 2
                    po = GB6.tiles[3 * kap + b3]
                    pv_ = po[0:16, 0:65 * nj].rearrange("p (j e) -> p j e", e=65)
                    V(lambda kap=kap, b3=b3, nj=nj, pv_=pv_: nc.vector.tensor_scalar(
                        out=rso[:, kap, 7 * b3:7 * b3 + nj].unsqueeze(2), in0=pv_[:, :, 64:65], scalar1=esk16[:, kap:kap + 1],
                        scalar2=None, op0=ALU.add), [po, esk16], [rso])
                    V(lambda kap=kap, b3=b3, nj=nj: nc.vector.reciprocal(out=rso[:, kap, 7 * b3:7 * b3 + nj], in_=rso[:, kap, 7 * b3:7 * b3 + nj]),
                      [rso], [rso])
                    V(lambda kap=kap, b3=b3, nj=nj, pv_=pv_: nc.vector.tensor_tensor(
                        out=osb[:, 7 * b3:7 * b3 + nj, :], in0=pv_[:, :, 0:64],
                        in1=rso[:, kap, 7 * b3:7 * b3 + nj].unsqueeze(2).to_broadcast([16, nj, 64]), op=ALU.mult), [po, rso], [osb])
                for g in range(4):
                    P.dma("sp", lambda kap=kap, g=g: nc.sync.dma_start(out=scr[:, :, kap, g, :].rearrange("j l d -> l j d"),
                                                                       in_=osb[4 * g:4 * g + 4, :, :]),
                          reads=[osb], writes=[scr_t], key="S_scr")
                P.dma("sp", lambda kap=kap: nc.sync.dma_start(out=yas[:, kap, :, :],
                                                              in_=scr[:, :, kap, :, :].rearrange("j l g d -> (j l) g d")),
                      reads=[scr_t], writes=[yas_t], key="L_yas")
            V(lambda: nc.vector.tensor_tensor(out=ybf[0:64, 512:1024], in0=xsp[0:64, 512:1024], in1=eaz[0:64], op=ALU.mult),
              [xsp, eaz], [ybf])

            transpose8(ybf, yT)
            pm_ = [GB.next(), GB.next()]
            for hf in range(2):
                T(lambda hf=hf: nc.tensor.matmul(pm_[hf][:], lhsT=ones1b[:, 0:128], rhs=bob[:, hf * 512:(hf + 1) * 512], start=True, stop=False),
                  [ones1b, bob], [pm_[hf]])
                for k in range(8):
                    T(lambda k=k, hf=hf: nc.tensor.matmul(pm_[hf][:], lhsT=yT[:, k, :], rhs=wo_bf[:, k, hf * 512:(hf + 1) * 512],
                                                          start=False, stop=(k == 7)), [yT, wo_bf], [pm_[hf]])
            V(lambda hp=hp: nc.vector.tensor_tensor(out=hp[:], in0=hp[:], in1=ln0g[:], op=ALU.mult), [hp, ln0g], [hp])
            for hf in range(2):
                V(lambda hf=hf, hp=hp: nc.vector.tensor_tensor(out=hp[:, hf * 512:(hf + 1) * 512], in0=hp[:, hf * 512:(hf + 1) * 512],
                                                               in1=pm_[hf][:], op=ALU.add),
                  [hp, pm_[hf]], [hp])
            layer_norm(hp, hp, lng, lnb)
            ST(ys_o, hp, hp[0:64, :])
            for (ci_, co_, off) in ((ck_i, sk_o, 0), (cv_i, sv_o, 128)):
                P.dma("sp", lambda ci_=ci_, co_=co_: nc.sync.dma_start(out=co_[:, 0:16, :], in_=ci_[:, 0:16, :]), key="X_c0")
                P.dma("sp", lambda ci_=ci_, co_=co_: nc.sync.dma_start(out=co_[:, 16:140, :], in_=ci_[:, 20:144, :]), key="X_c1")
                for j in range(16):
                    ST(co_[j, 140:144, :], kavf, kavf[4 * j:4 * j + 4, off:off + 128])

        GB = Rot(GB6.tiles[0:5])
        for s in range(NSLOT):
            kind = "meta" if s == 0 else ("prelast" if s == NSLOT - 1 else "pre")
            chunk(kind, xpre[s * 128:(s + 1) * 128, :], slot=s)
        bias_rows(N1, NIN)
        for c in range(NOWN):
            chunk("own", xown[c * 128:(c + 1) * 128, :], c=c)
        if DO_SAMPLE:
            sample_program()

        P.emit()
    return nc


def _consts():
    s = np.arange(128)[:, None]
    t = np.arange(128)[None, :]
    tri = (s <= t).astype(np.float32)
    prv = (s > t).astype(np.float32)
    return np.eye(128, dtype=np.float32), np.tile(tri, (1, 4)), np.tile(prv, (1, 4))


def prep_prompt(inputs, NOWN):
    NPRE = 3 * NOWN
    NSLOT = 1 + NPRE
    xp = np.asarray(inputs["x_prompt"], np.float32)
    meta = np.asarray(inputs["meta_tokens"], np.float32)
    cid, ctri, cprev = _consts()
    vecs = np.stack([np.asarray(inputs[k], np.float32).reshape(-1) for k in ("ln0_g", "ln0_b")] +
                    [np.asarray(inputs[k], np.float32).reshape(-1) for k in ("ln_g", "ln_b")])
    common = dict(cid=cid, ctri=ctri, cprev=cprev,
                  w_in=np.ascontiguousarray(np.concatenate([np.asarray(inputs["w_in"], np.float32)[0][:, a:b] for a, b in COL_PERM], 1)),
                  b_in=np.ascontiguousarray(np.concatenate([np.asarray(inputs["b_in"], np.float32).reshape(NIN)[a:b] for a, b in COL_PERM]).reshape(1, NIN)),
                  w_out=np.ascontiguousarray(np.asarray(inputs["w_out"], np.float32)[0]),
                  vecs=vecs, mng=np.asarray(inputs["m_norm_g"], np.float32).reshape(1, 512),
                  sinks=np.asarray(inputs["a_sinks"], np.float32).reshape(1, 8))
    maps = []
    for core in range(8):
        b, r = core // 4, core % 4
        xown = np.ascontiguousarray(xp[b, r * NOWN * 128:(r + 1) * NOWN * 128])
        xpre = np.zeros((NSLOT * 128, D), np.float32)
        valid = np.zeros((NSLOT * 128,), np.float32)
        xpre[0:NMETA] = meta
        valid[0:NMETA] = 1.0
        npre = r * NOWN
        if npre:
            xpre[(NSLOT - npre) * 128:] = xp[b, 0:npre * 128]
            valid[(NSLOT - npre) * 128:] = 1.0
        rvalid = np.tile(valid[None, :], (4, 1))
        rneg = np.where(rvalid > 0, 0.0, NEG).astype(np.float32)
        pm1 = cprev if r > 0 else np.zeros_like(cprev)
        m = dict(common)
        m.update(xown=xown, xpre=xpre, rvalid=rvalid, rneg=rneg, pm1=pm1)
        maps.append(m)
    return maps


def prep_sample(inputs, maps):
    xs = np.asarray(inputs["x_sample"], np.float32)
    ck = np.asarray(inputs["cache_swa_k"], np.float32)[0].reshape(128, 144, 128)
    cv = np.asarray(inputs["cache_swa_v"], np.float32)[0].reshape(128, 144, 128)
    sc = np.asarray(inputs["state_mlstm_c"], np.float32)[0]
    sn = np.asarray(inputs["state_mlstm_n"], np.float32)[0]
    sm = np.asarray(inputs["state_mlstm_m"], np.float32)[0]
    sinks = np.asarray(inputs["a_sinks"], np.float32).reshape(8)
    s_ = np.arange(128)
    j_ = np.arange(16)
    cseqcol = ((s_[:, None] // 4 == j_[None, :]) & (s_[:, None] < 64)).astype(np.float32)
    t_ = np.arange(64)
    cseqbc = np.broadcast_to((t_[None, :] // 4 == j_[:, None]).astype(np.float32).reshape(1, 1024), (128, 1024)).copy()
    t128 = np.arange(128)
    blk = ((s_[:, None] // 4 == t128[None, :] // 4) & (s_[:, None] <= t128[None, :]) & (s_[:, None] < 64) & (t128[None, :] < 64))
    cblk4 = np.tile(blk.astype(np.float32), (1, 4))
    l_ = np.arange(4)
    win = (s_[:, None] > l_[None, :]).astype(np.float32)
    cwin = np.broadcast_to(win[:, None, None, :], (128, 16, 4, 4)).reshape(128, 256).copy()
    new = ((s_[:, None, None] // 4 == j_[None, :, None]) & (s_[:, None, None] % 4 <= l_[None, None, :]) & (s_[:, None, None] < 64))
    cnew = np.broadcast_to(new[:, :, None, :], (128, 16, 4, 4)).astype(np.float32).reshape(128, 256).copy()
    sinks16 = np.zeros((16, 2), np.float32)
    for kap in range(2):
        for g in range(4):
            sinks16[4 * g:4 * g + 4, kap] = sinks[4 * kap + g]
    for core in range(8):
        sl = slice(16 * core, 16 * core + 16)
        maps[core].update(
            xs=np.ascontiguousarray(xs[sl].reshape(64, D)), ck=np.ascontiguousarray(ck[sl]), cv=np.ascontiguousarray(cv[sl]),
            sc=np.ascontiguousarray(sc[sl]), sn=np.ascontiguousarray(sn[sl].reshape(64, 128)),
            sm=np.ascontiguousarray(sm[sl].T), cseqcol=cseqcol, cblk4=cblk4, cwin=cwin, cnew=cnew, sinks16=sinks16)
    return maps


_NC_CACHE = {}


def kernel(**inputs):
    NOWN = np.asarray(inputs["x_prompt"]).shape[1] // 512
    B = 2
    maps = prep_sample(inputs, prep_prompt(inputs, NOWN))
    nc = build(NOWN, DO_SAMPLE=True)
    res = run_bass_kernel_spmd(nc, maps, core_ids=list(range(8)))
    R = res.results
    S = NOWN * 512
    y = np.zeros((B, S, D), np.float32)
    for core in range(8):
        b, r = core // 4, core % 4
        y[b, r * NOWN * 128:(r + 1) * NOWN * 128] = R[core]["y"]
    last = [3, 7]
    pk = np.stack([R[c]["pk"].reshape(144, 2, 64) for c in last])[None]
    pv = np.stack([R[c]["pv"].reshape(144, 2, 64) for c in last])[None]
    pc = np.stack([R[c]["pc"] for c in last])[None]
    pn = np.stack([R[c]["pn"] for c in last])[None]
    pm = np.stack([R[c]["pm"].reshape(4) for c in last])[None]
    ys = np.concatenate([R[c]["ys"].reshape(16, 4, D) for c in range(8)], 0)
    sk = np.concatenate([R[c]["sk"].reshape(16, 144, 2, 64) for c in range(8)], 0)[None]
    sv = np.concatenate([R[c]["sv"].reshape(16, 144, 2, 64) for c in range(8)], 0)[None]
    sco = np.concatenate([R[c]["sco"] for c in range(8)], 0)[None]
    sno = np.concatenate([R[c]["sno"].reshape(16, 4, 128) for c in range(8)], 0)[None]
    smo = np.concatenate([R[c]["smo"].T for c in range(8)], 0)[None]
    f = lambda a: np.ascontiguousarray(a, dtype=np.float32)
    return tuple(f(a) for a in (y, ys, pk, pv, pc, pn, pm, sk, sv, sco, sno, smo))
```
